# Optimizing a Trainium2 kernel written in Bass

```python
import math
import jax
import jax.numpy as jnp
from jax import lax
import numpy as np

D_MODEL = 1024
BATCH = 32
SEQ = 256
DEPTH = 2
DEC_BATCH = 2
DEC_SEQ = 2048
PAST_LEN = 256

GRID_W = 64
HEAD_DIM = 64
H_NA = D_MODEL // 128
H_DIFF = D_MODEL // 256
H_RET = D_MODEL // 128
H_RWKV = D_MODEL // 128
WIN_R_MAX = 8
WIN_W = 16
Q_BLOCK = 128
RET_CHUNK = 128
D_LORA_W = 64
D_LORA_A = 64
D_LORA_G = 128
FF_RAW = -(-8 * D_MODEL // 3)
D_FF = -(-FF_RAW // 256) * 256
N_ATTN_LAYERS = (DEPTH + 1) // 2
N_REC_LAYERS = DEPTH // 2
D_NA = H_NA * HEAD_DIM
D_DIFF = H_DIFF * 2 * HEAD_DIM
D_RET = H_RET * HEAD_DIM
D_RWKV = H_RWKV * HEAD_DIM
ATTN_IN_SIZES = (D_NA, D_NA, D_NA, D_DIFF, D_DIFF, D_DIFF)
REC_IN_SIZES = (D_RET, D_RET, D_RET, D_RET, D_RWKV, D_RWKV, D_RWKV, D_LORA_W, D_LORA_A, D_LORA_G)
D_IN_ATTN = 3 * D_NA + 3 * D_DIFF
D_MIX_ATTN = D_NA + D_DIFF
D_IN_REC = 4 * D_RET + 3 * D_RWKV + D_LORA_W + D_LORA_A + D_LORA_G
D_MIX_REC = D_RET + D_RWKV
ROPE_BASE = 10000.0
RMS_EPS = 1e-6
RWKV_GN_EPS = 64e-5
NEG_INF = -1e30

kernel_name = 'hybrid_diffusion_prefix_step'


def rms_norm(x, gain=None, eps=RMS_EPS):
    xf = x.astype(jnp.float32)
    y = xf * lax.rsqrt(jnp.mean(xf * xf, axis=-1, keepdims=True) + eps)
    if gain is not None:
        y = y * gain.astype(jnp.float32)
    return y.astype(x.dtype)


def ada_modulation(cond, w, b):
    m = (jax.nn.silu(cond) @ w + b)[..., None, :]
    return jnp.split(m, 6, axis=-1)


def modulate(h, shift, scale):
    return h * (1.0 + scale) + shift


def split_cols(x, sizes):
    out, start = [], 0
    for s in sizes:
        out.append(x[..., start:start + s])
        start += s
    return out


def axial_rope(x):
    n = x.shape[1]
    pos = jnp.arange(n)
    half = HEAD_DIM // 2
    quarter = half // 2
    inv_freq = ROPE_BASE ** (-jnp.arange(quarter, dtype=jnp.float32) / quarter)
    bshape = (n,) + (1,) * (x.ndim - 3) + (quarter,)

    def rot(xh, p):
        ang = (p.astype(jnp.float32)[:, None] * inv_freq[None, :]).reshape(bshape)
        cos, sin = jnp.cos(ang), jnp.sin(ang)
        x1, x2 = xh[..., :quarter], xh[..., quarter:]
        return jnp.concatenate([x1 * cos - x2 * sin, x1 * sin + x2 * cos], axis=-1)

    xf = x.astype(jnp.float32)
    out = jnp.concatenate([rot(xf[..., :half], pos // GRID_W), rot(xf[..., half:], pos % GRID_W)], axis=-1)
    return out.astype(x.dtype)


def sweep_query_blocks(fn, q):
    b, n = q.shape[0], q.shape[1]
    nb = n // Q_BLOCK
    qb = jnp.moveaxis(q.reshape((b, nb, Q_BLOCK) + q.shape[2:]), 1, 0)
    o = jnp.moveaxis(lax.map(fn, qb), 0, 1)
    return o.reshape((b, n) + o.shape[3:])


def softmax_attention(q, k, v):
    scale = HEAD_DIM ** -0.5

    def block(qb):
        s = jnp.einsum('bqhd,bkhd->bhqk', qb, k).astype(jnp.float32) * scale
        p = jax.nn.softmax(s, axis=-1).astype(v.dtype)
        return jnp.einsum('bhqk,bkhd->bqhd', p, v)

    return sweep_query_blocks(block, q)


def neighbourhood_attention(q, k, v, k_ctx, v_ctx, rpb):
    b, n, h, d = q.shape
    rows = n // GRID_W
    win_r = min(WIN_R_MAX, rows)
    r_idx = jnp.arange(rows)
    r_start = jnp.clip(r_idx - win_r // 2, 0, rows - win_r)
    key_rows = r_start[:, None] + jnp.arange(win_r)[None, :]
    col = jnp.arange(GRID_W)
    c_start = jnp.clip(col - WIN_W // 2, 0, GRID_W - WIN_W)
    col_in = (col[None, :] >= c_start[:, None]) & (col[None, :] < c_start[:, None] + WIN_W)
    dr_idx = key_rows - r_idx[:, None] + (WIN_R_MAX - 1)
    dc_idx = jnp.clip(col[None, :] - col[:, None] + (WIN_W - 1), 0, 2 * WIN_W - 2)
    bias = rpb[:, dr_idx[:, None, :, None], dc_idx[None, :, None, :]]
    qg = q.reshape(b, rows, GRID_W, h, d)
    kg = k.reshape(b, rows, GRID_W, h, d)[:, key_rows]
    vg = v.reshape(b, rows, GRID_W, h, d)[:, key_rows]
    scale = d ** -0.5
    s_win = jnp.einsum('brqhd,brikhd->bhrqik', qg, kg).astype(jnp.float32) * scale + bias.astype(jnp.float32)
    s_win = jnp.where(col_in[:, None, :], s_win, NEG_INF).reshape(b, h, rows, GRID_W, win_r * GRID_W)
    s_ctx = jnp.einsum('brqhd,blhd->bhrql', qg, k_ctx).astype(jnp.float32) * scale
    p = jax.nn.softmax(jnp.concatenate([s_win, s_ctx], axis=-1), axis=-1).astype(v.dtype)
    p_win = p[..., :win_r * GRID_W].reshape(b, h, rows, GRID_W, win_r, GRID_W)
    p_ctx = p[..., win_r * GRID_W:]
    o = jnp.einsum('bhrqik,brikhd->brqhd', p_win, vg) + jnp.einsum('bhrql,blhd->brqhd', p_ctx, v_ctx)
    return o.reshape(b, n, h, d)


def diff_attention(q, k, v, lam, lam_init):
    scale = HEAD_DIM ** -0.5

    def block(qb):
        s = jnp.einsum('bqhcd,bkhcd->bhcqk', qb, k).astype(jnp.float32) * scale
        p = jax.nn.softmax(s, axis=-1)
        p = (p[:, :, 0] - lam * p[:, :, 1]).astype(v.dtype)
        return jnp.einsum('bhqk,bkhe->bqhe', p, v)

    o = sweep_query_blocks(block, q)
    return rms_norm(o) * (1.0 - lam_init)


def attn_projections(h, w_in):
    b, n, _ = h.shape
    nq, nk, nv, dq, dk, dv = split_cols(h @ w_in, ATTN_IN_SIZES)
    na = lambda t: t.reshape(b, n, H_NA, HEAD_DIM)
    dqk = lambda t: t.reshape(b, n, H_DIFF, 2, HEAD_DIM)
    return na(nq), na(nk), na(nv), dqk(dq), dqk(dk), dv.reshape(b, n, H_DIFF, 2 * HEAD_DIM)


def retention_scan(q, k, v, log_g, r0):
    b, n, h, d = q.shape
    nc = n // RET_CHUNK
    idx = jnp.arange(RET_CHUNK, dtype=jnp.float32)
    diff = idx[:, None] - idx[None, :]
    decay_mask = jnp.where(diff[None] >= 0, jnp.exp(jnp.maximum(diff, 0.0)[None] * log_g[:, None, None]), 0.0)
    q_decay = jnp.exp((idx[:, None] + 1.0) * log_g[None, :])
    k_decay = jnp.exp((RET_CHUNK - 1.0 - idx)[:, None] * log_g[None, :])
    chunk_decay = jnp.exp(RET_CHUNK * log_g)

    def to_chunks(x):
        return jnp.moveaxis(x.reshape(b, nc, RET_CHUNK, h, d), 1, 0)

    def step(state, qkv):
        qc, kc, vc = qkv
        s = jnp.einsum('bihd,bjhd->bhij', qc, kc) * decay_mask[None]
        inner = jnp.einsum('bhij,bjhe->bihe', s, vc)
        cross = jnp.einsum('bihd,bhde->bihe', qc, state) * q_decay[None, :, :, None]
        new_state = state * chunk_decay[None, :, None, None] + jnp.einsum('bjhd,bjhe->bhde', kc * k_decay[None, :, :, None], vc)
        return new_state, inner + cross

    final, out = lax.scan(step, r0, (to_chunks(q), to_chunks(k), to_chunks(v)))
    return jnp.moveaxis(out, 0, 1).reshape(b, n, h, d), final


def retention_mixer(x_q, x_k, x_v, x_g, decay_logit, state0):
    b, n, _ = x_q.shape
    heads = lambda t: t.reshape(b, n, H_RET, HEAD_DIM).astype(jnp.float32)
    q, k, v = heads(x_q), heads(x_k) * (HEAD_DIM ** -0.5), heads(x_v)
    log_g = jax.nn.log_sigmoid(decay_logit.astype(jnp.float32))
    s0 = state0.astype(jnp.float32)
    o_f, s_f = retention_scan(q, k, v, log_g[0], s0[:, 0])
    o_b, s_b = retention_scan(q[:, ::-1], k[:, ::-1], v[:, ::-1], log_g[1], s0[:, 1])
    o = rms_norm(o_f + o_b[:, ::-1]).reshape(b, n, D_RET)
    out = jax.nn.silu(x_g.astype(jnp.float32)) * o
    return out.astype(x_q.dtype), jnp.stack([s_f, s_b], axis=1).astype(x_q.dtype)


def rwkv7_scan(r, w, k, v, a_vec, b_vec, s0):
    def step(S, inp):
        rt, wt, kt, vt, at, bt = inp
        sa = jnp.einsum('bhvk,bhk->bhv', S, at)
        S = S * wt[:, :, None, :] + sa[..., None] * bt[:, :, None, :] + vt[..., None] * kt[:, :, None, :]
        return S, jnp.einsum('bhvk,bhk->bhv', S, rt)

    xs = tuple(jnp.moveaxis(t, 1, 0) for t in (r, w, k, v, a_vec, b_vec))
    S, ys = lax.scan(step, s0, xs)
    return jnp.moveaxis(ys, 0, 1), S


def rwkv7_mixer(x_r, x_k, x_v, x_wd, x_ad, x_gd, w0, w_up, a0, a_up, g_up, k_k, k_a, r_k, ln_g, ln_b, state0):
    b, n, _ = x_r.shape
    heads = lambda t: t.reshape(b, n, H_RWKV, HEAD_DIM).astype(jnp.float32)
    r, k, v = heads(x_r), heads(x_k), heads(x_v)
    kk = heads(x_k * k_k)
    kk = kk * lax.rsqrt(jnp.sum(kk * kk, axis=-1, keepdims=True) + 1e-12)
    k_a_h = k_a.reshape(H_RWKV, HEAD_DIM).astype(jnp.float32)
    s0 = state0.astype(jnp.float32)
    ys, states = [], []
    for d in range(2):
        w_log = -math.exp(-0.5) * jax.nn.sigmoid(w0[d] + jnp.tanh(x_wd) @ w_up[d])
        decay = heads(jnp.exp(w_log))
        a = heads(jax.nn.sigmoid(a0[d] + x_ad @ a_up[d]))
        k_eff = k * (1.0 + (a - 1.0) * k_a_h)
        seq = (r, decay, k_eff, v, -kk, kk * a)
        if d == 1:
            seq = tuple(t[:, ::-1] for t in seq)
        y, s = rwkv7_scan(*seq, s0[:, d])
        if d == 1:
            y = y[:, ::-1]
        ys.append(y)
        states.append(s)
    y = ys[0] + ys[1]
    mu = jnp.mean(y, axis=-1, keepdims=True)
    var = jnp.mean(jnp.square(y - mu), axis=-1, keepdims=True)
    y = ((y - mu) * lax.rsqrt(var + RWKV_GN_EPS)).reshape(b, n, D_RWKV) * ln_g.astype(jnp.float32) + ln_b.astype(jnp.float32)
    bonus = (jnp.sum(r * r_k.astype(jnp.float32) * k, axis=-1, keepdims=True) * v).reshape(b, n, D_RWKV)
    g = (jax.nn.sigmoid(x_gd) @ g_up).astype(jnp.float32)
    out = (y + bonus) * g
    return out.astype(x_r.dtype), jnp.stack(states, axis=1).astype(x_r.dtype)


def rec_mixer(h, w_in, w_out, decay_logit, w0, w_up, a0, a_up, g_up, k_k, k_a, r_k, ln_g, ln_b, ret_state0, rwkv_state0):
    rq, rk, rv, rg, wr, wk, wv, wd, ad, gd = split_cols(h @ w_in, REC_IN_SIZES)
    o_ret, s_ret = retention_mixer(rq, rk, rv, rg, decay_logit, ret_state0)
    o_rw, s_rw = rwkv7_mixer(wr, wk, wv, wd, ad, gd, w0, w_up, a0, a_up, g_up, k_k, k_a, r_k, ln_g, ln_b, rwkv_state0)
    return jnp.concatenate([o_ret, o_rw], axis=-1) @ w_out, s_ret, s_rw


def swiglu(h, w_in, w_out):
    g, u = jnp.split(h @ w_in, 2, axis=-1)
    return (jax.nn.silu(g) * u) @ w_out


def setup_inputs(seed: int = 0) -> dict:
    key = jax.random.key(seed)
    ks = iter(jax.random.split(key, 48))
    nrm = lambda shape, scale: jax.random.normal(next(ks), shape, jnp.float32) * scale
    base_logit = jnp.log(2.0 ** (5.0 + jnp.arange(H_RET, dtype=jnp.float32)) - 1.0)
    return {
        'x_prompt': nrm((BATCH, SEQ, D_MODEL), 1.0),
        'x_sample': nrm((DEC_BATCH, DEC_SEQ, D_MODEL), 1.0),
        'cache_na_k': nrm((DEC_BATCH, N_ATTN_LAYERS, PAST_LEN, H_NA, HEAD_DIM), 1.0),
        'cache_na_v': nrm((DEC_BATCH, N_ATTN_LAYERS, PAST_LEN, H_NA, HEAD_DIM), 1.0),
        'cache_diff_k': nrm((DEC_BATCH, N_ATTN_LAYERS, PAST_LEN, H_DIFF, 2, HEAD_DIM), 1.0),
        'cache_diff_v': nrm((DEC_BATCH, N_ATTN_LAYERS, PAST_LEN, H_DIFF, 2 * HEAD_DIM), 1.0),
        'state_ret': nrm((DEC_BATCH, N_REC_LAYERS, 2, H_RET, HEAD_DIM, HEAD_DIM), 0.5),
        'state_rwkv': nrm((DEC_BATCH, N_REC_LAYERS, 2, H_RWKV, HEAD_DIM, HEAD_DIM), 0.5),
        'c': nrm((DEC_BATCH, D_MODEL), 1.0),
        'c_ctx': nrm((D_MODEL,), 1.0),
        'norm_mix_g': 1.0 + nrm((DEPTH, D_MODEL), 0.05),
        'norm_ffn_g': 1.0 + nrm((DEPTH, D_MODEL), 0.05),
        'norm_final_g': 1.0 + nrm((D_MODEL,), 0.05),
        'w_ada': nrm((DEPTH, D_MODEL, 6 * D_MODEL), 0.5 * D_MODEL ** -0.5),
        'b_ada': nrm((DEPTH, 6 * D_MODEL), 0.02),
        'w_in_attn': nrm((N_ATTN_LAYERS, D_MODEL, D_IN_ATTN), D_MODEL ** -0.5),
        'w_out_attn': nrm((N_ATTN_LAYERS, D_MIX_ATTN, D_MODEL), D_MIX_ATTN ** -0.5),
        'na_rpb': nrm((N_ATTN_LAYERS, H_NA, 2 * WIN_R_MAX - 1, 2 * WIN_W - 1), 0.1),
        'diff_lq1': nrm((N_ATTN_LAYERS, HEAD_DIM), 0.1),
        'diff_lk1': nrm((N_ATTN_LAYERS, HEAD_DIM), 0.1),
        'diff_lq2': nrm((N_ATTN_LAYERS, HEAD_DIM), 0.1),
        'diff_lk2': nrm((N_ATTN_LAYERS, HEAD_DIM), 0.1),
        'w_in_rec': nrm((N_REC_LAYERS, D_MODEL, D_IN_REC), D_MODEL ** -0.5),
        'w_out_rec': nrm((N_REC_LAYERS, D_MIX_REC, D_MODEL), D_MIX_REC ** -0.5),
        'ret_decay_logit': jnp.broadcast_to(base_logit, (N_REC_LAYERS, 2, H_RET)) + nrm((N_REC_LAYERS, 2, H_RET), 0.05),
        'rw_w0': nrm((N_REC_LAYERS, 2, D_RWKV), 0.5),
        'rw_w_up': nrm((N_REC_LAYERS, 2, D_LORA_W, D_RWKV), 0.5 * D_LORA_W ** -0.5),
        'rw_a0': nrm((N_REC_LAYERS, 2, D_RWKV), 0.5),
        'rw_a_up': nrm((N_REC_LAYERS, 2, D_LORA_A, D_RWKV), 0.5 * D_LORA_A ** -0.5),
        'rw_g_up': nrm((N_REC_LAYERS, D_LORA_G, D_RWKV), D_LORA_G ** -0.5),
        'rw_k_k': 0.85 + nrm((N_REC_LAYERS, D_RWKV), 0.05),
        'rw_k_a': 1.0 + nrm((N_REC_LAYERS, D_RWKV), 0.05),
        'rw_r_k': nrm((N_REC_LAYERS, H_RWKV, HEAD_DIM), 0.1),
        'rw_ln_g': 1.0 + nrm((N_REC_LAYERS, D_RWKV), 0.05),
        'rw_ln_b': nrm((N_REC_LAYERS, D_RWKV), 0.02),
        'w_ffn_in': nrm((DEPTH, D_MODEL, 2 * D_FF), D_MODEL ** -0.5),
        'w_ffn_out': nrm((DEPTH, D_FF, D_MODEL), D_FF ** -0.5),
    }


def reference(x_prompt, x_sample, cache_na_k, cache_na_v, cache_diff_k, cache_diff_v, state_ret, state_rwkv,
              c, c_ctx, norm_mix_g, norm_ffn_g, norm_final_g, w_ada, b_ada,
              w_in_attn, w_out_attn, na_rpb, diff_lq1, diff_lk1, diff_lq2, diff_lk2,
              w_in_rec, w_out_rec, ret_decay_logit, rw_w0, rw_w_up, rw_a0, rw_a_up, rw_g_up,
              rw_k_k, rw_k_a, rw_r_k, rw_ln_g, rw_ln_b, w_ffn_in, w_ffn_out):
    ctx, lat = x_prompt, x_sample
    bp = ctx.shape[0]
    na_k_list, na_v_list, df_k_list, df_v_list, ret_list, rwkv_list = [], [], [], [], [], []
    for layer in range(DEPTH):
        m_ctx = ada_modulation(c_ctx, w_ada[layer], b_ada[layer])
        m_lat = ada_modulation(c, w_ada[layer], b_ada[layer])
        h_ctx = modulate(rms_norm(ctx, norm_mix_g[layer]), m_ctx[0], m_ctx[1])
        h_lat = modulate(rms_norm(lat, norm_mix_g[layer]), m_lat[0], m_lat[1])
        if layer % 2 == 0:
            i = layer // 2
            lam_init = 0.8 - 0.6 * math.exp(-0.3 * layer)
            lam = (jnp.exp(jnp.sum(diff_lq1[i].astype(jnp.float32) * diff_lk1[i].astype(jnp.float32)))
                   - jnp.exp(jnp.sum(diff_lq2[i].astype(jnp.float32) * diff_lk2[i].astype(jnp.float32))) + lam_init)
            nq, nk, nv, dq, dk, dv = attn_projections(h_ctx, w_in_attn[i])
            o_na = softmax_attention(nq, nk, nv)
            o_df = diff_attention(dq, dk, dv, lam, lam_init)
            mix_ctx = jnp.concatenate([o_na.reshape(o_na.shape[:2] + (D_NA,)), o_df.reshape(o_df.shape[:2] + (D_DIFF,))], axis=-1) @ w_out_attn[i]
            na_k_list.append(nk)
            na_v_list.append(nv)
            df_k_list.append(dk)
            df_v_list.append(dv)
            nq, nk, nv, dq, dk, dv = attn_projections(h_lat, w_in_attn[i])
            dq, dk = axial_rope(dq), axial_rope(dk)
            o_na = neighbourhood_attention(nq, nk, nv, cache_na_k[:, i], cache_na_v[:, i], na_rpb[i])
            k_all = jnp.concatenate([dk, cache_diff_k[:, i]], axis=1)
            v_all = jnp.concatenate([dv, cache_diff_v[:, i]], axis=1)
            o_df = diff_attention(dq, k_all, v_all, lam, lam_init)
            mix_lat = jnp.concatenate([o_na.reshape(o_na.shape[:2] + (D_NA,)), o_df.reshape(o_df.shape[:2] + (D_DIFF,))], axis=-1) @ w_out_attn[i]
        else:
            j = layer // 2
            params = (w_in_rec[j], w_out_rec[j], ret_decay_logit[j], rw_w0[j], rw_w_up[j], rw_a0[j], rw_a_up[j],
                      rw_g_up[j], rw_k_k[j], rw_k_a[j], rw_r_k[j], rw_ln_g[j], rw_ln_b[j])
            zero_ret = jnp.zeros((bp, 2, H_RET, HEAD_DIM, HEAD_DIM), ctx.dtype)
            zero_rwkv = jnp.zeros((bp, 2, H_RWKV, HEAD_DIM, HEAD_DIM), ctx.dtype)
            mix_ctx, s_ret, s_rw = rec_mixer(h_ctx, *params, zero_ret, zero_rwkv)
            ret_list.append(s_ret)
            rwkv_list.append(s_rw)
            mix_lat, _, _ = rec_mixer(h_lat, *params, state_ret[:, j], state_rwkv[:, j])
        ctx = ctx + m_ctx[2] * mix_ctx
        lat = lat + m_lat[2] * mix_lat
        ctx = ctx + m_ctx[5] * swiglu(modulate(rms_norm(ctx, norm_ffn_g[layer]), m_ctx[3], m_ctx[4]), w_ffn_in[layer], w_ffn_out[layer])
        lat = lat + m_lat[5] * swiglu(modulate(rms_norm(lat, norm_ffn_g[layer]), m_lat[3], m_lat[4]), w_ffn_in[layer], w_ffn_out[layer])
    y_prompt = rms_norm(ctx, norm_final_g)
    y_sample = rms_norm(lat, norm_final_g)
    new_cache_na_k = jnp.stack(na_k_list, axis=1)
    new_cache_na_v = jnp.stack(na_v_list, axis=1)
    new_cache_diff_k = jnp.stack(df_k_list, axis=1)
    new_cache_diff_v = jnp.stack(df_v_list, axis=1)
    new_state_ret = jnp.stack(ret_list, axis=1)
    new_state_rwkv = jnp.stack(rwkv_list, axis=1)
    return (y_prompt, y_sample, new_cache_na_k, new_cache_na_v, new_cache_diff_k, new_cache_diff_v, new_state_ret, new_state_rwkv)
```

```python
import math
import os
import numpy as np
from contextlib import ExitStack
import concourse.bass as bass
import concourse.mybir as mybir
from concourse.bass_utils import run_bass_kernel_spmd

F32 = mybir.dt.float32
BF16 = mybir.dt.bfloat16
AF = mybir.ActivationFunctionType
ALU = mybir.AluOpType
AX = mybir.AxisListType

D = 1024
NCH = 8
EPS = 1e-6
GN_EPS = 64e-5
DFF = 2816
NFF = 22
GS = 512
NTG = GS // 128


class Buf:
    def __init__(self, name, t=None):
        self.name = name
        self.t = t
        self.w = {}
        self.r = {}

    def __getitem__(self, k):
        return self.t[k]


class Eng:
    def __init__(self, name, sem):
        self.name = name
        self.sem = sem
        self.n = 0
        self.seen = {}
        self.ops = []


class FW:
    N_DMA_SEMS = 24

    def __init__(self, nc):
        self.nc = nc
        self.stack = ExitStack()
        self.engs = {}
        for name in ("pe", "act", "dve", "pool", "sp"):
            sem = self.stack.enter_context(nc.semaphore("sem_" + name))
            self.engs[name] = Eng(name, sem)
        self.dsems = [self.stack.enter_context(nc.semaphore(f"sem_dma{i}")) for i in range(self.N_DMA_SEMS)]
        self.dval = [0] * self.N_DMA_SEMS
        self.dkey = [f"D{i}" for i in range(self.N_DMA_SEMS)]
        self.dnext = 0
        self.dnext_sw = 0
        self.nalloc = 0

    def sbuf(self, st, name, shape, dtype):
        self.nalloc += 1
        t = st.enter_context(self.nc.sbuf_tensor(f"{name}_{self.nalloc}", list(shape), dtype))
        return Buf(name, t)

    def psum(self, st, name, shape, dtype=F32):
        self.nalloc += 1
        t = st.enter_context(self.nc.psum_tensor(f"{name}_{self.nalloc}", list(shape), dtype))
        return Buf(name, t)

    def _deps(self, eng, reads, writes, own_key):
        need = {}

        def add(d):
            for k, (sem, val) in d.items():
                if k == own_key and k.startswith("Epe"):
                    continue
                if k not in need or need[k][1] < val:
                    need[k] = (sem, val)
        for b in reads:
            add(b.w)
        for b in writes:
            add(b.w)
            add(b.r)
        waits = []
        for k, (sem, val) in need.items():
            if eng.seen.get(k, 0) >= val:
                continue
            eng.seen[k] = val
            waits.append((sem, val))
        return waits

    SEM_LIMIT = 3800

    def op(self, en, fn, reads=(), writes=()):
        eng = self.engs[en]
        if eng.n >= self.SEM_LIMIT:
            eng.gen = getattr(eng, "gen", 0) + 1
            eng.sem = self.stack.enter_context(self.nc.semaphore(f"sem_{en}_{eng.gen}"))
            eng.n = 0
        key = "E" + en + str(getattr(eng, "gen", 0))
        waits = self._deps(eng, reads, writes, key)
        eng.n += 1
        tok = (eng.sem, eng.n)
        eng.ops.append((waits, fn, eng.sem, 1))
        for b in reads:
            b.r[key] = tok
        for b in writes:
            b.w = {key: tok}
            b.r = {}

    def dma(self, fn, reads=(), writes=(), queue="sp"):
        eng = self.engs[queue]
        half = self.N_DMA_SEMS // 2
        if queue == "pool":
            i = half + self.dnext_sw
            self.dnext_sw = (self.dnext_sw + 1) % half
        else:
            i = self.dnext
            self.dnext = (self.dnext + 1) % half
        if self.dval[i] >= self.SEM_LIMIT:
            self.dgen = getattr(self, "dgen", 0) + 1
            self.dsems[i] = self.stack.enter_context(self.nc.semaphore(f"sem_dma{i}_{self.dgen}"))
            self.dval[i] = 0
            self.dkey[i] = f"D{i}_{self.dgen}"
        sem = self.dsems[i]
        key = self.dkey[i]
        waits = self._deps(eng, reads, writes, None)
        prev = self.dval[i]
        if prev > 0 and eng.seen.get(key, 0) < prev:
            eng.seen[key] = prev
            waits.append((sem, prev))
        self.dval[i] = prev + 16
        tok = (sem, self.dval[i])
        eng.ops.append((waits, fn, sem, 16))
        for b in reads:
            b.r[key] = tok
        for b in writes:
            b.w = {key: tok}
            b.r = {}

    def barrier(self):
        toks = {}
        for name, eng in self.engs.items():
            if eng.n > 0:
                toks["E" + name + str(getattr(eng, "gen", 0))] = (eng.sem, eng.n)
        for i, v in enumerate(self.dval):
            if v > 0:
                toks[self.dkey[i]] = (self.dsems[i], v)
        for name, eng in self.engs.items():
            waits = []
            for k, (sem, val) in toks.items():
                if eng.seen.get(k, 0) >= val:
                    continue
                eng.seen[k] = val
                waits.append((sem, val))
            if waits:
                eng.ops.append((waits, None, None, 0))

    def emit(self):
        nc = self.nc
        with nc.Block() as block:
            def run(eng):
                def body(e):
                    for waits, fn, sem, inc in eng.ops:
                        for (s, v) in waits:
                            e.wait_ge(s, v)
                        if fn is not None:
                            fn(e).then_inc(sem, inc)
                    eng.ops = []
                return body
            block.tensor(run(self.engs["pe"]))
            block.scalar(run(self.engs["act"]))
            block.vector(run(self.engs["dve"]))
            block.gpsimd(run(self.engs["pool"]))
            block.sync(run(self.engs["sp"]))


IN_SPECS = [
    ("x_prompt", [1024, D]), ("x_sample", [2048, D]), ("cond", [2, D]),
    ("cache_na_k", [256, 512]), ("cache_na_v", [256, 512]), ("cache_diff_k", [256, 512]), ("cache_diff_v", [256, 512]),
    ("state_ret", [2, 8, 64, 64]), ("state_rwkv", [2, 8, 64, 64]),
    ("norm_mix_g", [2, D]), ("norm_ffn_g", [2, D]), ("norm_final_g", [D]),
    ("w_ada", [2, D, 6 * D]), ("b_ada", [2, 6 * D]),
    ("w_in_attn", [D, 3072]), ("w_out_attn", [D, D]), ("na_rpb", [8, 15, 31]),
    ("diff_l", [4, 64]),
    ("w_in_rec", [D, 3840]), ("w_out_rec", [D, D]), ("ret_decay_logit", [16]),
    ("rw_w0", [2, 512]), ("rw_w_up", [2, 64, 512]), ("rw_a0", [2, 512]), ("rw_a_up", [2, 64, 512]), ("rw_g_up", [128, 512]),
    ("rw_k_k", [512]), ("rw_k_a", [512]), ("rw_r_k", [512]), ("rw_ln_g", [512]), ("rw_ln_b", [512]),
    ("w_ffn_in", [2, D, 2 * DFF]), ("w_ffn_out", [2, DFF, D]),
    ("c_rpbT", [8, 128, 22, 64]), ("c_rowmask", [128, 4, 16, 8]), ("c_cos", [128, 2048]), ("c_sin", [128, 2048]), ("c_perm", [128, 128]),
    ("c_masks", [4, 128, 128]), ("c_diff", [128, 128]), ("c_ip", [2, 128, 128]), ("c_jc", [128, 2]), ("c_half", [128, 2]),
]
OUT_SPECS = [
    ("y_prompt", [1024, D]), ("y_sample", [2048, D]),
    ("o_na_k", [1024, 512]), ("o_na_v", [1024, 512]), ("o_df_k", [1024, 512]), ("o_df_v", [1024, 512]),
    ("o_sret", [4, 2, 8, 64, 64]), ("o_srw", [4, 2, 8, 64, 64]),
]


class Prog:
    def __init__(self, stop_after=None):
        self.stop_after = stop_after
        nc = bass.Bass("TRN2", target_bir_lowering=False)
        self.nc = nc
        self.I = {n: nc.dram_tensor(n, s, F32, kind="ExternalInput").ap() for n, s in IN_SPECS}
        self.O = {n: nc.dram_tensor(n, s, F32, kind="ExternalOutput").ap() for n, s in OUT_SPECS}
        self.fw = FW(nc)
        self.OUTB = Buf("outputs")
        self.gst = ExitStack()
        self.build()
        self.gst.close()
        self.fw.stack.close()

    def flush(self):
        self.fw.barrier()
        self.marks = getattr(self, "marks", [])
        self.marks.append(sum(1 for o in self.fw.engs["pe"].ops if o[1] is not None))
        self.fw.emit()

    def PS(self):
        rot = self.ps_rot
        b = rot[self.ps_i % len(rot)]
        self.ps_i = (self.ps_i + 1) % len(rot)
        return b

    def WS(self):
        b = self.ws_pool[self.ws_i]
        self.ws_i = (self.ws_i + 1) % len(self.ws_pool)
        return b

    def load_w(self, W, r0, nk, c0, ncols, dst=None, col_off=0):
        fw = self.fw
        ws = dst if dst is not None else self.WS()
        for k0 in range(0, nk, 4):
            k1 = min(nk, k0 + 4)
            src = W[r0 + 128 * k0:r0 + 128 * k1, c0:c0 + ncols].rearrange("(k p) n -> p k n", p=128)
            fw.dma(lambda e, src=src, k0=k0, k1=k1: e.dma_start(out=ws[:, k0:k1, col_off:col_off + ncols], in_=src), writes=[ws], queue="pool")
        return ws

    def col_load(self, dst, vec, n):
        self.fw.dma(lambda e: e.dma_start(out=dst, in_=vec.rearrange("(c p) -> p c", p=128), allow_slow_non_contiguous=True),
                    writes=[dst.buf] if hasattr(dst, "buf") else [])

    def build(self):
        nc, fw, I, O = self.nc, self.fw, self.I, self.O
        g = self.gst
        self.ps_pool = [fw.psum(g, f"ps{i}", [128, 512]) for i in range(8)]
        self.ps_i = 0
        self.ps_rot = self.ps_pool
        self.ws_pool = [fw.sbuf(g, f"ws{i}", [128, 8, 512], BF16) for i in range(3)]
        self.ws_i = 0
        C = self.C = {}
        ident = C["ident"] = fw.sbuf(g, "ident", [128, 128], F32)
        fw.op("pool", lambda e: e.memset(ident[:], 1.0), writes=[ident])
        fw.op("pool", lambda e: e.affine_select(out=ident[:], in_=ident[:], pattern=[[-1, 128]], compare_op=ALU.is_equal,
                                                fill=0.0, base=0, channel_multiplier=1), reads=[ident], writes=[ident])
        identb_ = C["identb"] = fw.sbuf(g, "identb", [128, 128], BF16)
        fw.op("dve", lambda e: e.tensor_copy(out=identb_[:], in_=ident[:]), reads=[ident], writes=[identb_])
        ones_f = C["ones_f"] = fw.sbuf(g, "ones_f", [128, 128], F32)
        fw.op("pool", lambda e: e.memset(ones_f[:], 1.0), writes=[ones_f])
        ones_b = C["ones_b"] = fw.sbuf(g, "ones_b", [128, 128], BF16)
        fw.op("pool", lambda e: e.memset(ones_b[:], 1.0), writes=[ones_b])
        blk_f = C["blk_f"] = fw.sbuf(g, "blk_f", [128, 128], F32)
        fw.op("pool", lambda e: e.memset(blk_f[:], 0.0), writes=[blk_f])
        fw.op("pool", lambda e: e.memset(blk_f[0:64, 0:64], 1.0), writes=[blk_f])
        fw.op("pool", lambda e: e.memset(blk_f[64:128, 64:128], 1.0), writes=[blk_f])
        blk_b = C["blk_b"] = fw.sbuf(g, "blk_b", [128, 128], BF16)
        fw.op("pool", lambda e: e.tensor_copy(out=blk_b[:], in_=blk_f[:]), reads=[blk_f], writes=[blk_b])
        epsc = C["eps"] = fw.sbuf(g, "epsc", [128, 4], F32)
        fw.op("pool", lambda e: e.memset(epsc[:, 0:1], EPS), writes=[epsc])
        fw.op("pool", lambda e: e.memset(epsc[:, 1:2], GN_EPS), writes=[epsc])
        fw.op("pool", lambda e: e.memset(epsc[:, 2:3], 1e-12), writes=[epsc])
        fw.op("pool", lambda e: e.memset(epsc[:, 3:4], 0.0), writes=[epsc])

        self.modT = fw.sbuf(g, "modT", [128, 2, 48, 2], F32)
        self.gains = fw.sbuf(g, "gains", [128, 5, 8], F32)
        for i, v in enumerate([I["norm_mix_g"][0], I["norm_mix_g"][1], I["norm_ffn_g"][0], I["norm_ffn_g"][1], I["norm_final_g"]]):
            gi = i
            fw.dma(lambda e, v=v, gi=gi: e.dma_start(out=self.gains[:, gi, :], in_=v.rearrange("(c p) -> p c", p=128),
                                                     allow_slow_non_contiguous=True), writes=[self.gains])
        self.compute_modulation()
        if self.stop_after == "mod":
            return
        self.compute_lambda()
        self.flush()
        if self.stop_after == "lam":
            return
        self.rec_consts()
        only = os.environ.get("ONLY_PHASE")
        if only != "1":
            self.phase(which=0)
        if only != "0":
            self.phase(which=1)

    def compute_modulation(self):
        fw, I, C = self.fw, self.I, self.C
        with ExitStack() as st:
            cc = fw.sbuf(st, "condc", [128, 8, 2], F32)
            for w in range(2):
                fw.dma(lambda e, w=w: e.dma_start(out=cc[:, :, w], in_=I["cond"][w].rearrange("(c p) -> p c", p=128),
                                                  allow_slow_non_contiguous=True), writes=[cc])
            sc = fw.sbuf(st, "condsb", [128, 8, 2], BF16)
            fw.op("act", lambda e: e.activation(out=sc[:], in_=cc[:], func=AF.Silu), reads=[cc], writes=[sc])
            row = fw.sbuf(st, "modrow", [2, 6 * D], F32)
            brow = fw.sbuf(st, "brow", [2, 6 * D], F32)
            for layer in range(2):
                for w in range(2):
                    fw.dma(lambda e, w=w, layer=layer: e.dma_start(out=brow[w:w + 1, :], in_=I["b_ada"][layer:layer + 1, :]), writes=[brow])
                for blk in range(12):
                    ws = self.load_w(I["w_ada"][layer], 0, 8, blk * 512, 512)
                    ps = self.PS()
                    for k in range(8):
                        fw.op("pe", lambda e, k=k, ws=ws, ps=ps: e.matmul(ps[0:2, :], sc[:, k, :], ws[:, k, 0:512], start=(k == 0), stop=(k == 7)),
                              reads=[sc, ws], writes=[ps])
                    fw.op("dve", lambda e, ps=ps, blk=blk: e.tensor_tensor(out=row[:, blk * 512:(blk + 1) * 512], in0=ps[0:2, :],
                                                                          in1=brow[:, blk * 512:(blk + 1) * 512], op=ALU.add),
                          reads=[ps, brow], writes=[row])
                ps = self.PS()
                for j in range(48):
                    fw.op("pe", lambda e, j=j, ps=ps: e.matmul(ps[:, 2 * j:2 * j + 2], row[0:2, j * 128:(j + 1) * 128], C["ident"][0:2, 0:2],
                                                               start=True, stop=True), reads=[row, C["ident"]], writes=[ps])
                fw.op("dve", lambda e, ps=ps, layer=layer: e.tensor_copy(out=self.modT[:, layer, :, :], in_=ps[:, 0:96].rearrange("p (j w) -> p j w", w=2)),
                      reads=[ps], writes=[self.modT])
            self.flush()

    def compute_lambda(self):
        fw, I = self.fw, self.I
        g = self.gst
        lam_init = 0.8 - 0.6 * math.exp(-0.3 * 0)
        self.lam_init = lam_init
        dl = fw.sbuf(g, "dl", [128, 4, 64], F32)
        fw.dma(lambda e: e.dma_start(out=dl[:].rearrange("p a b -> p (a b)"), in_=I["diff_l"].rearrange("a b -> (a b)").partition_broadcast(128)), writes=[dl])
        pr = fw.sbuf(g, "dlp", [128, 2, 64], F32)
        fw.op("dve", lambda e: e.tensor_tensor(out=pr[:, 0, :], in0=dl[:, 0, :], in1=dl[:, 1, :], op=ALU.mult), reads=[dl], writes=[pr])
        fw.op("dve", lambda e: e.tensor_tensor(out=pr[:, 1, :], in0=dl[:, 2, :], in1=dl[:, 3, :], op=ALU.mult), reads=[dl], writes=[pr])
        sm = fw.sbuf(g, "dls", [128, 2], F32)
        fw.op("dve", lambda e: e.tensor_reduce(out=sm[:], in_=pr[:], axis=AX.X, op=ALU.add), reads=[pr], writes=[sm])
        ex = fw.sbuf(g, "dle", [128, 2], F32)
        fw.op("act", lambda e: e.activation(out=ex[:], in_=sm[:], func=AF.Exp), reads=[sm], writes=[ex])
        lamc = self.lamc = fw.sbuf(g, "lamc", [128, 2], F32)
        fw.op("dve", lambda e: e.tensor_tensor(out=lamc[:, 0:1], in0=ex[:, 1:2], in1=ex[:, 0:1], op=ALU.subtract), reads=[ex], writes=[lamc])
        fw.op("dve", lambda e: e.tensor_scalar(out=lamc[:, 0:1], in0=lamc[:, 0:1], scalar1=-lam_init, scalar2=None, op0=ALU.add), reads=[lamc], writes=[lamc])

    def norm_mod(self, xT, NT, hT, gain_idx, layer, which, shift_i, scale_i, t0=0):
        fw, C = self.fw, self.C
        A = self.tmpA
        if shift_i is not None:
            fw.op("dve", lambda e: e.scalar_tensor_tensor(out=A[:, 0, :], in0=self.modT[:, layer, 8 * scale_i:8 * scale_i + 8, which], scalar=1.0,
                                                          in1=self.gains[:, gain_idx, :], op0=ALU.add, op1=ALU.mult),
                  reads=[self.modT, self.gains], writes=[A])
            fw.op("dve", lambda e: e.tensor_copy(out=A[:, 1, :], in_=self.modT[:, layer, 8 * shift_i:8 * shift_i + 8, which]), reads=[self.modT], writes=[A])
        else:
            fw.op("dve", lambda e: e.tensor_copy(out=A[:, 0, :], in_=self.gains[:, gain_idx, :]), reads=[self.gains], writes=[A])
            fw.op("dve", lambda e: e.memset(A[:, 1, :], 0.0), writes=[A])
        for b0 in range(0, NT, 512):
            ps = self.PS()
            for c in range(8):
                sq = self.ntmp[c % 4]
                fw.op("act", lambda e, c=c, b0=b0, sq=sq: e.activation(out=sq[:].bitcast(BF16)[:, 0:512], in_=xT[:, c, t0 + b0:t0 + b0 + 512], func=AF.Square), reads=[xT], writes=[sq])
                fw.op("pe", lambda e, c=c, ps=ps, sq=sq: e.matmul(ps[:, :], C["ones_b"][:, :], sq[:].bitcast(BF16)[:, 0:512], start=(c == 0), stop=(c == 7)),
                      reads=[sq, C["ones_b"]], writes=[ps])
            rs = self.rstd
            fw.op("act", lambda e, ps=ps: e.activation(out=rs[:], in_=ps[:], func=AF.Sqrt, bias=C["eps"][:, 0:1], scale=1.0 / D), reads=[ps, C["eps"]], writes=[rs])
            fw.op("dve", lambda e: e.reciprocal(out=rs[:], in_=rs[:]), reads=[rs], writes=[rs])
            for c in range(8):
                t = self.ntmp[c % 2]
                fw.op("dve", lambda e, c=c, b0=b0, t=t: e.scalar_tensor_tensor(out=t[:], in0=xT[:, c, t0 + b0:t0 + b0 + 512], scalar=A[:, 0, c:c + 1], in1=rs[:],
                                                                            op0=ALU.mult, op1=ALU.mult), reads=[xT, A, rs], writes=[t])
                fw.op("act", lambda e, c=c, b0=b0, t=t: e.activation(out=hT[:, c, b0:b0 + 512], in_=t[:], func=AF.Identity, bias=A[:, 1, c:c + 1], scale=1.0),
                      reads=[t, A], writes=[hT])

    def proj_fm(self, ws, nk, col0, hT, t0, nt, evac, hk0=0):
        fw = self.fw
        ps = self.PS()
        for k in range(nk):
            fw.op("pe", lambda e, k=k, ps=ps: e.matmul(ps[:, 0:nt], ws[:, k, col0:col0 + 128], hT[:, hk0 + k, t0:t0 + nt], start=(k == 0), stop=(k == nk - 1)),
                  reads=[ws, hT], writes=[ps])
        evac(ps)

    def proj_tm(self, ws, nk, col0, ncols, hT, t0, evac, hk0=0):
        fw = self.fw
        ps = self.PS()
        for k in range(nk):
            fw.op("pe", lambda e, k=k, ps=ps: e.matmul(ps[:, 0:ncols], hT[:, hk0 + k, t0:t0 + 128], ws[:, k, col0:col0 + ncols], start=(k == 0), stop=(k == nk - 1)),
                  reads=[ws, hT], writes=[ps])
        evac(ps)

    def load_xT(self, xT, src, NT):
        fw, C = self.fw, self.C
        for t in range(NT // 128):
            st_ = self.stage[t % 2]
            fw.dma(lambda e, t=t, st_=st_: e.dma_start(out=st_[:], in_=src[t * 128:(t + 1) * 128, :]), writes=[st_])
            for half in range(2):
                ps = self.PS()
                for cc in range(4):
                    c = half * 4 + cc
                    fw.op("pe", lambda e, c=c, cc=cc, ps=ps, st_=st_: e.transpose(ps[:, cc * 128:(cc + 1) * 128], st_[:, c * 128:(c + 1) * 128], C["ident"][:]),
                          reads=[st_, C["ident"]], writes=[ps])
                fw.op("act" if half else "dve",
                      (lambda e, half=half, t=t, ps=ps: e.activation(out=xT[:, half * 4:half * 4 + 4, t * 128:(t + 1) * 128],
                                                                     in_=ps[:].rearrange("p (c n) -> p c n", c=4), func=AF.Identity))
                      if half else
                      (lambda e, half=half, t=t, ps=ps: e.tensor_copy(out=xT[:, half * 4:half * 4 + 4, t * 128:(t + 1) * 128],
                                                                      in_=ps[:].rearrange("p (c n) -> p c n", c=4))),
                      reads=[ps], writes=[xT])

    def store_T(self, srcT, dst, NT):
        fw, C = self.fw, self.C
        for t in range(NT // 128):
            st_ = self.stage[t % 2]
            for half in range(2):
                ps = self.PS()
                for cc in range(4):
                    c = half * 4 + cc
                    fw.op("pe", lambda e, c=c, cc=cc, ps=ps, t=t: e.transpose(ps[:, cc * 128:(cc + 1) * 128], srcT[:, c, t * 128:(t + 1) * 128], C["ident"][:]),
                          reads=[srcT, C["ident"]], writes=[ps])
                if half:
                    fw.op("act", lambda e, ps=ps, st_=st_: e.activation(out=st_[:, 512:1024], in_=ps[:], func=AF.Identity), reads=[ps], writes=[st_])
                else:
                    fw.op("dve", lambda e, ps=ps, st_=st_: e.tensor_copy(out=st_[:, 0:512], in_=ps[:]), reads=[ps], writes=[st_])
            fw.dma(lambda e, t=t, st_=st_: e.dma_start(out=dst[t * 128:(t + 1) * 128, :], in_=st_[:]), reads=[st_], writes=[self.OUTB])

    def ffn(self, xT, hT, t0, ntok, layer, which):
        fw, I = self.fw, self.I
        actT = self.actT
        gate_i = 5
        nb = ntok // 512
        for n in range(11):
            ws = self.WS()
            self.load_w(I["w_ffn_in"][layer], 0, 8, n * 256, 256, dst=ws, col_off=0)
            self.load_w(I["w_ffn_in"][layer], 0, 8, DFF + n * 256, 256, dst=ws, col_off=256)
            for j in range(2):
                i = 2 * n + j
                for b in range(nb):
                    tb = t0 + b * 512
                    sg = self.ntmp[b % 2]

                    def ev_g(ps, sg=sg):
                        fw.op("act", lambda e: e.activation(out=sg[:], in_=ps[:], func=AF.Silu), reads=[ps], writes=[sg])
                    self.proj_fm(ws, 8, j * 128, hT, b * 512, 512, ev_g)

                    def ev_u(ps, sg=sg, i=i, b=b):
                        fw.op("dve", lambda e: e.tensor_tensor(out=actT[:, i, b * 512:(b + 1) * 512], in0=sg[:], in1=ps[:], op=ALU.mult),
                              reads=[sg, ps], writes=[actT])
                    self.proj_fm(ws, 8, 256 + j * 128, hT, b * 512, 512, ev_u)
        for s in range(4):
            ws = self.wsb_pool[s % 2]
            self.load_w(I["w_ffn_out"][layer], 0, 22, s * 256, 256, dst=ws)
            for cc in range(2):
                c = s * 2 + cc
                for b in range(nb):
                    tb = t0 + b * 512

                    def ev(ps, c=c, tb=tb):
                        fw.op("dve", lambda e: e.scalar_tensor_tensor(out=xT[:, c, tb:tb + 512], in0=ps[:], scalar=self.modT[:, layer, 8 * gate_i + c, which:which + 1],
                                                                      in1=xT[:, c, tb:tb + 512], op0=ALU.mult, op1=ALU.add),
                              reads=[ps, self.modT, xT], writes=[xT])
                    self.proj_fm(ws, 22, cc * 128, actT, b * 512, 512, ev)

    def out_proj(self, xT, mixT, W, t0, ntok, layer, which, m0=0):
        fw = self.fw
        for s in range(2):
            ws = self.load_w(W, 0, 8, s * 512, 512)
            for cc in range(4):
                c = s * 4 + cc
                for tb in range(0, ntok, 512):
                    def ev(ps, c=c, tb=tb):
                        fw.op("dve", lambda e: e.scalar_tensor_tensor(out=xT[:, c, t0 + tb:t0 + tb + 512], in0=ps[:], scalar=self.modT[:, layer, 16 + c, which:which + 1],
                                                                      in1=xT[:, c, t0 + tb:t0 + tb + 512], op0=ALU.mult, op1=ALU.add),
                              reads=[ps, self.modT, xT], writes=[xT])
                    self.proj_fm(ws, 8, cc * 128, mixT, m0 + tb, 512, ev)

    def ctx_attention(self, hT, mixT, st):
        fw, I, O, C = self.fw, self.I, self.O, self.C
        NT = 1024
        QTn = fw.sbuf(st, "QTn", [128, 4, NT], BF16)
        KTn = fw.sbuf(st, "KTn", [128, 4, NT], BF16)
        QTd = fw.sbuf(st, "QTd", [128, 4, NT], BF16)
        KTd = fw.sbuf(st, "KTd", [128, 4, NT], BF16)
        Vn = fw.sbuf(st, "Vn", [128, 8, 512], BF16)
        Vd = fw.sbuf(st, "Vd", [128, 8, 512], BF16)
        Wi = I["w_in_attn"]
        if self.stop_after == "nm":
            return

        def fm_into(dst):
            def run(ws):
                for j in range(4):
                    for tb in range(0, NT, 512):
                        def ev(ps, j=j, tb=tb):
                            fw.op("act", lambda e: e.activation(out=dst[:, j, tb:tb + 512], in_=ps[:], func=AF.Identity), reads=[ps], writes=[dst])
                        self.proj_fm(ws, 8, j * 128, hT, tb, 512, ev)
            return run

        def tm_out(dram, vdst):
            def run(ws):
                for t in range(NT // 128):
                    def ev(ps, t=t):
                        sg = self.ntmp[t % 2]
                        fw.op("dve", lambda e: e.tensor_copy(out=sg[:], in_=ps[:]), reads=[ps], writes=[sg])
                        fw.dma(lambda e: e.dma_start(out=dram[t * 128:(t + 1) * 128, :], in_=sg[:]), reads=[sg], writes=[self.OUTB])
                        if vdst is not None:
                            fw.op("act", lambda e: e.activation(out=vdst[:, t, :], in_=sg[:], func=AF.Identity), reads=[sg], writes=[vdst])
                    self.proj_tm(ws, 8, 0, 512, hT, t * 128, ev)
            return run
        plan = [(0, [fm_into(QTn)]), (1, [fm_into(KTn), tm_out(O["o_na_k"], None)]), (2, [tm_out(O["o_na_v"], Vn)]),
                (3, [fm_into(QTd)]), (4, [fm_into(KTd), tm_out(O["o_df_k"], None)]), (5, [tm_out(O["o_df_v"], Vd)])]
        if self.stop_after in ("p0", "p1"):
            plan = plan[2:3] if self.stop_after == "p0" else plan[1:2]
        for slab, fns in plan:
            ws = self.load_w(Wi, 0, 8, slab * 512, 512)
            for f in fns:
                f(ws)
        if self.stop_after in ("proj", "p0", "p1"):
            return
        PT = [fw.sbuf(st, f"PT{i}", [128, 256], BF16) for i in range(4)]
        rd = [fw.sbuf(st, f"rd{i}", [128, 256], F32) for i in range(2)]
        tt = [fw.sbuf(st, f"tt{i}", [128, 256], F32) for i in range(2)]
        def seq_body(s, q0):
            for hp in (range(4) if self.stop_after != "ad" else []):
                for hh in range(2):
                    h = 2 * hp + hh
                    po = 64 * hh
                    psn = self.PS()
                    pts = []
                    for kt in range(2):
                        ps = self.PS()
                        fw.op("pe", lambda e, ps=ps, po=po, hp=hp, kt=kt: e.matmul(ps[:, 0:256], KTn[po:po + 64, hp, q0 + kt * 128:q0 + (kt + 1) * 128],
                                                                                  QTn[po:po + 64, hp, q0:q0 + 256], start=True, stop=True),
                              reads=[KTn, QTn], writes=[ps])
                        pt = PT[(2 * hh + kt) % 4]
                        fw.op("act", lambda e, ps=ps, pt=pt: e.activation(out=pt[:], in_=ps[:, 0:256], func=AF.Exp, scale=0.125), reads=[ps], writes=[pt])
                        pts.append(pt)
                    for kt in range(2):
                        fw.op("pe", lambda e, kt=kt, hp=hp, pt=pts[kt], psn=psn: e.matmul(psn[:, 0:256], Vn[:, 2 * s + kt, 128 * hp:128 * hp + 128], pt[:],
                                                                                        start=(kt == 0), stop=(kt == 1)), reads=[Vn, pts[kt]], writes=[psn])
                    for kt in range(2):
                        fw.op("pe", lambda e, kt=kt, pt=pts[kt], psn=psn: e.matmul(psn[:, 256:512], C["ones_b"][:, :], pt[:],
                                                                                 start=(kt == 0), stop=(kt == 1)), reads=[C["ones_b"], pts[kt]], writes=[psn])
                    r = rd[hh]
                    fw.op("dve", lambda e, r=r, psn=psn, po=po: e.reciprocal(out=r[po:po + 64, :], in_=psn[po:po + 64, 256:512]), reads=[psn], writes=[r])
                    fw.op("dve", lambda e, r=r, psn=psn, hp=hp, po=po: e.tensor_tensor(out=mixT[po:po + 64, hp, q0:q0 + 256], in0=psn[po:po + 64, 0:256],
                                                                                     in1=r[po:po + 64, :], op=ALU.mult), reads=[psn, r], writes=[mixT])
            for j in (range(4) if self.stop_after != "an" else []):
                psA = self.PS()
                psB = self.PS()
                for c in range(2):
                    po = 64 * c
                    psx = psA if c == 0 else psB
                    pts = []
                    for kt in range(2):
                        ps = self.PS()
                        fw.op("pe", lambda e, ps=ps, po=po, j=j, kt=kt: e.matmul(ps[:, 0:256], KTd[po:po + 64, j, q0 + kt * 128:q0 + (kt + 1) * 128],
                                                                                QTd[po:po + 64, j, q0:q0 + 256], start=True, stop=True),
                              reads=[KTd, QTd], writes=[ps])
                        pt = PT[(2 * c + kt) % 4]
                        fw.op("act", lambda e, ps=ps, pt=pt: e.activation(out=pt[:], in_=ps[:, 0:256], func=AF.Exp, scale=0.125), reads=[ps], writes=[pt])
                        pts.append(pt)
                    for kt in range(2):
                        fw.op("pe", lambda e, kt=kt, j=j, pt=pts[kt], psx=psx: e.matmul(psx[:, 0:256], Vd[:, 2 * s + kt, 128 * j:128 * j + 128], pt[:],
                                                                                      start=(kt == 0), stop=(kt == 1)), reads=[Vd, pts[kt]], writes=[psx])
                    for kt in range(2):
                        fw.op("pe", lambda e, kt=kt, pt=pts[kt], psx=psx: e.matmul(psx[:, 256:512], C["ones_b"][:, :], pt[:],
                                                                                 start=(kt == 0), stop=(kt == 1)), reads=[C["ones_b"], pts[kt]], writes=[psx])
                self.diff_combine(psA, psB, rd, tt, mixT, 4 + j, q0, 256)
        for s_ in range(4):
            seq_body(s_, s_ * 256)

    def diff_combine(self, psA, psB, rd, tt, mixT, chunk, q0, n):
        fw, C = self.fw, self.C
        r0, r1 = rd
        t0, t1 = tt
        fw.op("dve", lambda e: e.reciprocal(out=r0[:, 0:n], in_=psA[:, 256:256 + n]), reads=[psA], writes=[r0])
        fw.op("dve", lambda e: e.reciprocal(out=r1[:, 0:n], in_=psB[:, 256:256 + n]), reads=[psB], writes=[r1])
        fw.op("dve", lambda e: e.tensor_tensor(out=t0[:, 0:n], in0=psA[:, 0:n], in1=r0[:, 0:n], op=ALU.mult), reads=[psA, r0], writes=[t0])
        fw.op("dve", lambda e: e.tensor_tensor(out=t1[:, 0:n], in0=psB[:, 0:n], in1=r1[:, 0:n], op=ALU.mult), reads=[psB, r1], writes=[t1])
        fw.op("dve", lambda e: e.scalar_tensor_tensor(out=t0[:, 0:n], in0=t1[:, 0:n], scalar=self.lamc[:, 0:1], in1=t0[:, 0:n], op0=ALU.mult, op1=ALU.add),
              reads=[t0, t1, self.lamc], writes=[t0])
        fw.op("act", lambda e: e.activation(out=t1[:, 0:n], in_=t0[:, 0:n], func=AF.Square), reads=[t0], writes=[t1])
        ps = self.PS()
        fw.op("pe", lambda e: e.matmul(ps[:, 0:n], C["ones_f"][:, :], t1[:, 0:n], start=True, stop=True), reads=[C["ones_f"], t1], writes=[ps])
        fw.op("act", lambda e: e.activation(out=r0[:, 0:n], in_=ps[:, 0:n], func=AF.Sqrt, bias=C["eps"][:, 0:1], scale=1.0 / 128), reads=[ps, C["eps"]], writes=[r0])
        fw.op("dve", lambda e: e.reciprocal(out=r0[:, 0:n], in_=r0[:, 0:n]), reads=[r0], writes=[r0])
        fw.op("dve", lambda e: e.scalar_tensor_tensor(out=mixT[:, chunk, q0:q0 + n], in0=t0[:, 0:n], scalar=1.0 - self.lam_init, in1=r0[:, 0:n],
                                                      op0=ALU.mult, op1=ALU.mult), reads=[t0, r0], writes=[mixT])

    def phase(self, which):
        fw, I, O = self.fw, self.I, self.O
        NT = 1024 if which == 0 else 2048
        with ExitStack() as st:
            xT = fw.sbuf(st, "xT", [128, 8, NT], F32)
            hT = fw.sbuf(st, "hT", [128, 8, 1024], BF16)
            self.rstd = fw.sbuf(st, "rstd", [128, 512], F32)
            self.ntmp = [fw.sbuf(st, f"ntmp{i}", [128, 512], F32) for i in range(4)]
            self.tmpA = fw.sbuf(st, "tmpA", [128, 2, 8], F32)
            with ExitStack() as st2:
                self.stage = [fw.sbuf(st2, f"stage{i}", [128, 1024], F32) for i in range(2)]
                self.load_xT(xT, I["x_prompt"] if which == 0 else I["x_sample"], NT)
                self.flush()
            def dump():
                with ExitStack() as st2:
                    self.stage = [fw.sbuf(st2, f"stage{i}", [128, 1024], F32) for i in range(2)]
                    self.store_T(xT, O["y_prompt"] if which == 0 else O["y_sample"], NT)
                    self.flush()
            if self.stop_after == "load":
                return dump()
            with ExitStack() as st2:
                mixT = fw.sbuf(st2, "mixT", [128, 8, NT], BF16)
                if which == 1:
                    self.hT_group = None
                    self.lat_attention(xT, hT, mixT)
                    if self.stop_after == "mixl":
                        mf = fw.sbuf(st2, "mixf", [128, 8, 512], F32)
                        self.stage = [fw.sbuf(st2, f"stage{i}", [128, 1024], F32) for i in range(2)]
                        for hf in range(4):
                            fw.op("dve", lambda e, hf=hf: e.tensor_copy(out=mf[:], in_=mixT[:, :, hf * 512:(hf + 1) * 512]), reads=[mixT], writes=[mf])
                            self.store_T(mf, O["y_sample"][hf * 512:(hf + 1) * 512, :], 512)
                        self.flush()
                        return
                    self.out_proj(xT, mixT, I["w_out_attn"], 0, NT, 0, which)
                if which == 0:
                    self.norm_mod(xT, NT, hT, 0, 0, which, 0, 1)
                    self.ctx_attention(hT, mixT, st2)
                    if self.stop_after not in ("proj", "attn", "nm", "p0", "p1", "an", "ad"):
                        self.out_proj(xT, mixT, I["w_out_attn"], 0, NT, 0, which)
                self.flush()
            if self.stop_after in ("l0a", "proj", "attn", "nm", "p0", "p1", "an", "ad"):
                return dump()
            self.ffn_all(xT, hT, NT, 0, which)
            if self.stop_after == "l0":
                return dump()
            with ExitStack() as st2:
                mixT = fw.sbuf(st2, "mixT1", [128, 8, NT], BF16)
                self.hT_group = None
                ng = NT // GS
                if which == 0:
                    sched = [([0, 1], g_, True) for g_ in range(ng)]
                else:
                    sched = [([1], g_, False) for g_ in reversed(range(ng))] + [([0], g_, True) for g_ in range(ng)]
                nseq, T = (4, 256) if which == 0 else (1, 2048)
                self.retention(xT, hT, NT, which, mixT, sched, nseq, T)
                if self.stop_after != "ret":
                    self.rwkv(xT, hT, NT, which, mixT, sched, nseq, T)
                if self.stop_after in ("ret", "rw"):
                    mf = fw.sbuf(st2, "mixf", [128, 8, 1024], F32)
                    fw.op("dve", lambda e: e.tensor_copy(out=mf[:], in_=mixT[:, :, 0:1024]), reads=[mixT], writes=[mf])
                    self.stage = [fw.sbuf(st2, f"stage{i}", [128, 1024], F32) for i in range(2)]
                    self.store_T(mf, O["y_prompt"], 1024)
                    self.flush()
                    return
                self.out_proj(xT, mixT, I["w_out_rec"], 0, NT, 1, which)
                self.flush()
            if self.stop_after == "l1a":
                return dump()
            self.ffn_all(xT, hT, NT, 1, which)
            with ExitStack() as st2:
                self.stage = [fw.sbuf(st2, f"stage{i}", [128, 1024], F32) for i in range(2)]
                yT = fw.sbuf(st2, "yT", [128, 8, 512], F32)
                self.final_norm_store(xT, NT, yT, O["y_prompt"] if which == 0 else O["y_sample"])
                self.flush()

    def ffn_all(self, xT, hT, NT, layer, which):
        fw = self.fw
        with ExitStack() as st2:
            self.actT = fw.sbuf(st2, "actT", [128, NFF, 1024], BF16)
            self.wsb_pool = [fw.sbuf(st2, f"wsb{i}", [128, NFF, 256], BF16) for i in range(2)]
            for t0 in range(0, NT, 1024):
                self.norm_mod(xT, 1024, hT, 2 + layer, layer, which, 3, 4, t0=t0)
                self.ffn(xT, hT, t0, 1024, layer, which)
            self.flush()

    def final_norm_store(self, xT, NT, yT, dst):
        fw, C = self.fw, self.C
        for b0 in range(0, NT, 512):
            ps = self.PS()
            for c in range(8):
                sq = self.ntmp[c % 4]
                fw.op("act", lambda e, c=c, b0=b0, sq=sq: e.activation(out=sq[:].bitcast(BF16)[:, 0:512], in_=xT[:, c, b0:b0 + 512], func=AF.Square), reads=[xT], writes=[sq])
                fw.op("pe", lambda e, c=c, ps=ps, sq=sq: e.matmul(ps[:, :], C["ones_b"][:, :], sq[:].bitcast(BF16)[:, 0:512], start=(c == 0), stop=(c == 7)),
                      reads=[sq, C["ones_b"]], writes=[ps])
            rs = self.rstd
            fw.op("act", lambda e, ps=ps: e.activation(out=rs[:], in_=ps[:], func=AF.Sqrt, bias=C["eps"][:, 0:1], scale=1.0 / D), reads=[ps, C["eps"]], writes=[rs])
            fw.op("dve", lambda e: e.reciprocal(out=rs[:], in_=rs[:]), reads=[rs], writes=[rs])
            for c in range(8):
                fw.op("dve", lambda e, c=c, b0=b0: e.scalar_tensor_tensor(out=yT[:, c, :], in0=xT[:, c, b0:b0 + 512], scalar=self.gains[:, 4, c:c + 1], in1=rs[:],
                                                                        op0=ALU.mult, op1=ALU.mult), reads=[xT, self.gains, rs], writes=[yT])
            self.store_T(yT, dst[b0:b0 + 512, :], 512)


    def rec_consts(self):
        fw, I, g = self.fw, self.I, self.gst
        R = self.R = {}
        M = R["M"] = fw.sbuf(g, "masks", [128, 4, 128], F32)
        fw.dma(lambda e: e.dma_start(out=M[:], in_=I["c_masks"].rearrange("m p n -> p m n")), writes=[M])
        DIFF = R["DIFF"] = fw.sbuf(g, "cdiff", [128, 128], F32)
        fw.dma(lambda e: e.dma_start(out=DIFF[:], in_=I["c_diff"]), writes=[DIFF])
        IP = R["IP"] = fw.sbuf(g, "cip", [128, 2, 128], F32)
        fw.dma(lambda e: e.dma_start(out=IP[:], in_=I["c_ip"].rearrange("m p n -> p m n")), writes=[IP])
        JC = R["JC"] = fw.sbuf(g, "cjc", [128, 2], F32)
        fw.dma(lambda e: e.dma_start(out=JC[:], in_=I["c_jc"]), writes=[JC])
        HALF = R["HALF"] = fw.sbuf(g, "chalf", [128, 2], F32)
        fw.dma(lambda e: e.dma_start(out=HALF[:], in_=I["c_half"]), writes=[HALF])
        lg = R["lg"] = fw.sbuf(g, "lgrep", [128, 16], F32)
        nlg = R["nlg"] = fw.sbuf(g, "nlgrep", [128, 16], F32)
        fw.dma(lambda e: e.dma_start(out=lg[:], in_=I["ret_decay_logit"].partition_broadcast(128)), writes=[lg])
        fw.op("act", lambda e: e.activation(out=nlg[:], in_=lg[:], func=AF.Exp, scale=-1.0), reads=[lg], writes=[nlg])
        fw.op("dve", lambda e: e.tensor_scalar(out=nlg[:], in0=nlg[:], scalar1=1.0, scalar2=None, op0=ALU.add), reads=[nlg], writes=[nlg])
        fw.op("act", lambda e: e.activation(out=nlg[:], in_=nlg[:], func=AF.Ln), reads=[nlg], writes=[nlg])
        fw.op("dve", lambda e: e.tensor_scalar(out=lg[:], in0=nlg[:], scalar1=-1.0, scalar2=None, op0=ALU.mult), reads=[nlg], writes=[lg])
        LGc = R["LGc"] = fw.sbuf(g, "lgcol", [128, 8], F32)
        tmp = fw.sbuf(g, "lgtmp", [128, 8], F32)
        lgv = lg[:].rearrange("p (a two) -> p a two", two=2)
        fw.op("dve", lambda e: e.tensor_scalar(out=LGc[:], in0=lgv[:, :, 0], scalar1=HALF[:, 0:1], scalar2=None, op0=ALU.mult), reads=[lg, HALF], writes=[LGc])
        fw.op("dve", lambda e: e.tensor_scalar(out=tmp[:], in0=lgv[:, :, 1], scalar1=HALF[:, 1:2], scalar2=None, op0=ALU.mult), reads=[lg, HALF], writes=[tmp])
        fw.op("dve", lambda e: e.tensor_tensor(out=LGc[:], in0=LGc[:], in1=tmp[:], op=ALU.add), reads=[LGc, tmp], writes=[LGc])
        CD = R["CD"] = fw.sbuf(g, "cdec", [128, 8], F32)
        fw.op("act", lambda e: e.activation(out=CD[:], in_=LGc[:], func=AF.Exp, scale=128.0), reads=[LGc], writes=[CD])
        KD = R["KD"] = fw.sbuf(g, "kdec", [128, 16], F32)
        fw.op("act", lambda e: e.activation(out=KD[:, 0:8], in_=lg[:, 0:8], func=AF.Exp, scale=JC[:, 0:1]), reads=[lg, JC], writes=[KD])
        fw.op("act", lambda e: e.activation(out=KD[:, 8:16], in_=lg[:, 8:16], func=AF.Exp, scale=JC[:, 1:2]), reads=[lg, JC], writes=[KD])
        def cols(name, vec, n):
            t = fw.sbuf(g, name, [128, n], F32)
            fw.dma(lambda e: e.dma_start(out=t[:], in_=vec.rearrange("(c p) -> p c", p=128), allow_slow_non_contiguous=True), writes=[t])
            return t
        R["KKc"] = cols("kkc", I["rw_k_k"], 4)
        R["KAc"] = cols("kac", I["rw_k_a"], 4)
        R["RKc"] = cols("rkc", I["rw_r_k"], 4)
        R["LGNc"] = cols("lngc", I["rw_ln_g"], 4)
        R["LNBc"] = cols("lnbc", I["rw_ln_b"], 4)
        A0c = R["A0c"] = fw.sbuf(g, "a0c", [128, 2, 4], F32)
        for d in range(2):
            fw.dma(lambda e, d=d: e.dma_start(out=A0c[:, d, :], in_=I["rw_a0"][d].rearrange("(c p) -> p c", p=128), allow_slow_non_contiguous=True), writes=[A0c])

        self.flush()

    def blk_stat(self, src, n, scale, eps_col, out):
        fw, C = self.fw, self.C
        sq = self.ntmp[3]
        srcbuf = self._srcbuf
        fw.op("act", lambda e: e.activation(out=sq[:].bitcast(BF16)[:, 0:n], in_=src, func=AF.Square), reads=[srcbuf], writes=[sq])
        ps = self.PS()
        fw.op("pe", lambda e: e.matmul(ps[:, 0:n], C["blk_b"][:, :], sq[:].bitcast(BF16)[:, 0:n], start=True, stop=True), reads=[C["blk_b"], sq], writes=[ps])
        fw.op("act", lambda e: e.activation(out=out[:, 0:n], in_=ps[:, 0:n], func=AF.Sqrt, bias=C["eps"][:, eps_col:eps_col + 1], scale=scale), reads=[ps, C["eps"]], writes=[out])
        fw.op("dve", lambda e: e.reciprocal(out=out[:, 0:n], in_=out[:, 0:n]), reads=[out], writes=[out])

    def retention(self, xT, hT, NT, which, mixT, sched, nseq, T):
        fw, I, O, C, R = self.fw, self.I, self.O, self.C, self.R
        Wi = I["w_in_rec"]
        with ExitStack() as st:
            wsl = list(self.ws_pool) + [fw.sbuf(st, "rws3", [128, 8, 512], BF16)]
            for i in range(4):
                self.load_w(Wi, 0, 8, i * 512, 512, dst=wsl[i])
            qT = fw.sbuf(st, "r_qT", [128, GS], BF16)
            kT = fw.sbuf(st, "r_kT", [128, GS], BF16)
            sgT = fw.sbuf(st, "r_sgT", [128, GS], BF16)
            vTr = fw.sbuf(st, "r_vT", [128, GS], BF16)
            Ktok = fw.sbuf(st, "r_Ktok", [128, NTG, 128], BF16)
            Vpad = fw.sbuf(st, "r_Vpad", [128, NTG, 2, 128], BF16)
            fw.op("pool", lambda e: e.memset(Vpad[:], 0.0), writes=[Vpad])
            M, DIFF, IP, lg, nlg, LGc = R["M"], R["DIFF"], R["IP"], R["lg"], R["nlg"], R["LGc"]
            RM = R["RM"] = fw.sbuf(st, "retmask", [128, 16, 128], F32)
            for dh in range(16):
                d = dh // 8
                src = lg if d == 0 else nlg
                fw.op("act", lambda e, dh=dh, src=src: e.activation(out=RM[:, dh, :], in_=DIFF[:], func=AF.Exp, scale=src[:, dh:dh + 1]), reads=[DIFF, src], writes=[RM])
                mi = 1 if d == 0 else 3
                fw.op("dve", lambda e, dh=dh, mi=mi: e.tensor_tensor(out=RM[:, dh, :], in0=RM[:, dh, :], in1=M[:, mi, :], op=ALU.mult), reads=[RM, M], writes=[RM])
            QD = R["QD"] = fw.sbuf(st, "qdec", [128, 8, 128], F32)
            for dp in range(8):
                fw.op("act", lambda e, dp=dp: e.activation(out=QD[:, dp, :], in_=IP[:, dp // 4, :], func=AF.Exp, scale=LGc[:, dp:dp + 1]), reads=[IP, LGc], writes=[QD])
            acc = fw.sbuf(st, "r_acc", [128, NT], F32)
            SM = [fw.sbuf(st, f"r_SM{i}", [128, 128], BF16) for i in range(2)]
            KS = [fw.sbuf(st, f"r_KS{i}", [128, 128], BF16) for i in range(2)]
            for t_ in KS:
                fw.op("pool", lambda e, t_=t_: e.memset(t_[:], 0.0), writes=[t_])
            qd = fw.sbuf(st, "r_qd", [128, 128], BF16)
            RS = [fw.sbuf(st, f"r_RS{d}", [128, 128], F32) for d in range(2)]
            RSb = [fw.sbuf(st, f"r_RSb{d}", [128, 128], BF16) for d in range(2)]
            rs_t = fw.sbuf(st, "r_rst", [128, 512], F32)
            om = fw.sbuf(st, "r_om", [128, 512], F32)
            nchunk_seq = T // 128

            def project(p, grp):
                for tb in range(0, GS, 512):
                    self.proj_fm(wsl[0], 8, 128 * p, hT, self.hoff + tb, 512, lambda ps, tb=tb: fw.op("act", lambda e: e.activation(out=qT[:, tb:tb + 512], in_=ps[:], func=AF.Identity), reads=[ps], writes=[qT]))
                    self.proj_fm(wsl[1], 8, 128 * p, hT, self.hoff + tb, 512, lambda ps, tb=tb: fw.op("act", lambda e: e.activation(out=kT[:, tb:tb + 512], in_=ps[:], func=AF.Identity, scale=0.125), reads=[ps], writes=[kT]))
                    self.proj_fm(wsl[3], 8, 128 * p, hT, self.hoff + tb, 512, lambda ps, tb=tb: fw.op("act", lambda e: e.activation(out=sgT[:, tb:tb + 512], in_=ps[:], func=AF.Silu), reads=[ps], writes=[sgT]))
                for tb in range(0, GS, 512):
                    self.proj_fm(wsl[2], 8, 128 * p, hT, self.hoff + tb, 512, lambda ps, tb=tb: fw.op("act", lambda e: e.activation(out=vTr[:, tb:tb + 512], in_=ps[:], func=AF.Identity), reads=[ps], writes=[vTr]))
                for t in range(NTG):
                    psk = self.PS()
                    fw.op("pe", lambda e, psk=psk, t=t: e.matmul(psk[:, 0:128], kT[:, t * 128:(t + 1) * 128], C["identb"][:], start=True, stop=True), reads=[kT, C["identb"]], writes=[psk])
                    fw.op("act", lambda e, psk=psk, t=t: e.activation(out=Ktok[:, t, :], in_=psk[:, 0:128], func=AF.Identity), reads=[psk], writes=[Ktok])
                    psv = self.PS()
                    fw.op("pe", lambda e, psv=psv, t=t: e.matmul(psv[:, 0:128], vTr[:, t * 128:(t + 1) * 128], C["identb"][:], start=True, stop=True), reads=[vTr, C["identb"]], writes=[psv])
                    fw.op("dve", lambda e, psv=psv, t=t: e.tensor_copy(out=Vpad[:, t, 0, 0:64], in_=psv[:, 0:64]), reads=[psv], writes=[Vpad])
                    fw.op("dve", lambda e, psv=psv, t=t: e.tensor_copy(out=Vpad[:, t, 1, 64:128], in_=psv[:, 64:128]), reads=[psv], writes=[Vpad])

            def chunk(p, d, t, gtok, first):
                c0 = t * 128
                for hh in range(2):
                    h = 2 * p + hh
                    po = 64 * hh
                    ps = self.PS()
                    fw.op("pe", lambda e, ps=ps, po=po: e.matmul(ps[:, 0:128], kT[po:po + 64, c0:c0 + 128], qT[po:po + 64, c0:c0 + 128], start=True, stop=True), reads=[kT, qT], writes=[ps])
                    fw.op("dve", lambda e, ps=ps, hh=hh, h=h: e.tensor_tensor(out=SM[hh][:], in0=ps[:, 0:128], in1=R["RM"][:, d * 8 + h, :], op=ALU.mult), reads=[ps, R["RM"]], writes=[SM[hh]])
                    fw.op("dve", lambda e, hh=hh, h=h, po=po: e.tensor_scalar(out=KS[hh][:, po:po + 64], in0=Ktok[:, t, po:po + 64], scalar1=R["KD"][:, d * 8 + h:d * 8 + h + 1], scalar2=None, op0=ALU.mult),
                          reads=[Ktok, R["KD"]], writes=[KS[hh]])
                fw.op("dve", lambda e: e.tensor_tensor(out=qd[:], in0=qT[:, c0:c0 + 128], in1=R["QD"][:, d * 4 + p, :], op=ALU.mult), reads=[qT, R["QD"]], writes=[qd])
                pso = self.PS()
                fw.op("pe", lambda e: e.matmul(pso[:, 0:128], RSb[d][:, :], qd[:], start=True, stop=False), reads=[RSb[d], qd], writes=[pso])
                for hh in range(2):
                    fw.op("pe", lambda e, hh=hh: e.matmul(pso[:, 0:128], Vpad[:, t, hh, :], SM[hh][:], start=False, stop=(hh == 1)), reads=[Vpad, SM[hh]], writes=[pso])
                a0 = gtok + c0
                if first:
                    fw.op("act", lambda e: e.activation(out=acc[:, a0:a0 + 128], in_=pso[:, 0:128], func=AF.Identity), reads=[pso], writes=[acc])
                else:
                    fw.op("dve", lambda e: e.tensor_tensor(out=acc[:, a0:a0 + 128], in0=pso[:, 0:128], in1=acc[:, a0:a0 + 128], op=ALU.add), reads=[pso, acc], writes=[acc])
                psS = self.PS()
                for hh in range(2):
                    fw.op("pe", lambda e, hh=hh: e.matmul(psS[:, 0:128], KS[hh][:], Vpad[:, t, hh, :], start=(hh == 0), stop=(hh == 1)), reads=[KS[hh], Vpad], writes=[psS])
                fw.op("dve", lambda e: e.scalar_tensor_tensor(out=RS[d][:], in0=RS[d][:], scalar=R["CD"][:, d * 4 + p:d * 4 + p + 1], in1=psS[:, 0:128], op0=ALU.mult, op1=ALU.add),
                      reads=[RS[d], R["CD"], psS], writes=[RS[d]])
                fw.op("act", lambda e: e.activation(out=RSb[d][:], in_=RS[d][:], func=AF.Identity), reads=[RS[d]], writes=[RSb[d]])

            def init_state(p, d):
                fw.op("pool", lambda e: e.memset(RS[d][:], 0.0), writes=[RS[d]])
                if which == 1:
                    for hh in range(2):
                        po = 64 * hh
                        fw.dma(lambda e, hh=hh, po=po: e.dma_start(out=RS[d][po:po + 64, po:po + 64], in_=I["state_ret"][d, 2 * p + hh]), writes=[RS[d]])
                fw.op("act", lambda e: e.activation(out=RSb[d][:], in_=RS[d][:], func=AF.Identity), reads=[RS[d]], writes=[RSb[d]])

            def out_state(p, d, s):
                for hh in range(2):
                    po = 64 * hh
                    fw.dma(lambda e, hh=hh, po=po: e.dma_start(out=O["o_sret"][s, d, 2 * p + hh], in_=RS[d][po:po + 64, po:po + 64]), reads=[RS[d]], writes=[self.OUTB])

            def finalize(p, grp):
                g0 = grp * GS
                for tb in range(0, GS, 512):
                    self._srcbuf = acc
                    self.blk_stat(acc[:, g0 + tb:g0 + tb + 512], 512, 1.0 / 64, 0, rs_t)
                    fw.op("dve", lambda e, tb=tb: e.tensor_tensor(out=om[:], in0=acc[:, g0 + tb:g0 + tb + 512], in1=rs_t[:], op=ALU.mult), reads=[acc, rs_t], writes=[om])
                    fw.op("dve", lambda e, tb=tb: e.tensor_tensor(out=mixT[:, p, g0 + tb:g0 + tb + 512], in0=om[:], in1=sgT[:, tb:tb + 512], op=ALU.mult), reads=[om, sgT], writes=[mixT])

            self.run_sched(xT, hT, which, sched, nseq, nchunk_seq, project, chunk, init_state, out_state, finalize)
            self.flush()

    def run_sched(self, xT, hT, which, sched, nseq, nchunk_seq, project, chunk, init_state, out_state, finalize, prep_dir=None):
        ngroups = (1024 if which == 0 else 2048) // GS
        for p in range(4):
            seen_first = set()
            for (dirs, grp, fin) in sched:
                blk = (grp * GS) // 1024
                if self.hT_group != blk:
                    self.norm_mod(xT, 1024, hT, 1, 1, which, 0, 1, t0=blk * 1024)
                    self.hT_group = blk
                    self.lora_group = None
                self.hoff = (grp * GS) % 1024
                project(p, grp)
                for d in dirs:
                    if prep_dir is not None:
                        prep_dir(p, d)
                    if which == 0:
                        spg = GS // (nchunk_seq * 128)
                        for sl in range(spg):
                            init_state(p, d)
                            tiles = list(range(sl * nchunk_seq, (sl + 1) * nchunk_seq))
                            if d == 1:
                                tiles = tiles[::-1]
                            for t in tiles:
                                chunk(p, d, t, grp * GS, (grp, t) not in seen_first)
                                seen_first.add((grp, t))
                            out_state(p, d, grp * spg + sl)
                    else:
                        start_grp = 0 if d == 0 else ngroups - 1
                        if grp == start_grp:
                            init_state(p, d)
                        tiles = list(range(NTG))
                        if d == 1:
                            tiles = tiles[::-1]
                        for t in tiles:
                            chunk(p, d, t, grp * GS, (grp, t) not in seen_first)
                            seen_first.add((grp, t))
                if fin:
                    finalize(p, grp)

    def rwkv(self, xT, hT, NT, which, mixT, sched, nseq, T):
        fw, I, O, C, R = self.fw, self.I, self.O, self.C, self.R
        Wi = I["w_in_rec"]
        M = R["M"]
        NEG_E = -math.exp(-0.5)
        with ExitStack() as st:
            wsl = self.ws_pool
            for i in range(3):
                self.load_w(Wi, 0, 8, 2048 + i * 512, 512, dst=wsl[i])
            W0r = R["W0r"] = fw.sbuf(st, "w0r", [128, 2, 128], F32)
            A0r = R["A0r"] = fw.sbuf(st, "a0r", [128, 2, 128], F32)
            KKr = R["KKr"] = fw.sbuf(st, "kkr", [128, 128], F32)
            KAr = R["KAr"] = fw.sbuf(st, "kar", [128, 128], F32)
            wup = R["wup"] = fw.sbuf(st, "wupb", [128, 2, 128], BF16)
            aup = R["aup"] = fw.sbuf(st, "aupb", [128, 2, 128], BF16)
            gup = R["gup"] = fw.sbuf(st, "gupb", [128, 512], BF16)
            fw.dma(lambda e: e.dma_start(out=gup[:], in_=I["rw_g_up"]), writes=[gup], queue="pool")

            def load_pair_params(p):
                pc = 128 * p
                for d in range(2):
                    fw.dma(lambda e, d=d: e.dma_start(out=W0r[:, d, :], in_=I["rw_w0"][d, pc:pc + 128].partition_broadcast(128)), writes=[W0r])
                    fw.dma(lambda e, d=d: e.dma_start(out=A0r[:, d, :], in_=I["rw_a0"][d, pc:pc + 128].partition_broadcast(128)), writes=[A0r])
                fw.dma(lambda e: e.dma_start(out=KKr[:], in_=I["rw_k_k"][pc:pc + 128].partition_broadcast(128)), writes=[KKr])
                fw.dma(lambda e: e.dma_start(out=KAr[:], in_=I["rw_k_a"][pc:pc + 128].partition_broadcast(128)), writes=[KAr])
                fw.dma(lambda e: e.dma_start(out=wup[0:64, :, :], in_=I["rw_w_up"][:, :, pc:pc + 128].rearrange("d k n -> k d n")), writes=[wup], queue="pool")
                fw.dma(lambda e: e.dma_start(out=aup[64:128, :, :], in_=I["rw_a_up"][:, :, pc:pc + 128].rearrange("d k n -> k d n")), writes=[aup], queue="pool")
            self._pair_loaded = None
            wsx = fw.sbuf(st, "wwsx", [128, 8, 256], BF16)
            self.load_w(Wi, 0, 8, 3584, 256, dst=wsx)
            sb = lambda name, shape, dt=F32: fw.sbuf(st, name, shape, dt)
            lora = sb("w_lora", [128, GS], BF16)
            sgd = sb("w_sgd", [128, GS], BF16)
            rT = sb("w_rT", [128, GS], BF16); kT = sb("w_kT", [128, GS]); vT = sb("w_vT", [128, GS], BF16); kkT = sb("w_kkT", [128, GS], BF16)
            ktok = sb("w_ktok", [128, NTG, 128]); kktok = sb("w_kktok", [128, NTG, 128])
            Vpad = sb("w_Vpad", [128, NTG, 2, 128], BF16)
            fw.op("pool", lambda e: e.memset(Vpad[:], 0.0), writes=[Vpad])
            keffT = sb("w_keffT", [128, GS], BF16); bT = sb("w_bT", [128, GS], BF16)
            atok = sb("w_atok", [128, NTG, 128]); kefftok = sb("w_kefftok", [128, NTG, 128]); btok = sb("w_btok", [128, NTG, 128]); wlog = sb("w_wlog", [128, NTG, 128])
            acc = sb("w_acc", [128, NT])
            ssq = sb("w_ssq", [128, 2 * NTG])
            eL = sb("w_eL", [128, 128]); eLp = sb("w_eLp", [128, 128]); enL = sb("w_enL", [128, 128]); eD = sb("w_eD", [128, 128])
            AR = sb("w_AR", [128, 2, 128], BF16); BT = sb("w_BT", [128, 128], BF16); KTt = sb("w_KTt", [128, 128], BF16)
            pad = lambda name: [sb(f"{name}{i}", [128, 128], BF16) for i in range(2)]
            Bp, Kp, Up = pad("w_Bp"), pad("w_Kp"), pad("w_Up")
            for t_ in Bp + Kp + Up:
                fw.op("pool", lambda e, t_=t_: e.memset(t_[:], 0.0), writes=[t_])
            Mbr, Mak, Mkr = pad("w_Mbr"), pad("w_Mak"), pad("w_Mkr")
            XA = [[sb(f"w_X{h}{i}", [128, 2, 128], BF16) for i in range(2)] for h in range(2)]
            YA = [[sb(f"w_Y{h}{i}", [128, 128], BF16) for i in range(2)] for h in range(2)]
            XP = [[Buf("xp", XA[h][i].t) for i in range(2)] for h in range(2)]
            XT_ = [[Buf("xt", XA[h][i].t) for i in range(2)] for h in range(2)]
            XTs = [sb(f"w_XTs{i}", [128, 64], BF16) for i in range(2)]
            identb = sb("w_identb", [128, 128], BF16)
            fw.op("dve", lambda e: e.tensor_copy(out=identb[:], in_=C["ident"][:]), reads=[C["ident"]], writes=[identb])
            ST = [sb(f"w_ST{d}", [128, 128]) for d in range(2)]
            STb = [sb(f"w_STb{d}", [128, 128], BF16) for d in range(2)]
            stg = sb("w_stg", [128, 128])
            t512 = self.ntmp[0:3]
            nchunk_seq = T // 128

            def lora_inputs():
                for tb in range(0, GS, 512):
                    def ev1(ps, tb=tb):
                        fw.op("act", lambda e: e.activation(out=lora[0:64, tb:tb + 512], in_=ps[0:64, :], func=AF.Tanh), reads=[ps], writes=[lora])
                        fw.op("act", lambda e: e.activation(out=lora[64:128, tb:tb + 512], in_=ps[64:128, :], func=AF.Identity), reads=[ps], writes=[lora])
                    self.proj_fm(wsx, 8, 0, hT, self.hoff + tb, 512, ev1)
                    self.proj_fm(wsx, 8, 128, hT, self.hoff + tb, 512, lambda ps, tb=tb: fw.op("act", lambda e: e.activation(out=sgd[:, tb:tb + 512], in_=ps[:], func=AF.Sigmoid), reads=[ps], writes=[sgd]))

            def project(p, grp):
                if self._pair_loaded != p:
                    load_pair_params(p)
                    self._pair_loaded = p
                if self.lora_group != grp:
                    lora_inputs()
                    self.lora_group = grp
                pc = 128 * p
                for tb in range(0, GS, 512):
                    for w_, dst in ((wsl[0], rT), (wsl[1], kT), (wsl[2], vT)):
                        self.proj_fm(w_, 8, pc, hT, self.hoff + tb, 512, lambda ps, tb=tb, dst=dst: fw.op("act", lambda e: e.activation(out=dst[:, tb:tb + 512], in_=ps[:], func=AF.Identity), reads=[ps], writes=[dst]))
                    t1, t2 = t512[0], t512[1]
                    fw.op("dve", lambda e, tb=tb: e.tensor_scalar(out=t1[:], in0=kT[:, tb:tb + 512], scalar1=R["KKc"][:, p:p + 1], scalar2=None, op0=ALU.mult), reads=[kT, R["KKc"]], writes=[t1])
                    self._srcbuf = t1
                    self.blk_stat(t1[:], 512, 1.0, 2, t2)
                    fw.op("dve", lambda e, tb=tb: e.tensor_tensor(out=kkT[:, tb:tb + 512], in0=t1[:], in1=t2[:], op=ALU.mult), reads=[t1, t2], writes=[kkT])
                for t in range(NTG):
                    psk = self.PS()
                    fw.op("pe", lambda e, psk=psk, t=t: e.transpose(psk[:, 0:128], kT[:, t * 128:(t + 1) * 128], C["ident"][:]), reads=[kT, C["ident"]], writes=[psk])
                    fw.op("act", lambda e, psk=psk, t=t: e.activation(out=ktok[:, t, :], in_=psk[:, 0:128], func=AF.Identity), reads=[psk], writes=[ktok])
                    psv = self.PS()
                    fw.op("pe", lambda e, psv=psv, t=t: e.matmul(psv[:, 0:128], vT[:, t * 128:(t + 1) * 128], C["identb"][:], start=True, stop=True), reads=[vT, C["identb"]], writes=[psv])
                    fw.op("dve", lambda e, psv=psv, t=t: e.tensor_copy(out=Vpad[:, t, 0, 0:64], in_=psv[:, 0:64]), reads=[psv], writes=[Vpad])
                    fw.op("dve", lambda e, psv=psv, t=t: e.tensor_copy(out=Vpad[:, t, 1, 64:128], in_=psv[:, 64:128]), reads=[psv], writes=[Vpad])
                kk3 = kktok[:].rearrange("p t (h f) -> p (t h) f", f=64)
                fw.op("dve", lambda e: e.tensor_tensor(out=kktok[:], in0=ktok[:], in1=R["KKr"][:, :].unsqueeze(1).broadcast_to([128, NTG, 128]), op=ALU.mult),
                      reads=[ktok, R["KKr"]], writes=[kktok])
                sq = sb_sq
                fw.op("act", lambda e: e.activation(out=sq[:], in_=kktok[:], func=AF.Square), reads=[kktok], writes=[sq])
                fw.op("dve", lambda e: e.tensor_reduce(out=ssq[:], in_=sq[:].rearrange("p t (h f) -> p (t h) f", f=64), axis=AX.X, op=ALU.add), reads=[sq], writes=[ssq])
                fw.op("act", lambda e: e.activation(out=ssq[:], in_=ssq[:], func=AF.Sqrt, bias=C["eps"][:, 2:3], scale=1.0), reads=[ssq, C["eps"]], writes=[ssq])
                fw.op("dve", lambda e: e.reciprocal(out=ssq[:], in_=ssq[:]), reads=[ssq], writes=[ssq])
                fw.op("dve", lambda e: e.tensor_tensor(out=kk3, in0=kk3, in1=ssq[:].unsqueeze(2).broadcast_to([128, 2 * NTG, 64]), op=ALU.mult), reads=[kktok, ssq], writes=[kktok])

            def prep_dir(p, d):
                pc = 128 * p
                for tb in range(0, GS, 512):
                    ps = self.PS()
                    fw.op("pe", lambda e, ps=ps, tb=tb: e.matmul(ps[:, :], R["aup"][64:128, d, :], lora[64:128, tb:tb + 512], start=True, stop=True), reads=[R["aup"], lora], writes=[ps])
                    aTt = t512[2]
                    fw.op("act", lambda e, ps=ps, tb=tb: e.activation(out=aTt[:], in_=ps[:], func=AF.Sigmoid, bias=R["A0c"][:, d, p:p + 1], scale=1.0), reads=[ps, R["A0c"]], writes=[aTt])
                    t1 = t512[0]
                    fw.op("dve", lambda e, tb=tb: e.tensor_scalar(out=t1[:], in0=aTt[:], scalar1=-1.0, scalar2=None, op0=ALU.add), reads=[aTt], writes=[t1])
                    fw.op("dve", lambda e, tb=tb: e.tensor_scalar(out=t1[:], in0=t1[:], scalar1=R["KAc"][:, p:p + 1], scalar2=None, op0=ALU.mult), reads=[t1, R["KAc"]], writes=[t1])
                    fw.op("dve", lambda e, tb=tb: e.scalar_tensor_tensor(out=keffT[:, tb:tb + 512], in0=t1[:], scalar=1.0, in1=kT[:, tb:tb + 512], op0=ALU.add, op1=ALU.mult), reads=[t1, kT], writes=[keffT])
                    fw.op("dve", lambda e, tb=tb: e.tensor_tensor(out=bT[:, tb:tb + 512], in0=kkT[:, tb:tb + 512], in1=aTt[:], op=ALU.mult), reads=[kkT, aTt], writes=[bT])
                for t in range(NTG):
                    ps = self.PS()
                    psb = self.PS()
                    fw.op("pe", lambda e, ps=ps, t=t: e.matmul(ps[:, 0:128], lora[64:128, t * 128:(t + 1) * 128], R["aup"][64:128, d, :], start=True, stop=True), reads=[R["aup"], lora], writes=[ps])
                    fw.op("pe", lambda e, psb=psb, t=t: e.matmul(psb[:, 0:128], lora[0:64, t * 128:(t + 1) * 128], R["wup"][0:64, d, :], start=True, stop=True), reads=[R["wup"], lora], writes=[psb])
                    fw.op("dve", lambda e, ps=ps, t=t: e.tensor_tensor(out=atok[:, t, :], in0=ps[:, 0:128], in1=R["A0r"][:, d, :], op=ALU.add), reads=[ps, R["A0r"]], writes=[atok])
                    fw.op("dve", lambda e, psb=psb, t=t: e.tensor_tensor(out=wlog[:, t, :], in0=psb[:, 0:128], in1=R["W0r"][:, d, :], op=ALU.add), reads=[psb, R["W0r"]], writes=[wlog])
                fw.op("act", lambda e: e.activation(out=atok[:], in_=atok[:], func=AF.Sigmoid), reads=[atok], writes=[atok])
                fw.op("act", lambda e: e.activation(out=wlog[:], in_=wlog[:], func=AF.Sigmoid), reads=[wlog], writes=[wlog])
                fw.op("dve", lambda e: e.tensor_scalar(out=wlog[:], in0=wlog[:], scalar1=NEG_E, scalar2=None, op0=ALU.mult), reads=[wlog], writes=[wlog])
                kar = R["KAr"][:, :].unsqueeze(1).broadcast_to([128, NTG, 128])
                fw.op("dve", lambda e: e.scalar_tensor_tensor(out=kefftok[:], in0=atok[:], scalar=-1.0, in1=kar, op0=ALU.add, op1=ALU.mult), reads=[atok, R["KAr"]], writes=[kefftok])
                fw.op("dve", lambda e: e.scalar_tensor_tensor(out=kefftok[:], in0=kefftok[:], scalar=1.0, in1=ktok[:], op0=ALU.add, op1=ALU.mult), reads=[kefftok, ktok], writes=[kefftok])
                fw.op("dve", lambda e: e.tensor_tensor(out=btok[:], in0=kktok[:], in1=atok[:], op=ALU.mult), reads=[kktok, atok], writes=[btok])

            import os
            DBG = int(os.environ.get("RWDBG", "9"))

            def chunk(p, d, t, gtok, first):
                c0 = t * 128
                if DBG < 2:
                    return
                incl, excl, after = (1, 0, 2) if d == 0 else (3, 2, 0)
                m_s, m_i, m_t = (0, 1, 2) if d == 0 else (2, 3, 0)
                psL = self.PS()
                fw.op("pe", lambda e: e.matmul(psL[:, 0:128], wlog[:, t, :], M[:, incl, :], start=True, stop=True), reads=[wlog, M], writes=[psL])
                fw.op("pe", lambda e: e.matmul(psL[:, 128:256], wlog[:, t, :], M[:, excl, :], start=True, stop=True), reads=[wlog, M], writes=[psL])
                fw.op("pe", lambda e: e.matmul(psL[:, 256:384], M[:, after, :], wlog[:, t, :], start=True, stop=True), reads=[wlog, M], writes=[psL])
                fw.op("act", lambda e: e.activation(out=eL[:], in_=psL[:, 0:128], func=AF.Exp), reads=[psL], writes=[eL])
                fw.op("act", lambda e: e.activation(out=eLp[:], in_=psL[:, 128:256], func=AF.Exp), reads=[psL], writes=[eLp])
                fw.op("act", lambda e: e.activation(out=enL[:], in_=psL[:, 0:128], func=AF.Exp, scale=-1.0), reads=[psL], writes=[enL])
                fw.op("act", lambda e: e.activation(out=eD[:], in_=psL[:, 256:384], func=AF.Exp), reads=[psL], writes=[eD])
                fw.op("dve", lambda e: e.scalar_tensor_tensor(out=AR[:, 0, :], in0=kkT[:, c0:c0 + 128], scalar=-1.0, in1=eLp[:], op0=ALU.mult, op1=ALU.mult), reads=[kkT, eLp], writes=[AR])
                fw.op("dve", lambda e: e.tensor_tensor(out=AR[:, 1, :], in0=rT[:, c0:c0 + 128], in1=eL[:], op=ALU.mult), reads=[rT, eL], writes=[AR])
                fw.op("dve", lambda e: e.tensor_tensor(out=BT[:], in0=bT[:, c0:c0 + 128], in1=enL[:], op=ALU.mult), reads=[bT, enL], writes=[BT])
                fw.op("dve", lambda e: e.tensor_tensor(out=KTt[:], in0=keffT[:, c0:c0 + 128], in1=enL[:], op=ALU.mult), reads=[keffT, enL], writes=[KTt])
                Tfin = [None, None]
                if DBG < 3:
                    return
                for hh in range(2):
                    po = 64 * hh
                    fw.op("dve", lambda e, hh=hh, po=po: e.tensor_tensor(out=Bp[hh][:, po:po + 64], in0=btok[:, t, po:po + 64], in1=eD[:, po:po + 64], op=ALU.mult), reads=[btok, eD], writes=[Bp[hh]])
                    fw.op("dve", lambda e, hh=hh, po=po: e.tensor_tensor(out=Kp[hh][:, po:po + 64], in0=kefftok[:, t, po:po + 64], in1=eD[:, po:po + 64], op=ALU.mult), reads=[kefftok, eD], writes=[Kp[hh]])
                psGs, psNs = [], []
                for hh in range(2):
                    po = 64 * hh
                    psG = self.PS()
                    arv = AR[po:po + 64, :, :].rearrange("p a n -> p (a n)")
                    fw.op("pe", lambda e, psG=psG, po=po, arv=arv: e.matmul(psG[:, 0:256], BT[po:po + 64, :], arv, start=True, stop=True), reads=[BT, AR], writes=[psG])
                    fw.op("pe", lambda e, psG=psG, po=po, arv=arv: e.matmul(psG[:, 256:512], KTt[po:po + 64, :], arv, start=True, stop=True), reads=[KTt, AR], writes=[psG])
                    psN = self.PS()
                    fw.op("pe", lambda e, psN=psN, po=po: e.matmul(psN[:, 0:128], AR[po:po + 64, 0, :], BT[po:po + 64, :], start=True, stop=True), reads=[BT, AR], writes=[psN])
                    psGs.append(psG)
                    psNs.append(psN)
                for hh in range(2):
                    psG, psN = psGs[hh], psNs[hh]
                    X, Y = XA[hh][0], YA[hh][0]
                    fw.op("dve", lambda e, psG=psG, X=X: e.tensor_tensor(out=X[:, 0, :], in0=psG[:, 0:128], in1=M[:, m_s, :], op=ALU.mult), reads=[psG, M], writes=[XP[hh][0]])
                    fw.op("dve", lambda e, psN=psN, Y=Y: e.tensor_tensor(out=Y[:], in0=psN[:, 0:128], in1=M[:, m_t, :], op=ALU.mult), reads=[psN, M], writes=[Y])
                    fw.op("pool", lambda e, X=X: e.tensor_copy(out=X[:, 1, :], in_=identb[:]), reads=[identb], writes=[XT_[hh][0]])
                    fw.op("dve", lambda e, psG=psG, hh=hh: e.tensor_tensor(out=Mbr[hh][:], in0=psG[:, 128:256], in1=M[:, m_i, :], op=ALU.mult), reads=[psG, M], writes=[Mbr[hh]])
                    fw.op("dve", lambda e, psG=psG, hh=hh: e.tensor_tensor(out=Mak[hh][:], in0=psG[:, 256:384], in1=M[:, m_s, :], op=ALU.mult), reads=[psG, M], writes=[Mak[hh]])
                    fw.op("dve", lambda e, psG=psG, hh=hh: e.tensor_tensor(out=Mkr[hh][:], in0=psG[:, 384:512], in1=M[:, m_i, :], op=ALU.mult), reads=[psG, M], writes=[Mkr[hh]])
                cur = 0
                for lvl in range(7 if DBG >= 4 else 0):
                    for hh in range(2):
                        X, Y = XA[hh][cur], YA[hh][cur]
                        Xn, Yn = XA[hh][1 - cur], YA[hh][1 - cur]
                        xp, xt, xpn, xtn = XP[hh][cur], XT_[hh][cur], XP[hh][1 - cur], XT_[hh][1 - cur]
                        ps1b = self.PS()
                        fw.op("pe", lambda e, ps1b=ps1b, X=X, Y=Y: e.matmul(ps1b[:, 0:128], Y[:], X[:, 1, :], start=True, stop=True), reads=[xt, Y], writes=[ps1b])
                        if lvl < 6:
                            ps1a = self.PS()
                            fw.op("pe", lambda e, ps1a=ps1a, X=X, Y=Y: e.matmul(ps1a[:, 0:128], Y[:], X[:, 0, :], start=True, stop=True), reads=[xp, Y], writes=[ps1a])
                            ps2 = self.PS()
                            fw.op("pe", lambda e, ps2=ps2, X=X, Y=Y: e.matmul(ps2[:, 0:128], X[:, 0, :], Y[:], start=True, stop=True), reads=[xp, Y], writes=[ps2])
                        fw.op("dve", lambda e, ps1b=ps1b, X=X, Xn=Xn: e.tensor_tensor(out=Xn[:, 1, :], in0=ps1b[:, 0:128], in1=X[:, 1, :], op=ALU.add), reads=[ps1b, xt], writes=[xtn])
                        if lvl < 6:
                            fw.op("act", lambda e, ps1a=ps1a, Xn=Xn: e.activation(out=Xn[:, 0, :], in_=ps1a[:, 0:128], func=AF.Identity), reads=[ps1a], writes=[xpn])
                            fw.op("act", lambda e, ps2=ps2, Yn=Yn: e.activation(out=Yn[:], in_=ps2[:, 0:128], func=AF.Identity), reads=[ps2], writes=[Yn])
                    cur = 1 - cur
                if DBG < 5:
                    return
                psXs = []
                for hh in range(2):
                    po = 64 * hh
                    psX = self.PS()
                    fw.op("pe", lambda e, psX=psX, po=po: e.matmul(psX[:, 0:64], AR[po:po + 64, 0, :], STb[d][po:po + 64, po:po + 64], start=True, stop=False), reads=[AR, STb[d]], writes=[psX])
                    fw.op("pe", lambda e, psX=psX, po=po, hh=hh: e.matmul(psX[:, 0:64], Mak[hh][:], Vpad[:, t, hh, po:po + 64], start=False, stop=True), reads=[Mak[hh], Vpad], writes=[psX])
                    psXs.append(psX)
                for hh in range(2):
                    psX = psXs[hh]
                    fw.op("act" if hh == 0 else "dve",
                          (lambda e, psX=psX, hh=hh: e.activation(out=XTs[hh][:], in_=psX[:, 0:64], func=AF.Identity)) if hh == 0 else
                          (lambda e, psX=psX, hh=hh: e.tensor_copy(out=XTs[hh][:], in_=psX[:, 0:64])), reads=[psX], writes=[XTs[hh]])
                psUs = []
                for hh in range(2):
                    Tf = XA[hh][cur]
                    psU = self.PS()
                    fw.op("pe", lambda e, psU=psU, Tf=Tf, hh=hh: e.matmul(psU[:, 0:64], Tf[:, 1, :], XTs[hh][:], start=True, stop=True), reads=[XT_[hh][cur], XTs[hh]], writes=[psU])
                    psUs.append(psU)
                for hh in range(2):
                    po = 64 * hh
                    psU = psUs[hh]
                    fw.op("act" if hh == 0 else "dve",
                          (lambda e, psU=psU, hh=hh, po=po: e.activation(out=Up[hh][:, po:po + 64], in_=psU[:, 0:64], func=AF.Identity)) if hh == 0 else
                          (lambda e, psU=psU, hh=hh, po=po: e.tensor_copy(out=Up[hh][:, po:po + 64], in_=psU[:, 0:64])), reads=[psU], writes=[Up[hh]])
                if DBG < 6:
                    return
                psY = self.PS()
                fw.op("pe", lambda e: e.matmul(psY[:, 0:128], STb[d][:], AR[:, 1, :], start=True, stop=False), reads=[STb[d], AR], writes=[psY])
                for hh in range(2):
                    fw.op("pe", lambda e, hh=hh: e.matmul(psY[:, 0:128], Up[hh][:], Mbr[hh][:], start=False, stop=False), reads=[Up[hh], Mbr[hh]], writes=[psY])
                    fw.op("pe", lambda e, hh=hh: e.matmul(psY[:, 0:128], Vpad[:, t, hh, :], Mkr[hh][:], start=False, stop=(hh == 1)), reads=[Vpad, Mkr[hh]], writes=[psY])
                a0 = gtok + c0
                if first:
                    fw.op("act", lambda e: e.activation(out=acc[:, a0:a0 + 128], in_=psY[:, 0:128], func=AF.Identity), reads=[psY], writes=[acc])
                else:
                    fw.op("dve", lambda e: e.tensor_tensor(out=acc[:, a0:a0 + 128], in0=psY[:, 0:128], in1=acc[:, a0:a0 + 128], op=ALU.add), reads=[psY, acc], writes=[acc])
                psS = self.PS()
                for hh in range(2):
                    fw.op("pe", lambda e, hh=hh: e.matmul(psS[:, 0:128], Bp[hh][:], Up[hh][:], start=(hh == 0), stop=False), reads=[Bp[hh], Up[hh]], writes=[psS])
                    fw.op("pe", lambda e, hh=hh: e.matmul(psS[:, 0:128], Kp[hh][:], Vpad[:, t, hh, :], start=False, stop=(hh == 1)), reads=[Kp[hh], Vpad], writes=[psS])
                gcol = eL[:, 127:128] if d == 0 else eL[:, 0:1]
                fw.op("dve", lambda e: e.scalar_tensor_tensor(out=ST[d][:], in0=ST[d][:], scalar=gcol, in1=psS[:, 0:128], op0=ALU.mult, op1=ALU.add), reads=[ST[d], eL, psS], writes=[ST[d]])
                fw.op("act", lambda e: e.activation(out=STb[d][:], in_=ST[d][:], func=AF.Identity), reads=[ST[d]], writes=[STb[d]])

            def init_state(p, d):
                fw.op("pool", lambda e: e.memset(ST[d][:], 0.0), writes=[ST[d]])
                if which == 1:
                    fw.op("pool", lambda e: e.memset(stg[:], 0.0), writes=[stg])
                    for hh in range(2):
                        po = 64 * hh
                        fw.dma(lambda e, hh=hh, po=po: e.dma_start(out=stg[po:po + 64, po:po + 64], in_=I["state_rwkv"][d, 2 * p + hh]), writes=[stg])
                    ps = self.PS()
                    fw.op("pe", lambda e: e.transpose(ps[:, 0:128], stg[:], C["ident"][:]), reads=[stg, C["ident"]], writes=[ps])
                    fw.op("dve", lambda e: e.tensor_copy(out=ST[d][:], in_=ps[:, 0:128]), reads=[ps], writes=[ST[d]])
                fw.op("act", lambda e: e.activation(out=STb[d][:], in_=ST[d][:], func=AF.Identity), reads=[ST[d]], writes=[STb[d]])
                if not self._pd_done.get((p, d, self.hT_group)):
                    pass

            def out_state(p, d, s):
                ps = self.PS()
                fw.op("pe", lambda e: e.transpose(ps[:, 0:128], ST[d][:], C["ident"][:]), reads=[ST[d], C["ident"]], writes=[ps])
                fw.op("dve", lambda e: e.tensor_copy(out=stg[:], in_=ps[:, 0:128]), reads=[ps], writes=[stg])
                for hh in range(2):
                    po = 64 * hh
                    fw.dma(lambda e, hh=hh, po=po: e.dma_start(out=O["o_srw"][s, d, 2 * p + hh], in_=stg[po:po + 64, po:po + 64]), reads=[stg], writes=[self.OUTB])

            def finalize(p, grp):
                g0 = grp * GS
                pc = 128 * p
                for tb in range(0, GS, 512):
                    y = acc[:, g0 + tb:g0 + tb + 512]
                    t1, t2, t3 = t512
                    ps = self.PS()
                    fw.op("pe", lambda e, ps=ps, y=y: e.matmul(ps[:, :], C["blk_f"][:, :], y, start=True, stop=True), reads=[C["blk_f"], acc], writes=[ps])
                    fw.op("dve", lambda e, ps=ps, y=y: e.scalar_tensor_tensor(out=t1[:], in0=ps[:], scalar=-1.0 / 64, in1=y, op0=ALU.mult, op1=ALU.add), reads=[ps, acc], writes=[t1])
                    self._srcbuf = t1
                    self.blk_stat(t1[:], 512, 1.0 / 64, 1, t2)
                    fw.op("dve", lambda e: e.tensor_tensor(out=t1[:], in0=t1[:], in1=t2[:], op=ALU.mult), reads=[t1, t2], writes=[t1])
                    fw.op("dve", lambda e: e.tensor_scalar(out=t1[:], in0=t1[:], scalar1=R["LGNc"][:, p:p + 1], scalar2=R["LNBc"][:, p:p + 1], op0=ALU.mult, op1=ALU.add), reads=[t1, R["LGNc"], R["LNBc"]], writes=[t1])
                    fw.op("dve", lambda e, tb=tb: e.scalar_tensor_tensor(out=t2[:], in0=rT[:, tb:tb + 512], scalar=R["RKc"][:, p:p + 1], in1=kT[:, tb:tb + 512], op0=ALU.mult, op1=ALU.mult), reads=[rT, kT, R["RKc"]], writes=[t2])
                    ps2 = self.PS()
                    fw.op("pe", lambda e, ps2=ps2: e.matmul(ps2[:, :], C["blk_f"][:, :], t2[:], start=True, stop=True), reads=[C["blk_f"], t2], writes=[ps2])
                    fw.op("dve", lambda e, ps2=ps2, tb=tb: e.tensor_tensor(out=t3[:], in0=ps2[:], in1=vT[:, tb:tb + 512], op=ALU.mult), reads=[ps2, vT], writes=[t3])
                    fw.op("dve", lambda e: e.tensor_tensor(out=t1[:], in0=t1[:], in1=t3[:], op=ALU.add), reads=[t1, t3], writes=[t1])
                    ps3 = self.PS()
                    fw.op("pe", lambda e, ps3=ps3, tb=tb: e.matmul(ps3[:, :], R["gup"][:, pc:pc + 128], sgd[:, tb:tb + 512], start=True, stop=True), reads=[R["gup"], sgd], writes=[ps3])
                    fw.op("dve", lambda e, ps3=ps3, tb=tb: e.tensor_tensor(out=mixT[:, 4 + p, g0 + tb:g0 + tb + 512], in0=t1[:], in1=ps3[:], op=ALU.mult), reads=[t1, ps3], writes=[mixT])

            sb_sq = atok
            self.lora_group = None
            self._pd_done = {}
            if DBG < 1:
                lvl0 = int(os.environ.get("RWSUB", "0"))
                noop = lambda *a, **k: None
                self.run_sched(xT, hT, which, sched, nseq, nchunk_seq, project, chunk, noop if lvl0 < 3 else init_state, noop if lvl0 < 3 else out_state,
                               noop if lvl0 < 2 else finalize, prep_dir=None if lvl0 < 1 else prep_dir)
            else:
                self.run_sched(xT, hT, which, sched, nseq, nchunk_seq, project, chunk, init_state, out_state, finalize, prep_dir=prep_dir)
            self.flush()

    def hT_for(self, xT, hT, blk, gain_idx, layer, which, shift_i, scale_i):
        if self.hT_group != blk:
            self.norm_mod(xT, 1024, hT, gain_idx, layer, which, shift_i, scale_i, t0=blk * 1024)
            self.hT_group = blk

    def cache_T(self, dst, src_dram, c0, st_tile):
        fw, C = self.fw, self.C
        for t in range(2):
            fw.dma(lambda e, t=t: e.dma_start(out=st_tile[:, t, :], in_=src_dram[t * 128:(t + 1) * 128, c0:c0 + 128]), writes=[st_tile])
        ps = self.PS()
        for t in range(2):
            fw.op("pe", lambda e, t=t, ps=ps: e.transpose(ps[:, t * 128:(t + 1) * 128], st_tile[:, t, :], C["ident"][:]), reads=[st_tile, C["ident"]], writes=[ps])
        fw.op("dve", lambda e, ps=ps: e.tensor_copy(out=dst[:, 2048:2304], in_=ps[:, 0:256]), reads=[ps], writes=[dst])

    def lat_attention(self, xT, hT, mixT):
        fw, I, C = self.fw, self.I, self.C
        which = 1
        Wi = I["w_in_attn"]
        acc_banks = self.ps_pool[0:4]
        self.ps_rot = self.ps_pool[4:8]
        self.ps_i = 0
        st0 = ExitStack()
        hTx = fw.sbuf(st0, "hTx", [128, 8, 1024], BF16)
        hTs = [hT, hTx]
        self.norm_mod(xT, 1024, hT, 0, 0, which, 0, 1, t0=0)
        self.norm_mod(xT, 1024, hTx, 0, 0, which, 0, 1, t0=1024)
        with ExitStack() as st:
            wq, wk, wv = self.ws_pool
            for i, w_ in enumerate((wq, wk, wv)):
                self.load_w(Wi, 0, 8, i * 512, 512, dst=w_)
            KT = fw.sbuf(st, "l_KT", [128, 2304], BF16)
            QT = fw.sbuf(st, "l_QT", [128, 2048], BF16)
            V = fw.sbuf(st, "l_V", [128, 18, 128], BF16)
            T3 = fw.sbuf(st, "l_T3", [128, 22, 64], F32)
            RMK = fw.sbuf(st, "l_RMK", [128, 4, 16, 8], F32)
            fw.dma(lambda e: e.dma_start(out=RMK[:], in_=I["c_rowmask"]), writes=[RMK])
            cst = fw.sbuf(st, "l_cst", [128, 2, 128], F32)
            PT = [fw.sbuf(st, f"l_PT{i}", [128, 512], BF16) for i in range(4)]
            rdn = fw.sbuf(st, "l_rdn", [128, 512], F32)
            for hp in range(4):
                pc = 128 * hp
                for blk in range(2):
                    for tb in (0, 512):
                        g0 = blk * 1024 + tb
                        self.proj_fm(wq, 8, pc, hTs[blk], tb, 512, lambda ps, g0=g0: fw.op("act", lambda e: e.activation(out=QT[:, g0:g0 + 512], in_=ps[:], func=AF.Identity), reads=[ps], writes=[QT]))
                        self.proj_fm(wk, 8, pc, hTs[blk], tb, 512, lambda ps, g0=g0: fw.op("act", lambda e: e.activation(out=KT[:, g0:g0 + 512], in_=ps[:], func=AF.Identity), reads=[ps], writes=[KT]))
                    for t in range(8):
                        self.proj_tm(wv, 8, pc, 128, hTs[blk], t * 128, lambda ps, t=t, blk=blk: fw.op("act", lambda e: e.activation(out=V[:, blk * 8 + t, :], in_=ps[:, 0:128], func=AF.Identity), reads=[ps], writes=[V]))
                self.cache_T(KT, I["cache_na_k"], pc, cst)
                for t in range(2):
                    fw.dma(lambda e, t=t, pc=pc: e.dma_start(out=V[:, 16 + t, :], in_=I["cache_na_v"][t * 128:(t + 1) * 128, pc:pc + 128]), writes=[V], queue="pool")
                for hh in range(2):
                    h = 2 * hp + hh
                    po = 64 * hh
                    fw.dma(lambda e, h=h: e.dma_start(out=T3[:], in_=I["c_rpbT"][h]), writes=[T3])
                    for qb in range(4):
                        self.na_block(h, hp, hh, po, qb, KT, QT, V, T3, RMK, PT, rdn, mixT, acc_banks)
            self.flush()
        with ExitStack() as st:
            wq, wk, wv = self.ws_pool
            for i, w_ in enumerate((wq, wk, wv)):
                self.load_w(Wi, 0, 8, 1536 + i * 512, 512, dst=w_)
            KT = fw.sbuf(st, "d_KT", [128, 2304], BF16)
            QT = fw.sbuf(st, "d_QT", [128, 2048], BF16)
            V = fw.sbuf(st, "d_V", [128, 18, 128], BF16)
            cos = fw.sbuf(st, "d_cos", [128, 1024], F32)
            sin = fw.sbuf(st, "d_sin", [128, 1024], F32)
            perm = fw.sbuf(st, "d_perm", [128, 128], BF16)
            fw.dma(lambda e: e.dma_start(out=perm[:], in_=I["c_perm"]), writes=[perm], queue="pool")
            cst = fw.sbuf(st, "d_cst", [128, 2, 128], F32)
            xb = fw.sbuf(st, "d_xb", [128, 512], BF16)
            PT = [fw.sbuf(st, f"d_PT{i}", [128, 512], BF16) for i in range(4)]
            for j in range(4):
                pc = 128 * j
                for blk in range(2):
                    fw.dma(lambda e, blk=blk: e.dma_start(out=cos[:], in_=I["c_cos"][:, blk * 1024:(blk + 1) * 1024]), writes=[cos])
                    fw.dma(lambda e, blk=blk: e.dma_start(out=sin[:], in_=I["c_sin"][:, blk * 1024:(blk + 1) * 1024]), writes=[sin])
                    for tb in (0, 512):
                        g0 = blk * 1024 + tb
                        for w_, dst in ((wq, QT), (wk, KT)):
                            def ev(ps, g0=g0, tb=tb, dst=dst):
                                xf = self.ntmp[0]
                                t2 = self.ntmp[1]
                                fw.op("act", lambda e: e.activation(out=xf[:], in_=ps[:], func=AF.Identity), reads=[ps], writes=[xf])
                                fw.op("dve", lambda e: e.tensor_copy(out=xb[:], in_=xf[:]), reads=[xf], writes=[xb])
                                ps2 = self.PS()
                                fw.op("pe", lambda e: e.matmul(ps2[:, :], perm[:], xb[:], start=True, stop=True), reads=[perm, xb], writes=[ps2])
                                fw.op("dve", lambda e: e.tensor_tensor(out=t2[:], in0=ps2[:], in1=sin[:, tb:tb + 512], op=ALU.mult), reads=[ps2, sin], writes=[t2])
                                fw.op("dve", lambda e: e.tensor_tensor(out=xf[:], in0=xf[:], in1=cos[:, tb:tb + 512], op=ALU.mult), reads=[xf, cos], writes=[xf])
                                fw.op("dve", lambda e: e.tensor_tensor(out=dst[:, g0:g0 + 512], in0=xf[:], in1=t2[:], op=ALU.add), reads=[xf, t2], writes=[dst])
                            self.proj_fm(w_, 8, pc, hTs[blk], tb, 512, ev)
                    for t in range(8):
                        self.proj_tm(wv, 8, pc, 128, hTs[blk], t * 128, lambda ps, t=t, blk=blk: fw.op("act", lambda e: e.activation(out=V[:, blk * 8 + t, :], in_=ps[:, 0:128], func=AF.Identity), reads=[ps], writes=[V]))
                self.cache_T(KT, I["cache_diff_k"], pc, cst)
                for t in range(2):
                    fw.dma(lambda e, t=t, pc=pc: e.dma_start(out=V[:, 16 + t, :], in_=I["cache_diff_v"][t * 128:(t + 1) * 128, pc:pc + 128]), writes=[V], queue="pool")
                for qb in range(4):
                    self.df_block(j, qb, KT, QT, V, PT, mixT, acc_banks)
            self.flush()
        st0.close()
        self.ps_rot = self.ps_pool
        self.ps_i = 0

    def na_block(self, h, hp, hh, po, qb, KT, QT, V, T3, RMK, PT, rdn, mixT, acc_banks):
        fw, C = self.fw, self.C
        R0 = 8 * qb
        q0 = qb * 512
        lo = min(max(R0 - 4, 0), 24)
        hi = min(max(R0 + 3, 0), 24) + 8
        tiles = list(range(lo // 2, (hi - 1) // 2 + 1)) + [16, 17]
        num, den = acc_banks[0], acc_banks[1]
        n = len(tiles)
        def issue_s(i, kt):
            ps = self.PS()
            fw.op("pe", lambda e, ps=ps: e.matmul(ps[:, :], KT[po:po + 64, kt * 128:(kt + 1) * 128], QT[po:po + 64, q0:q0 + 512], start=True, stop=True), reads=[KT, QT], writes=[ps])
            pt = PT[i % 4]
            if kt < 16:
                sp = self.ntmp[1 + (i % 3)]
                mlo = R0 - 2 * kt + 7 + 3
                fw.op("dve", lambda e, ps=ps, sp=sp, mlo=mlo: e.scalar_tensor_tensor(out=sp[:], in0=ps[:], scalar=0.125, in1=T3[:, mlo:mlo + 8, :].rearrange("p m q -> p (m q)"),
                                                                                 op0=ALU.mult, op1=ALU.add), reads=[ps, T3], writes=[sp])
                fw.op("dve", lambda e, sp=sp: e.tensor_tensor(out=sp[:].rearrange("p (m q) -> p m q", q=64), in0=sp[:].rearrange("p (m q) -> p m q", q=64),
                                                             in1=RMK[:, qb, kt, :].unsqueeze(2).broadcast_to([128, 8, 64]), op=ALU.add), reads=[sp, RMK], writes=[sp])
                fw.op("act", lambda e, sp=sp, pt=pt: e.activation(out=pt[:], in_=sp[:], func=AF.Exp), reads=[sp], writes=[pt])
            else:
                fw.op("act", lambda e, ps=ps, pt=pt: e.activation(out=pt[:], in_=ps[:], func=AF.Exp, scale=0.125), reads=[ps], writes=[pt])
        issue_s(0, tiles[0])
        issue_s(1, tiles[1])
        for i, kt in enumerate(tiles):
            if i + 2 < n:
                issue_s(i + 2, tiles[i + 2])
            pt = PT[i % 4]
            fw.op("pe", lambda e, kt=kt, pt=pt, i=i: e.matmul(num[:, :], V[:, kt, :], pt[:], start=(i == 0), stop=(i == n - 1)), reads=[V, pt], writes=[num])
            fw.op("pe", lambda e, pt=pt, i=i: e.matmul(den[:, :], C["ones_b"][:, :], pt[:], start=(i == 0), stop=(i == n - 1)), reads=[C["ones_b"], pt], writes=[den])
        fw.op("dve", lambda e: e.reciprocal(out=rdn[po:po + 64, :], in_=den[po:po + 64, :]), reads=[den], writes=[rdn])
        fw.op("dve", lambda e: e.tensor_tensor(out=mixT[po:po + 64, hp, q0:q0 + 512], in0=num[po:po + 64, :], in1=rdn[po:po + 64, :], op=ALU.mult), reads=[num, rdn], writes=[mixT])

    def df_block(self, j, qb, KT, QT, V, PT, mixT, acc_banks):
        fw, C = self.fw, self.C
        q0 = qb * 512
        for c in range(2):
            po = 64 * c
            num, den = acc_banks[2 * c], acc_banks[2 * c + 1]
            def issue_s(kt, po=po):
                ps = self.PS()
                fw.op("pe", lambda e, ps=ps: e.matmul(ps[:, :], KT[po:po + 64, kt * 128:(kt + 1) * 128], QT[po:po + 64, q0:q0 + 512], start=True, stop=True), reads=[KT, QT], writes=[ps])
                pt = PT[kt % 4]
                fw.op("act", lambda e, ps=ps, pt=pt: e.activation(out=pt[:], in_=ps[:], func=AF.Exp, scale=0.125), reads=[ps], writes=[pt])
            issue_s(0)
            issue_s(1)
            for kt in range(18):
                if kt + 2 < 18:
                    issue_s(kt + 2)
                pt = PT[kt % 4]
                fw.op("pe", lambda e, kt=kt, pt=pt, num=num: e.matmul(num[:, :], V[:, kt, :], pt[:], start=(kt == 0), stop=(kt == 17)), reads=[V, pt], writes=[num])
                fw.op("pe", lambda e, kt=kt, pt=pt, den=den: e.matmul(den[:, :], C["ones_b"][:, :], pt[:], start=(kt == 0), stop=(kt == 17)), reads=[C["ones_b"], pt], writes=[den])
        nA, dA, nB, dB = acc_banks
        r0, r1, t0, t1 = self.ntmp
        n = 512
        fw.op("dve", lambda e: e.reciprocal(out=r0[:], in_=dA[:]), reads=[dA], writes=[r0])
        fw.op("dve", lambda e: e.reciprocal(out=r1[:], in_=dB[:]), reads=[dB], writes=[r1])
        fw.op("dve", lambda e: e.tensor_tensor(out=t0[:], in0=nA[:], in1=r0[:], op=ALU.mult), reads=[nA, r0], writes=[t0])
        fw.op("dve", lambda e: e.tensor_tensor(out=t1[:], in0=nB[:], in1=r1[:], op=ALU.mult), reads=[nB, r1], writes=[t1])
        fw.op("dve", lambda e: e.scalar_tensor_tensor(out=t0[:], in0=t1[:], scalar=self.lamc[:, 0:1], in1=t0[:], op0=ALU.mult, op1=ALU.add), reads=[t0, t1, self.lamc], writes=[t0])
        fw.op("act", lambda e: e.activation(out=t1[:], in_=t0[:], func=AF.Square), reads=[t0], writes=[t1])
        ps = self.PS()
        fw.op("pe", lambda e: e.matmul(ps[:, :], C["ones_f"][:, :], t1[:], start=True, stop=True), reads=[C["ones_f"], t1], writes=[ps])
        fw.op("act", lambda e: e.activation(out=r0[:], in_=ps[:], func=AF.Sqrt, bias=C["eps"][:, 0:1], scale=1.0 / 128), reads=[ps, C["eps"]], writes=[r0])
        fw.op("dve", lambda e: e.reciprocal(out=r0[:], in_=r0[:]), reads=[r0], writes=[r0])
        fw.op("dve", lambda e: e.scalar_tensor_tensor(out=mixT[:, 4 + j, q0:q0 + n], in0=t0[:], scalar=1.0 - self.lam_init, in1=r0[:], op0=ALU.mult, op1=ALU.mult), reads=[t0, r0], writes=[mixT])


_PROG = {}


def get_prog(stop_after=None):
    if stop_after not in _PROG:
        _PROG[stop_after] = Prog(stop_after)
    return _PROG[stop_after]


def host_consts():
    r = np.arange(128)
    row, col = r[:, None], r[None, :]
    masks = np.stack([row < col, row <= col, row > col, row >= col]).astype(np.float32)
    diff = (col - row).astype(np.float32) * np.ones((128, 128), np.float32)
    ip = np.stack([np.broadcast_to(col + 1.0, (128, 128)), np.broadcast_to(128.0 - col, (128, 128))]).astype(np.float32)
    jc = np.stack([127.0 - r, r * 1.0], 1).astype(np.float32)
    half = np.stack([r < 64, r >= 64], 1).astype(np.float32)
    NEG = -1e30
    rs = np.clip(np.arange(32) - 4, 0, 24)
    rowmask = np.zeros((4, 16, 128, 8), np.float32)
    for qb in range(4):
        for kt in range(16):
            for e in range(2):
                kr = 2 * kt + e
                for i in range(8):
                    qr = 8 * qb + i
                    if not (rs[qr] <= kr < rs[qr] + 8):
                        rowmask[qb, kt, 64 * e:64 * e + 64, i] = NEG
    n = np.arange(2048)
    d = np.arange(128) % 64
    pos = np.where((d < 32)[:, None], (n // 64)[None, :], (n % 64)[None, :]).astype(np.float32)
    inv = (10000.0 ** (-(d % 16).astype(np.float32) / 16.0)).astype(np.float32)
    ang = pos * inv[:, None]
    perm = np.zeros((128, 128), np.float32)
    for o in range(128):
        if (o % 32) < 16:
            perm[o + 16, o] = -1.0
        else:
            perm[o - 16, o] = 1.0
    return {"c_masks": masks, "c_diff": diff, "c_ip": ip, "c_jc": jc, "c_half": half, "c_rowmask": np.ascontiguousarray(rowmask.transpose(2, 0, 1, 3)),
            "c_cos": np.cos(ang).astype(np.float32), "c_sin": np.sin(ang).astype(np.float32), "c_perm": perm}


def rpb_table(rpb):
    NEG = np.float32(-1e30)
    e = (np.arange(128) // 64)[:, None, None]
    kc = (np.arange(128) % 64)[:, None, None]
    mm = np.arange(22)[None, :, None]
    qc = np.arange(64)[None, None, :]
    dr = e + 7 - (mm - 3)
    cstart = np.clip(qc - 8, 0, 48)
    ok = (dr >= -7) & (dr <= 7) & (kc >= cstart) & (kc < cstart + 16)
    dri = np.clip(dr + 7, 0, 14) + 0 * kc + 0 * qc
    dci = np.clip(kc - qc + 15, 0, 30) + 0 * mm
    out = np.empty((8, 128, 22, 64), np.float32)
    for h in range(8):
        g = rpb[h][dri, dci]
        out[h] = np.where(ok, g, NEG)
    return out


def make_in_maps(inp):
    f = lambda a: np.ascontiguousarray(np.asarray(a, dtype=np.float32))
    hc = host_consts()
    rpbT = rpb_table(np.asarray(inp["na_rpb"][0], dtype=np.float32))
    maps = []
    for i in range(8):
        b = i // 4
        m = {
            "x_prompt": f(inp["x_prompt"][4 * i:4 * i + 4].reshape(1024, D)),
            "x_sample": f(inp["x_sample"][b]),
            "cond": f(np.stack([inp["c_ctx"], inp["c"][b]])),
            "cache_na_k": f(inp["cache_na_k"][b, 0].reshape(256, 512)),
            "cache_na_v": f(inp["cache_na_v"][b, 0].reshape(256, 512)),
            "cache_diff_k": f(inp["cache_diff_k"][b, 0].reshape(256, 512)),
            "cache_diff_v": f(inp["cache_diff_v"][b, 0].reshape(256, 512)),
            "state_ret": f(inp["state_ret"][b, 0]),
            "state_rwkv": f(inp["state_rwkv"][b, 0]),
            "norm_mix_g": f(inp["norm_mix_g"]), "norm_ffn_g": f(inp["norm_ffn_g"]), "norm_final_g": f(inp["norm_final_g"]),
            "w_ada": f(inp["w_ada"]), "b_ada": f(inp["b_ada"]),
            "w_in_attn": f(inp["w_in_attn"][0]), "w_out_attn": f(inp["w_out_attn"][0]), "na_rpb": f(inp["na_rpb"][0]),
            "diff_l": f(np.stack([inp["diff_lq1"][0], inp["diff_lk1"][0], inp["diff_lq2"][0], inp["diff_lk2"][0]])),
            "w_in_rec": f(inp["w_in_rec"][0]), "w_out_rec": f(inp["w_out_rec"][0]),
            "ret_decay_logit": f(inp["ret_decay_logit"][0].reshape(16)),
            "rw_w0": f(inp["rw_w0"][0]), "rw_w_up": f(inp["rw_w_up"][0]), "rw_a0": f(inp["rw_a0"][0]), "rw_a_up": f(inp["rw_a_up"][0]),
            "rw_g_up": f(inp["rw_g_up"][0]), "rw_k_k": f(inp["rw_k_k"][0]), "rw_k_a": f(inp["rw_k_a"][0]),
            "rw_r_k": f(inp["rw_r_k"][0].reshape(512)), "rw_ln_g": f(inp["rw_ln_g"][0]), "rw_ln_b": f(inp["rw_ln_b"][0]),
            "w_ffn_in": f(inp["w_ffn_in"]), "w_ffn_out": f(inp["w_ffn_out"]),
        }
        m.update({k: f(v) for k, v in hc.items()})
        m["c_rpbT"] = rpbT
        maps.append(m)
    return maps


def kernel(**inputs):
    prog = get_prog()
    maps = make_in_maps(inputs)
    res = run_bass_kernel_spmd(prog.nc, maps, core_ids=list(range(8)))
    R = res.results
    y_prompt = np.concatenate([R[i]["y_prompt"].reshape(4, 256, D) for i in range(8)], 0)
    y_sample = np.stack([R[0]["y_sample"], R[4]["y_sample"]], 0)
    cat = lambda k, shp: np.concatenate([R[i][k].reshape((4,) + shp) for i in range(8)], 0)
    na_k = cat("o_na_k", (1, 256, 8, 64))
    na_v = cat("o_na_v", (1, 256, 8, 64))
    df_k = cat("o_df_k", (1, 256, 4, 2, 64))
    df_v = cat("o_df_v", (1, 256, 4, 128))
    sret = cat("o_sret", (1, 2, 8, 64, 64))
    srw = cat("o_srw", (1, 2, 8, 64, 64))
    return tuple(np.ascontiguousarray(a, dtype=np.float32) for a in (y_prompt, y_sample, na_k, na_v, df_k, df_v, sret, srw))
```

```python
import math
import os
import numpy as np
from contextlib import ExitStack
import concourse.bass as bass
import concourse.mybir as mybir
from concourse.bass_utils import run_bass_kernel_spmd

F32 = mybir.dt.float32
BF16 = mybir.dt.bfloat16
AF = mybir.ActivationFunctionType
ALU = mybir.AluOpType
AX = mybir.AxisListType

D = 1024
NCH = 8
EPS = 1e-6
GN_EPS = 64e-5
DFF = 2816
NFF = 22
GS = 512
NTG = GS // 128


class Buf:
    def __init__(self, name, t=None):
        self.name = name
        self.t = t
        self.w = {}
        self.r = {}

    def __getitem__(self, k):
        return self.t[k]


class Eng:
    def __init__(self, name, sem):
        self.name = name
        self.sem = sem
        self.n = 0
        self.seen = {}
        self.ops = []


class FW:
    N_DMA_SEMS = 24

    def __init__(self, nc):
        self.nc = nc
        self.stack = ExitStack()
        self.engs = {}
        for name in ("pe", "act", "dve", "pool", "sp"):
            sem = self.stack.enter_context(nc.semaphore("sem_" + name))
            self.engs[name] = Eng(name, sem)
        self.dsems = [self.stack.enter_context(nc.semaphore(f"sem_dma{i}")) for i in range(self.N_DMA_SEMS)]
        self.dval = [0] * self.N_DMA_SEMS
        self.dkey = [f"D{i}" for i in range(self.N_DMA_SEMS)]
        self.dnext = 0
        self.dnext_sw = 0
        self.nalloc = 0

    def sbuf(self, st, name, shape, dtype):
        self.nalloc += 1
        t = st.enter_context(self.nc.sbuf_tensor(f"{name}_{self.nalloc}", list(shape), dtype))
        return Buf(name, t)

    def psum(self, st, name, shape, dtype=F32):
        self.nalloc += 1
        t = st.enter_context(self.nc.psum_tensor(f"{name}_{self.nalloc}", list(shape), dtype))
        return Buf(name, t)

    def _deps(self, eng, reads, writes, own_key):
        need = {}

        def add(d):
            for k, (sem, val) in d.items():
                if k == own_key and k.startswith("Epe"):
                    continue
                if k not in need or need[k][1] < val:
                    need[k] = (sem, val)
        for b in reads:
            add(b.w)
        for b in writes:
            add(b.w)
            add(b.r)
        waits = []
        for k, (sem, val) in need.items():
            if eng.seen.get(k, 0) >= val:
                continue
            eng.seen[k] = val
            waits.append((sem, val))
        return waits

    SEM_LIMIT = 3800

    def op(self, en, fn, reads=(), writes=()):
        eng = self.engs[en]
        if eng.n >= self.SEM_LIMIT:
            eng.gen = getattr(eng, "gen", 0) + 1
            eng.sem = self.stack.enter_context(self.nc.semaphore(f"sem_{en}_{eng.gen}"))
            eng.n = 0
        key = "E" + en + str(getattr(eng, "gen", 0))
        waits = self._deps(eng, reads, writes, key)
        eng.n += 1
        tok = (eng.sem, eng.n)
        eng.ops.append((waits, fn, eng.sem, 1))
        for b in reads:
            b.r[key] = tok
        for b in writes:
            b.w = {key: tok}
            b.r = {}

    def dma(self, fn, reads=(), writes=(), queue="sp"):
        eng = self.engs[queue]
        half = self.N_DMA_SEMS // 2
        if queue == "pool":
            i = half + self.dnext_sw
            self.dnext_sw = (self.dnext_sw + 1) % half
        else:
            i = self.dnext
            self.dnext = (self.dnext + 1) % half
        if self.dval[i] >= self.SEM_LIMIT:
            self.dgen = getattr(self, "dgen", 0) + 1
            self.dsems[i] = self.stack.enter_context(self.nc.semaphore(f"sem_dma{i}_{self.dgen}"))
            self.dval[i] = 0
            self.dkey[i] = f"D{i}_{self.dgen}"
        sem = self.dsems[i]
        key = self.dkey[i]
        waits = self._deps(eng, reads, writes, None)
        prev = self.dval[i]
        if prev > 0 and eng.seen.get(key, 0) < prev:
            eng.seen[key] = prev
            waits.append((sem, prev))
        self.dval[i] = prev + 16
        tok = (sem, self.dval[i])
        eng.ops.append((waits, fn, sem, 16))
        for b in reads:
            b.r[key] = tok
        for b in writes:
            b.w = {key: tok}
            b.r = {}

    def barrier(self):
        toks = {}
        for name, eng in self.engs.items():
            if eng.n > 0:
                toks["E" + name + str(getattr(eng, "gen", 0))] = (eng.sem, eng.n)
        for i, v in enumerate(self.dval):
            if v > 0:
                toks[self.dkey[i]] = (self.dsems[i], v)
        for name, eng in self.engs.items():
            waits = []
            for k, (sem, val) in toks.items():
                if eng.seen.get(k, 0) >= val:
                    continue
                eng.seen[k] = val
                waits.append((sem, val))
            if waits:
                eng.ops.append((waits, None, None, 0))

    def emit(self):
        nc = self.nc
        with nc.Block() as block:
            def run(eng):
                def body(e):
                    for waits, fn, sem, inc in eng.ops:
                        for (s, v) in waits:
                            e.wait_ge(s, v)
                        if fn is not None:
                            fn(e).then_inc(sem, inc)
                    eng.ops = []
                return body
            block.tensor(run(self.engs["pe"]))
            block.scalar(run(self.engs["act"]))
            block.vector(run(self.engs["dve"]))
            block.gpsimd(run(self.engs["pool"]))
            block.sync(run(self.engs["sp"]))


IN_SPECS = [
    ("x_prompt", [1024, D]), ("x_sample", [2048, D]), ("cond", [2, D]),
    ("cache_na_k", [256, 512]), ("cache_na_v", [256, 512]), ("cache_diff_k", [256, 512]), ("cache_diff_v", [256, 512]),
    ("state_ret", [2, 8, 64, 64]), ("state_rwkv", [2, 8, 64, 64]),
    ("norm_mix_g", [2, D]), ("norm_ffn_g", [2, D]), ("norm_final_g", [D]),
    ("w_ada", [2, D, 6 * D]), ("b_ada", [2, 6 * D]),
    ("w_in_attn", [D, 3072]), ("w_out_attn", [D, D]), ("na_rpb", [8, 15, 31]),
    ("diff_l", [4, 64]),
    ("w_in_rec", [D, 3840]), ("w_out_rec", [D, D]), ("ret_decay_logit", [16]),
    ("rw_w0", [2, 512]), ("rw_w_up", [2, 64, 512]), ("rw_a0", [2, 512]), ("rw_a_up", [2, 64, 512]), ("rw_g_up", [128, 512]),
    ("rw_k_k", [512]), ("rw_k_a", [512]), ("rw_r_k", [512]), ("rw_ln_g", [512]), ("rw_ln_b", [512]),
    ("w_ffn_in", [2, D, 2 * DFF]), ("w_ffn_out", [2, DFF, D]),
    ("c_rpbT", [8, 128, 22, 64]), ("c_rowmask", [128, 4, 16, 8]), ("c_cos", [128, 2048]), ("c_sin", [128, 2048]), ("c_perm", [128, 128]),
    ("c_masks", [4, 128, 128]), ("c_diff", [128, 128]), ("c_ip", [2, 128, 128]), ("c_jc", [128, 2]), ("c_half", [128, 2]),
]
OUT_SPECS = [
    ("y_prompt", [1024, D]), ("y_sample", [2048, D]),
    ("o_na_k", [1024, 512]), ("o_na_v", [1024, 512]), ("o_df_k", [1024, 512]), ("o_df_v", [1024, 512]),
    ("o_sret", [4, 2, 8, 64, 64]), ("o_srw", [4, 2, 8, 64, 64]),
]


class Prog:
    def __init__(self, stop_after=None):
        self.stop_after = stop_after
        nc = bass.Bass("TRN2", target_bir_lowering=False)
        self.nc = nc
        self.I = {n: nc.dram_tensor(n, s, F32, kind="ExternalInput").ap() for n, s in IN_SPECS}
        self.O = {n: nc.dram_tensor(n, s, F32, kind="ExternalOutput").ap() for n, s in OUT_SPECS}
        self.fw = FW(nc)
        self.OUTB = Buf("outputs")
        self.gst = ExitStack()
        self.build()
        self.gst.close()
        self.fw.stack.close()

    def flush(self):
        self.fw.barrier()
        self.marks = getattr(self, "marks", [])
        self.marks.append(sum(1 for o in self.fw.engs["pe"].ops if o[1] is not None))
        self.fw.emit()

    def PS(self):
        rot = self.ps_rot
        b = rot[self.ps_i % len(rot)]
        self.ps_i = (self.ps_i + 1) % len(rot)
        return b

    def WS(self):
        b = self.ws_pool[self.ws_i]
        self.ws_i = (self.ws_i + 1) % len(self.ws_pool)
        return b

    def load_w(self, W, r0, nk, c0, ncols, dst=None, col_off=0):
        fw = self.fw
        ws = dst if dst is not None else self.WS()
        for k0 in range(0, nk, 4):
            k1 = min(nk, k0 + 4)
            src = W[r0 + 128 * k0:r0 + 128 * k1, c0:c0 + ncols].rearrange("(k p) n -> p k n", p=128)
            fw.dma(lambda e, src=src, k0=k0, k1=k1: e.dma_start(out=ws[:, k0:k1, col_off:col_off + ncols], in_=src), writes=[ws], queue="pool")
        return ws

    def col_load(self, dst, vec, n):
        self.fw.dma(lambda e: e.dma_start(out=dst, in_=vec.rearrange("(c p) -> p c", p=128), allow_slow_non_contiguous=True),
                    writes=[dst.buf] if hasattr(dst, "buf") else [])

    def build(self):
        nc, fw, I, O = self.nc, self.fw, self.I, self.O
        g = self.gst
        self.ps_pool = [fw.psum(g, f"ps{i}", [128, 512]) for i in range(8)]
        self.ps_i = 0
        self.ps_rot = self.ps_pool
        self.ws_pool = [fw.sbuf(g, f"ws{i}", [128, 8, 512], BF16) for i in range(3)]
        self.ws_i = 0
        C = self.C = {}
        ident = C["ident"] = fw.sbuf(g, "ident", [128, 128], F32)
        fw.op("pool", lambda e: e.memset(ident[:], 1.0), writes=[ident])
        fw.op("pool", lambda e: e.affine_select(out=ident[:], in_=ident[:], pattern=[[-1, 128]], compare_op=ALU.is_equal,
                                                fill=0.0, base=0, channel_multiplier=1), reads=[ident], writes=[ident])
        identb_ = C["identb"] = fw.sbuf(g, "identb", [128, 128], BF16)
        fw.op("dve", lambda e: e.tensor_copy(out=identb_[:], in_=ident[:]), reads=[ident], writes=[identb_])
        ones_f = C["ones_f"] = fw.sbuf(g, "ones_f", [128, 128], F32)
        fw.op("pool", lambda e: e.memset(ones_f[:], 1.0), writes=[ones_f])
        ones_b = C["ones_b"] = fw.sbuf(g, "ones_b", [128, 128], BF16)
        fw.op("pool", lambda e: e.memset(ones_b[:], 1.0), writes=[ones_b])
        blk_f = C["blk_f"] = fw.sbuf(g, "blk_f", [128, 128], F32)
        fw.op("pool", lambda e: e.memset(blk_f[:], 0.0), writes=[blk_f])
        fw.op("pool", lambda e: e.memset(blk_f[0:64, 0:64], 1.0), writes=[blk_f])
        fw.op("pool", lambda e: e.memset(blk_f[64:128, 64:128], 1.0), writes=[blk_f])
        blk_b = C["blk_b"] = fw.sbuf(g, "blk_b", [128, 128], BF16)
        fw.op("pool", lambda e: e.tensor_copy(out=blk_b[:], in_=blk_f[:]), reads=[blk_f], writes=[blk_b])
        epsc = C["eps"] = fw.sbuf(g, "epsc", [128, 4], F32)
        fw.op("pool", lambda e: e.memset(epsc[:, 0:1], EPS), writes=[epsc])
        fw.op("pool", lambda e: e.memset(epsc[:, 1:2], GN_EPS), writes=[epsc])
        fw.op("pool", lambda e: e.memset(epsc[:, 2:3], 1e-12), writes=[epsc])
        fw.op("pool", lambda e: e.memset(epsc[:, 3:4], 0.0), writes=[epsc])

        self.modT = fw.sbuf(g, "modT", [128, 2, 48, 2], F32)
        self.gains = fw.sbuf(g, "gains", [128, 5, 8], F32)
        for i, v in enumerate([I["norm_mix_g"][0], I["norm_mix_g"][1], I["norm_ffn_g"][0], I["norm_ffn_g"][1], I["norm_final_g"]]):
            gi = i
            fw.dma(lambda e, v=v, gi=gi: e.dma_start(out=self.gains[:, gi, :], in_=v.rearrange("(c p) -> p c", p=128),
                                                     allow_slow_non_contiguous=True), writes=[self.gains])
        self.compute_modulation()
        if self.stop_after == "mod":
            return
        self.compute_lambda()
        self.flush()
        if self.stop_after == "lam":
            return
        self.rec_consts()
        only = os.environ.get("ONLY_PHASE")
        if only != "1":
            self.phase(which=0)
        if only != "0":
            self.phase(which=1)

    def compute_modulation(self):
        fw, I, C = self.fw, self.I, self.C
        with ExitStack() as st:
            cc = fw.sbuf(st, "condc", [128, 8, 2], F32)
            for w in range(2):
                fw.dma(lambda e, w=w: e.dma_start(out=cc[:, :, w], in_=I["cond"][w].rearrange("(c p) -> p c", p=128),
                                                  allow_slow_non_contiguous=True), writes=[cc])
            sc = fw.sbuf(st, "condsb", [128, 8, 2], BF16)
            fw.op("act", lambda e: e.activation(out=sc[:], in_=cc[:], func=AF.Silu), reads=[cc], writes=[sc])
            row = fw.sbuf(st, "modrow", [2, 6 * D], F32)
            brow = fw.sbuf(st, "brow", [2, 6 * D], F32)
            for layer in range(2):
                for w in range(2):
                    fw.dma(lambda e, w=w, layer=layer: e.dma_start(out=brow[w:w + 1, :], in_=I["b_ada"][layer:layer + 1, :]), writes=[brow])
                for blk in range(12):
                    ws = self.load_w(I["w_ada"][layer], 0, 8, blk * 512, 512)
                    ps = self.PS()
                    for k in range(8):
                        fw.op("pe", lambda e, k=k, ws=ws, ps=ps: e.matmul(ps[0:2, :], sc[:, k, :], ws[:, k, 0:512], start=(k == 0), stop=(k == 7)),
                              reads=[sc, ws], writes=[ps])
                    fw.op("dve", lambda e, ps=ps, blk=blk: e.tensor_tensor(out=row[:, blk * 512:(blk + 1) * 512], in0=ps[0:2, :],
                                                                          in1=brow[:, blk * 512:(blk + 1) * 512], op=ALU.add),
                          reads=[ps, brow], writes=[row])
                ps = self.PS()
                for j in range(48):
                    fw.op("pe", lambda e, j=j, ps=ps: e.matmul(ps[:, 2 * j:2 * j + 2], row[0:2, j * 128:(j + 1) * 128], C["ident"][0:2, 0:2],
                                                               start=True, stop=True), reads=[row, C["ident"]], writes=[ps])
                fw.op("dve", lambda e, ps=ps, layer=layer: e.tensor_copy(out=self.modT[:, layer, :, :], in_=ps[:, 0:96].rearrange("p (j w) -> p j w", w=2)),
                      reads=[ps], writes=[self.modT])
            self.flush()

    def compute_lambda(self):
        fw, I = self.fw, self.I
        g = self.gst
        lam_init = 0.8 - 0.6 * math.exp(-0.3 * 0)
        self.lam_init = lam_init
        dl = fw.sbuf(g, "dl", [128, 4, 64], F32)
        fw.dma(lambda e: e.dma_start(out=dl[:].rearrange("p a b -> p (a b)"), in_=I["diff_l"].rearrange("a b -> (a b)").partition_broadcast(128)), writes=[dl])
        pr = fw.sbuf(g, "dlp", [128, 2, 64], F32)
        fw.op("dve", lambda e: e.tensor_tensor(out=pr[:, 0, :], in0=dl[:, 0, :], in1=dl[:, 1, :], op=ALU.mult), reads=[dl], writes=[pr])
        fw.op("dve", lambda e: e.tensor_tensor(out=pr[:, 1, :], in0=dl[:, 2, :], in1=dl[:, 3, :], op=ALU.mult), reads=[dl], writes=[pr])
        sm = fw.sbuf(g, "dls", [128, 2], F32)
        fw.op("dve", lambda e: e.tensor_reduce(out=sm[:], in_=pr[:], axis=AX.X, op=ALU.add), reads=[pr], writes=[sm])
        ex = fw.sbuf(g, "dle", [128, 2], F32)
        fw.op("act", lambda e: e.activation(out=ex[:], in_=sm[:], func=AF.Exp), reads=[sm], writes=[ex])
        lamc = self.lamc = fw.sbuf(g, "lamc", [128, 2], F32)
        fw.op("dve", lambda e: e.tensor_tensor(out=lamc[:, 0:1], in0=ex[:, 1:2], in1=ex[:, 0:1], op=ALU.subtract), reads=[ex], writes=[lamc])
        fw.op("dve", lambda e: e.tensor_scalar(out=lamc[:, 0:1], in0=lamc[:, 0:1], scalar1=-lam_init, scalar2=None, op0=ALU.add), reads=[lamc], writes=[lamc])

    def norm_mod(self, xT, NT, hT, gain_idx, layer, which, shift_i, scale_i, t0=0):
        fw, C = self.fw, self.C
        A = self.tmpA
        if shift_i is not None:
            fw.op("dve", lambda e: e.scalar_tensor_tensor(out=A[:, 0, :], in0=self.modT[:, layer, 8 * scale_i:8 * scale_i + 8, which], scalar=1.0,
                                                          in1=self.gains[:, gain_idx, :], op0=ALU.add, op1=ALU.mult),
                  reads=[self.modT, self.gains], writes=[A])
            fw.op("dve", lambda e: e.tensor_copy(out=A[:, 1, :], in_=self.modT[:, layer, 8 * shift_i:8 * shift_i + 8, which]), reads=[self.modT], writes=[A])
        else:
            fw.op("dve", lambda e: e.tensor_copy(out=A[:, 0, :], in_=self.gains[:, gain_idx, :]), reads=[self.gains], writes=[A])
            fw.op("dve", lambda e: e.memset(A[:, 1, :], 0.0), writes=[A])
        for b0 in range(0, NT, 512):
            ps = self.PS()
            for c in range(8):
                sq = self.ntmp[c % 4]
                fw.op("act", lambda e, c=c, b0=b0, sq=sq: e.activation(out=sq[:].bitcast(BF16)[:, 0:512], in_=xT[:, c, t0 + b0:t0 + b0 + 512], func=AF.Square), reads=[xT], writes=[sq])
                fw.op("pe", lambda e, c=c, ps=ps, sq=sq: e.matmul(ps[:, :], C["ones_b"][:, :], sq[:].bitcast(BF16)[:, 0:512], start=(c == 0), stop=(c == 7)),
                      reads=[sq, C["ones_b"]], writes=[ps])
            rs = self.rstd
            fw.op("act", lambda e, ps=ps: e.activation(out=rs[:], in_=ps[:], func=AF.Sqrt, bias=C["eps"][:, 0:1], scale=1.0 / D), reads=[ps, C["eps"]], writes=[rs])
            fw.op("dve", lambda e: e.reciprocal(out=rs[:], in_=rs[:]), reads=[rs], writes=[rs])
            for c in range(8):
                t = self.ntmp[c % 2]
                fw.op("dve", lambda e, c=c, b0=b0, t=t: e.scalar_tensor_tensor(out=t[:], in0=xT[:, c, t0 + b0:t0 + b0 + 512], scalar=A[:, 0, c:c + 1], in1=rs[:],
                                                                            op0=ALU.mult, op1=ALU.mult), reads=[xT, A, rs], writes=[t])
                fw.op("act", lambda e, c=c, b0=b0, t=t: e.activation(out=hT[:, c, b0:b0 + 512], in_=t[:], func=AF.Identity, bias=A[:, 1, c:c + 1], scale=1.0),
                      reads=[t, A], writes=[hT])

    def proj_fm(self, ws, nk, col0, hT, t0, nt, evac, hk0=0):
        fw = self.fw
        ps = self.PS()
        for k in range(nk):
            fw.op("pe", lambda e, k=k, ps=ps: e.matmul(ps[:, 0:nt], ws[:, k, col0:col0 + 128], hT[:, hk0 + k, t0:t0 + nt], start=(k == 0), stop=(k == nk - 1)),
                  reads=[ws, hT], writes=[ps])
        evac(ps)

    def proj_tm(self, ws, nk, col0, ncols, hT, t0, evac, hk0=0):
        fw = self.fw
        ps = self.PS()
        for k in range(nk):
            fw.op("pe", lambda e, k=k, ps=ps: e.matmul(ps[:, 0:ncols], hT[:, hk0 + k, t0:t0 + 128], ws[:, k, col0:col0 + ncols], start=(k == 0), stop=(k == nk - 1)),
                  reads=[ws, hT], writes=[ps])
        evac(ps)

    def load_xT(self, xT, src, NT):
        fw, C = self.fw, self.C
        for t in range(NT // 128):
            st_ = self.stage[t % 2]
            fw.dma(lambda e, t=t, st_=st_: e.dma_start(out=st_[:], in_=src[t * 128:(t + 1) * 128, :]), writes=[st_])
            for half in range(2):
                ps = self.PS()
                for cc in range(4):
                    c = half * 4 + cc
                    fw.op("pe", lambda e, c=c, cc=cc, ps=ps, st_=st_: e.transpose(ps[:, cc * 128:(cc + 1) * 128], st_[:, c * 128:(c + 1) * 128], C["ident"][:]),
                          reads=[st_, C["ident"]], writes=[ps])
                fw.op("act" if half else "dve",
                      (lambda e, half=half, t=t, ps=ps: e.activation(out=xT[:, half * 4:half * 4 + 4, t * 128:(t + 1) * 128],
                                                                     in_=ps[:].rearrange("p (c n) -> p c n", c=4), func=AF.Identity))
                      if half else
                      (lambda e, half=half, t=t, ps=ps: e.tensor_copy(out=xT[:, half * 4:half * 4 + 4, t * 128:(t + 1) * 128],
                                                                      in_=ps[:].rearrange("p (c n) -> p c n", c=4))),
                      reads=[ps], writes=[xT])

    def store_T(self, srcT, dst, NT):
        fw, C = self.fw, self.C
        for t in range(NT // 128):
            st_ = self.stage[t % 2]
            for half in range(2):
                ps = self.PS()
                for cc in range(4):
                    c = half * 4 + cc
                    fw.op("pe", lambda e, c=c, cc=cc, ps=ps, t=t: e.transpose(ps[:, cc * 128:(cc + 1) * 128], srcT[:, c, t * 128:(t + 1) * 128], C["ident"][:]),
                          reads=[srcT, C["ident"]], writes=[ps])
                if half:
                    fw.op("act", lambda e, ps=ps, st_=st_: e.activation(out=st_[:, 512:1024], in_=ps[:], func=AF.Identity), reads=[ps], writes=[st_])
                else:
                    fw.op("dve", lambda e, ps=ps, st_=st_: e.tensor_copy(out=st_[:, 0:512], in_=ps[:]), reads=[ps], writes=[st_])
            fw.dma(lambda e, t=t, st_=st_: e.dma_start(out=dst[t * 128:(t + 1) * 128, :], in_=st_[:]), reads=[st_], writes=[self.OUTB])

    def ffn(self, xT, hT, t0, ntok, layer, which):
        fw, I = self.fw, self.I
        actT = self.actT
        gate_i = 5
        nb = ntok // 512
        for n in range(11):
            ws = self.WS()
            self.load_w(I["w_ffn_in"][layer], 0, 8, n * 256, 256, dst=ws, col_off=0)
            self.load_w(I["w_ffn_in"][layer], 0, 8, DFF + n * 256, 256, dst=ws, col_off=256)
            for j in range(2):
                i = 2 * n + j
                for b in range(nb):
                    tb = t0 + b * 512
                    sg = self.ntmp[b % 2]

                    def ev_g(ps, sg=sg):
                        fw.op("act", lambda e: e.activation(out=sg[:], in_=ps[:], func=AF.Silu), reads=[ps], writes=[sg])
                    self.proj_fm(ws, 8, j * 128, hT, b * 512, 512, ev_g)

                    def ev_u(ps, sg=sg, i=i, b=b):
                        fw.op("dve", lambda e: e.tensor_tensor(out=actT[:, i, b * 512:(b + 1) * 512], in0=sg[:], in1=ps[:], op=ALU.mult),
                              reads=[sg, ps], writes=[actT])
                    self.proj_fm(ws, 8, 256 + j * 128, hT, b * 512, 512, ev_u)
        for s in range(4):
            ws = self.wsb_pool[s % 2]
            self.load_w(I["w_ffn_out"][layer], 0, 22, s * 256, 256, dst=ws)
            for cc in range(2):
                c = s * 2 + cc
                for b in range(nb):
                    tb = t0 + b * 512

                    def ev(ps, c=c, tb=tb):
                        fw.op("dve", lambda e: e.scalar_tensor_tensor(out=xT[:, c, tb:tb + 512], in0=ps[:], scalar=self.modT[:, layer, 8 * gate_i + c, which:which + 1],
                                                                      in1=xT[:, c, tb:tb + 512], op0=ALU.mult, op1=ALU.add),
                              reads=[ps, self.modT, xT], writes=[xT])
                    self.proj_fm(ws, 22, cc * 128, actT, b * 512, 512, ev)

    def out_proj(self, xT, mixT, W, t0, ntok, layer, which, m0=0):
        fw = self.fw
        for s in range(2):
            ws = self.load_w(W, 0, 8, s * 512, 512)
            for cc in range(4):
                c = s * 4 + cc
                for tb in range(0, ntok, 512):
                    def ev(ps, c=c, tb=tb):
                        fw.op("dve", lambda e: e.scalar_tensor_tensor(out=xT[:, c, t0 + tb:t0 + tb + 512], in0=ps[:], scalar=self.modT[:, layer, 16 + c, which:which + 1],
                                                                      in1=xT[:, c, t0 + tb:t0 + tb + 512], op0=ALU.mult, op1=ALU.add),
                              reads=[ps, self.modT, xT], writes=[xT])
                    self.proj_fm(ws, 8, cc * 128, mixT, m0 + tb, 512, ev)

    def ctx_attention(self, hT, mixT, st):
        fw, I, O, C = self.fw, self.I, self.O, self.C
        NT = 1024
        QTn = fw.sbuf(st, "QTn", [128, 4, NT], BF16)
        KTn = fw.sbuf(st, "KTn", [128, 4, NT], BF16)
        QTd = fw.sbuf(st, "QTd", [128, 4, NT], BF16)
        KTd = fw.sbuf(st, "KTd", [128, 4, NT], BF16)
        Vn = fw.sbuf(st, "Vn", [128, 8, 512], BF16)
        Vd = fw.sbuf(st, "Vd", [128, 8, 512], BF16)
        Wi = I["w_in_attn"]
        if self.stop_after == "nm":
            return

        def fm_into(dst):
            def run(ws):
                for j in range(4):
                    for tb in range(0, NT, 512):
                        def ev(ps, j=j, tb=tb):
                            fw.op("act", lambda e: e.activation(out=dst[:, j, tb:tb + 512], in_=ps[:], func=AF.Identity), reads=[ps], writes=[dst])
                        self.proj_fm(ws, 8, j * 128, hT, tb, 512, ev)
            return run

        def tm_out(dram, vdst):
            def run(ws):
                for t in range(NT // 128):
                    def ev(ps, t=t):
                        sg = self.ntmp[t % 2]
                        fw.op("dve", lambda e: e.tensor_copy(out=sg[:], in_=ps[:]), reads=[ps], writes=[sg])
                        fw.dma(lambda e: e.dma_start(out=dram[t * 128:(t + 1) * 128, :], in_=sg[:]), reads=[sg], writes=[self.OUTB])
                        if vdst is not None:
                            fw.op("act", lambda e: e.activation(out=vdst[:, t, :], in_=sg[:], func=AF.Identity), reads=[sg], writes=[vdst])
                    self.proj_tm(ws, 8, 0, 512, hT, t * 128, ev)
            return run
        plan = [(0, [fm_into(QTn)]), (1, [fm_into(KTn), tm_out(O["o_na_k"], None)]), (2, [tm_out(O["o_na_v"], Vn)]),
                (3, [fm_into(QTd)]), (4, [fm_into(KTd), tm_out(O["o_df_k"], None)]), (5, [tm_out(O["o_df_v"], Vd)])]
        if self.stop_after in ("p0", "p1"):
            plan = plan[2:3] if self.stop_after == "p0" else plan[1:2]
        for slab, fns in plan:
            ws = self.load_w(Wi, 0, 8, slab * 512, 512)
            for f in fns:
                f(ws)
        if self.stop_after in ("proj", "p0", "p1"):
            return
        PT = [fw.sbuf(st, f"PT{i}", [128, 256], BF16) for i in range(4)]
        rd = [fw.sbuf(st, f"rd{i}", [128, 256], F32) for i in range(2)]
        tt = [fw.sbuf(st, f"tt{i}", [128, 256], F32) for i in range(2)]
        def seq_body(s, q0):
            for hp in (range(4) if self.stop_after != "ad" else []):
                for hh in range(2):
                    h = 2 * hp + hh
                    po = 64 * hh
                    psn = self.PS()
                    pts = []
                    for kt in range(2):
                        ps = self.PS()
                        fw.op("pe", lambda e, ps=ps, po=po, hp=hp, kt=kt: e.matmul(ps[:, 0:256], KTn[po:po + 64, hp, q0 + kt * 128:q0 + (kt + 1) * 128],
                                                                                  QTn[po:po + 64, hp, q0:q0 + 256], start=True, stop=True),
                              reads=[KTn, QTn], writes=[ps])
                        pt = PT[(2 * hh + kt) % 4]
                        fw.op("act", lambda e, ps=ps, pt=pt: e.activation(out=pt[:], in_=ps[:, 0:256], func=AF.Exp, scale=0.125), reads=[ps], writes=[pt])
                        pts.append(pt)
                    for kt in range(2):
                        fw.op("pe", lambda e, kt=kt, hp=hp, pt=pts[kt], psn=psn: e.matmul(psn[:, 0:256], Vn[:, 2 * s + kt, 128 * hp:128 * hp + 128], pt[:],
                                                                                        start=(kt == 0), stop=(kt == 1)), reads=[Vn, pts[kt]], writes=[psn])
                    for kt in range(2):
                        fw.op("pe", lambda e, kt=kt, pt=pts[kt], psn=psn: e.matmul(psn[:, 256:512], C["ones_b"][:, :], pt[:],
                                                                                 start=(kt == 0), stop=(kt == 1)), reads=[C["ones_b"], pts[kt]], writes=[psn])
                    r = rd[hh]
                    fw.op("dve", lambda e, r=r, psn=psn, po=po: e.reciprocal(out=r[po:po + 64, :], in_=psn[po:po + 64, 256:512]), reads=[psn], writes=[r])
                    fw.op("dve", lambda e, r=r, psn=psn, hp=hp, po=po: e.tensor_tensor(out=mixT[po:po + 64, hp, q0:q0 + 256], in0=psn[po:po + 64, 0:256],
                                                                                     in1=r[po:po + 64, :], op=ALU.mult), reads=[psn, r], writes=[mixT])
            for j in (range(4) if self.stop_after != "an" else []):
                psA = self.PS()
                psB = self.PS()
                for c in range(2):
                    po = 64 * c
                    psx = psA if c == 0 else psB
                    pts = []
                    for kt in range(2):
                        ps = self.PS()
                        fw.op("pe", lambda e, ps=ps, po=po, j=j, kt=kt: e.matmul(ps[:, 0:256], KTd[po:po + 64, j, q0 + kt * 128:q0 + (kt + 1) * 128],
                                                                                QTd[po:po + 64, j, q0:q0 + 256], start=True, stop=True),
                              reads=[KTd, QTd], writes=[ps])
                        pt = PT[(2 * c + kt) % 4]
                        fw.op("act", lambda e, ps=ps, pt=pt: e.activation(out=pt[:], in_=ps[:, 0:256], func=AF.Exp, scale=0.125), reads=[ps], writes=[pt])
                        pts.append(pt)
                    for kt in range(2):
                        fw.op("pe", lambda e, kt=kt, j=j, pt=pts[kt], psx=psx: e.matmul(psx[:, 0:256], Vd[:, 2 * s + kt, 128 * j:128 * j + 128], pt[:],
                                                                                      start=(kt == 0), stop=(kt == 1)), reads=[Vd, pts[kt]], writes=[psx])
                    for kt in range(2):
                        fw.op("pe", lambda e, kt=kt, pt=pts[kt], psx=psx: e.matmul(psx[:, 256:512], C["ones_b"][:, :], pt[:],
                                                                                 start=(kt == 0), stop=(kt == 1)), reads=[C["ones_b"], pts[kt]], writes=[psx])
                self.diff_combine(psA, psB, rd, tt, mixT, 4 + j, q0, 256)
        for s_ in range(4):
            seq_body(s_, s_ * 256)

    def diff_combine(self, psA, psB, rd, tt, mixT, chunk, q0, n):
        fw, C = self.fw, self.C
        r0, r1 = rd
        t0, t1 = tt
        fw.op("dve", lambda e: e.reciprocal(out=r0[:, 0:n], in_=psA[:, 256:256 + n]), reads=[psA], writes=[r0])
        fw.op("dve", lambda e: e.reciprocal(out=r1[:, 0:n], in_=psB[:, 256:256 + n]), reads=[psB], writes=[r1])
        fw.op("dve", lambda e: e.tensor_tensor(out=t0[:, 0:n], in0=psA[:, 0:n], in1=r0[:, 0:n], op=ALU.mult), reads=[psA, r0], writes=[t0])
        fw.op("dve", lambda e: e.tensor_tensor(out=t1[:, 0:n], in0=psB[:, 0:n], in1=r1[:, 0:n], op=ALU.mult), reads=[psB, r1], writes=[t1])
        fw.op("dve", lambda e: e.scalar_tensor_tensor(out=t0[:, 0:n], in0=t1[:, 0:n], scalar=self.lamc[:, 0:1], in1=t0[:, 0:n], op0=ALU.mult, op1=ALU.add),
              reads=[t0, t1, self.lamc], writes=[t0])
        fw.op("act", lambda e: e.activation(out=t1[:, 0:n], in_=t0[:, 0:n], func=AF.Square), reads=[t0], writes=[t1])
        ps = self.PS()
        fw.op("pe", lambda e: e.matmul(ps[:, 0:n], C["ones_f"][:, :], t1[:, 0:n], start=True, stop=True), reads=[C["ones_f"], t1], writes=[ps])
        fw.op("act", lambda e: e.activation(out=r0[:, 0:n], in_=ps[:, 0:n], func=AF.Sqrt, bias=C["eps"][:, 0:1], scale=1.0 / 128), reads=[ps, C["eps"]], writes=[r0])
        fw.op("dve", lambda e: e.reciprocal(out=r0[:, 0:n], in_=r0[:, 0:n]), reads=[r0], writes=[r0])
        fw.op("dve", lambda e: e.scalar_tensor_tensor(out=mixT[:, chunk, q0:q0 + n], in0=t0[:, 0:n], scalar=1.0 - self.lam_init, in1=r0[:, 0:n],
                                                      op0=ALU.mult, op1=ALU.mult), reads=[t0, r0], writes=[mixT])

    def phase(self, which):
        fw, I, O = self.fw, self.I, self.O
        NT = 1024 if which == 0 else 2048
        with ExitStack() as st:
            xT = fw.sbuf(st, "xT", [128, 8, NT], F32)
            hT = fw.sbuf(st, "hT", [128, 8, 1024], BF16)
            self.rstd = fw.sbuf(st, "rstd", [128, 512], F32)
            self.ntmp = [fw.sbuf(st, f"ntmp{i}", [128, 512], F32) for i in range(4)]
            self.tmpA = fw.sbuf(st, "tmpA", [128, 2, 8], F32)
            with ExitStack() as st2:
                self.stage = [fw.sbuf(st2, f"stage{i}", [128, 1024], F32) for i in range(2)]
                self.load_xT(xT, I["x_prompt"] if which == 0 else I["x_sample"], NT)
                self.flush()
            def dump():
                with ExitStack() as st2:
                    self.stage = [fw.sbuf(st2, f"stage{i}", [128, 1024], F32) for i in range(2)]
                    self.store_T(xT, O["y_prompt"] if which == 0 else O["y_sample"], NT)
                    self.flush()
            if self.stop_after == "load":
                return dump()
            with ExitStack() as st2:
                mixT = fw.sbuf(st2, "mixT", [128, 8, NT], BF16)
                if which == 1:
                    self.hT_group = None
                    self.lat_attention(xT, hT, mixT)
                    if self.stop_after == "mixl":
                        mf = fw.sbuf(st2, "mixf", [128, 8, 512], F32)
                        self.stage = [fw.sbuf(st2, f"stage{i}", [128, 1024], F32) for i in range(2)]
                        for hf in range(4):
                            fw.op("dve", lambda e, hf=hf: e.tensor_copy(out=mf[:], in_=mixT[:, :, hf * 512:(hf + 1) * 512]), reads=[mixT], writes=[mf])
                            self.store_T(mf, O["y_sample"][hf * 512:(hf + 1) * 512, :], 512)
                        self.flush()
                        return
                    self.out_proj(xT, mixT, I["w_out_attn"], 0, NT, 0, which)
                if which == 0:
                    self.norm_mod(xT, NT, hT, 0, 0, which, 0, 1)
                    self.ctx_attention(hT, mixT, st2)
                    if self.stop_after not in ("proj", "attn", "nm", "p0", "p1", "an", "ad"):
                        self.out_proj(xT, mixT, I["w_out_attn"], 0, NT, 0, which)
                self.flush()
            if self.stop_after in ("l0a", "proj", "attn", "nm", "p0", "p1", "an", "ad"):
                return dump()
            self.ffn_all(xT, hT, NT, 0, which)
            if self.stop_after == "l0":
                return dump()
            with ExitStack() as st2:
                mixT = fw.sbuf(st2, "mixT1", [128, 8, NT], BF16)
                self.hT_group = None
                ng = NT // GS
                if which == 0:
                    sched = [([0, 1], g_, True) for g_ in range(ng)]
                else:
                    sched = [([1], g_, False) for g_ in reversed(range(ng))] + [([0], g_, True) for g_ in range(ng)]
                nseq, T = (4, 256) if which == 0 else (1, 2048)
                self.retention(xT, hT, NT, which, mixT, sched, nseq, T)
                if self.stop_after != "ret":
                    self.rwkv(xT, hT, NT, which, mixT, sched, nseq, T)
                if self.stop_after in ("ret", "rw"):
                    mf = fw.sbuf(st2, "mixf", [128, 8, 1024], F32)
                    fw.op("dve", lambda e: e.tensor_copy(out=mf[:], in_=mixT[:, :, 0:1024]), reads=[mixT], writes=[mf])
                    self.stage = [fw.sbuf(st2, f"stage{i}", [128, 1024], F32) for i in range(2)]
                    self.store_T(mf, O["y_prompt"], 1024)
                    self.flush()
                    return
                self.out_proj(xT, mixT, I["w_out_rec"], 0, NT, 1, which)
                self.flush()
            if self.stop_after == "l1a":
                return dump()
            self.ffn_all(xT, hT, NT, 1, which)
            with ExitStack() as st2:
                self.stage = [fw.sbuf(st2, f"stage{i}", [128, 1024], F32) for i in range(2)]
                yT = fw.sbuf(st2, "yT", [128, 8, 512], F32)
                self.final_norm_store(xT, NT, yT, O["y_prompt"] if which == 0 else O["y_sample"])
                self.flush()

    def ffn_all(self, xT, hT, NT, layer, which):
        fw = self.fw
        with ExitStack() as st2:
            self.actT = fw.sbuf(st2, "actT", [128, NFF, 1024], BF16)
            self.wsb_pool = [fw.sbuf(st2, f"wsb{i}", [128, NFF, 256], BF16) for i in range(2)]
            for t0 in range(0, NT, 1024):
                self.norm_mod(xT, 1024, hT, 2 + layer, layer, which, 3, 4, t0=t0)
                self.ffn(xT, hT, t0, 1024, layer, which)
            self.flush()

    def final_norm_store(self, xT, NT, yT, dst):
        fw, C = self.fw, self.C
        for b0 in range(0, NT, 512):
            ps = self.PS()
            for c in range(8):
                sq = self.ntmp[c % 4]
                fw.op("act", lambda e, c=c, b0=b0, sq=sq: e.activation(out=sq[:].bitcast(BF16)[:, 0:512], in_=xT[:, c, b0:b0 + 512], func=AF.Square), reads=[xT], writes=[sq])
                fw.op("pe", lambda e, c=c, ps=ps, sq=sq: e.matmul(ps[:, :], C["ones_b"][:, :], sq[:].bitcast(BF16)[:, 0:512], start=(c == 0), stop=(c == 7)),
                      reads=[sq, C["ones_b"]], writes=[ps])
            rs = self.rstd
            fw.op("act", lambda e, ps=ps: e.activation(out=rs[:], in_=ps[:], func=AF.Sqrt, bias=C["eps"][:, 0:1], scale=1.0 / D), reads=[ps, C["eps"]], writes=[rs])
            fw.op("dve", lambda e: e.reciprocal(out=rs[:], in_=rs[:]), reads=[rs], writes=[rs])
            for c in range(8):
                fw.op("dve", lambda e, c=c, b0=b0: e.scalar_tensor_tensor(out=yT[:, c, :], in0=xT[:, c, b0:b0 + 512], scalar=self.gains[:, 4, c:c + 1], in1=rs[:],
                                                                        op0=ALU.mult, op1=ALU.mult), reads=[xT, self.gains, rs], writes=[yT])
            self.store_T(yT, dst[b0:b0 + 512, :], 512)


    def rec_consts(self):
        fw, I, g = self.fw, self.I, self.gst
        R = self.R = {}
        M = R["M"] = fw.sbuf(g, "masks", [128, 4, 128], F32)
        fw.dma(lambda e: e.dma_start(out=M[:], in_=I["c_masks"].rearrange("m p n -> p m n")), writes=[M])
        DIFF = R["DIFF"] = fw.sbuf(g, "cdiff", [128, 128], F32)
        fw.dma(lambda e: e.dma_start(out=DIFF[:], in_=I["c_diff"]), writes=[DIFF])
        IP = R["IP"] = fw.sbuf(g, "cip", [128, 2, 128], F32)
        fw.dma(lambda e: e.dma_start(out=IP[:], in_=I["c_ip"].rearrange("m p n -> p m n")), writes=[IP])
        JC = R["JC"] = fw.sbuf(g, "cjc", [128, 2], F32)
        fw.dma(lambda e: e.dma_start(out=JC[:], in_=I["c_jc"]), writes=[JC])
        HALF = R["HALF"] = fw.sbuf(g, "chalf", [128, 2], F32)
        fw.dma(lambda e: e.dma_start(out=HALF[:], in_=I["c_half"]), writes=[HALF])
        lg = R["lg"] = fw.sbuf(g, "lgrep", [128, 16], F32)
        nlg = R["nlg"] = fw.sbuf(g, "nlgrep", [128, 16], F32)
        fw.dma(lambda e: e.dma_start(out=lg[:], in_=I["ret_decay_logit"].partition_broadcast(128)), writes=[lg])
        fw.op("act", lambda e: e.activation(out=nlg[:], in_=lg[:], func=AF.Exp, scale=-1.0), reads=[lg], writes=[nlg])
        fw.op("dve", lambda e: e.tensor_scalar(out=nlg[:], in0=nlg[:], scalar1=1.0, scalar2=None, op0=ALU.add), reads=[nlg], writes=[nlg])
        fw.op("act", lambda e: e.activation(out=nlg[:], in_=nlg[:], func=AF.Ln), reads=[nlg], writes=[nlg])
        fw.op("dve", lambda e: e.tensor_scalar(out=lg[:], in0=nlg[:], scalar1=-1.0, scalar2=None, op0=ALU.mult), reads=[nlg], writes=[lg])
        LGc = R["LGc"] = fw.sbuf(g, "lgcol", [128, 8], F32)
        tmp = fw.sbuf(g, "lgtmp", [128, 8], F32)
        lgv = lg[:].rearrange("p (a two) -> p a two", two=2)
        fw.op("dve", lambda e: e.tensor_scalar(out=LGc[:], in0=lgv[:, :, 0], scalar1=HALF[:, 0:1], scalar2=None, op0=ALU.mult), reads=[lg, HALF], writes=[LGc])
        fw.op("dve", lambda e: e.tensor_scalar(out=tmp[:], in0=lgv[:, :, 1], scalar1=HALF[:, 1:2], scalar2=None, op0=ALU.mult), reads=[lg, HALF], writes=[tmp])
        fw.op("dve", lambda e: e.tensor_tensor(out=LGc[:], in0=LGc[:], in1=tmp[:], op=ALU.add), reads=[LGc, tmp], writes=[LGc])
        CD = R["CD"] = fw.sbuf(g, "cdec", [128, 8], F32)
        fw.op("act", lambda e: e.activation(out=CD[:], in_=LGc[:], func=AF.Exp, scale=128.0), reads=[LGc], writes=[CD])
        KD = R["KD"] = fw.sbuf(g, "kdec", [128, 16], F32)
        fw.op("act", lambda e: e.activation(out=KD[:, 0:8], in_=lg[:, 0:8], func=AF.Exp, scale=JC[:, 0:1]), reads=[lg, JC], writes=[KD])
        fw.op("act", lambda e: e.activation(out=KD[:, 8:16], in_=lg[:, 8:16], func=AF.Exp, scale=JC[:, 1:2]), reads=[lg, JC], writes=[KD])
        def cols(name, vec, n):
            t = fw.sbuf(g, name, [128, n], F32)
            fw.dma(lambda e: e.dma_start(out=t[:], in_=vec.rearrange("(c p) -> p c", p=128), allow_slow_non_contiguous=True), writes=[t])
            return t
        R["KKc"] = cols("kkc", I["rw_k_k"], 4)
        R["KAc"] = cols("kac", I["rw_k_a"], 4)
        R["RKc"] = cols("rkc", I["rw_r_k"], 4)
        R["LGNc"] = cols("lngc", I["rw_ln_g"], 4)
        R["LNBc"] = cols("lnbc", I["rw_ln_b"], 4)
        A0c = R["A0c"] = fw.sbuf(g, "a0c", [128, 2, 4], F32)
        for d in range(2):
            fw.dma(lambda e, d=d: e.dma_start(out=A0c[:, d, :], in_=I["rw_a0"][d].rearrange("(c p) -> p c", p=128), allow_slow_non_contiguous=True), writes=[A0c])

        self.flush()

    def blk_stat(self, src, n, scale, eps_col, out):
        fw, C = self.fw, self.C
        sq = self.ntmp[3]
        srcbuf = self._srcbuf
        fw.op("act", lambda e: e.activation(out=sq[:].bitcast(BF16)[:, 0:n], in_=src, func=AF.Square), reads=[srcbuf], writes=[sq])
        ps = self.PS()
        fw.op("pe", lambda e: e.matmul(ps[:, 0:n], C["blk_b"][:, :], sq[:].bitcast(BF16)[:, 0:n], start=True, stop=True), reads=[C["blk_b"], sq], writes=[ps])
        fw.op("act", lambda e: e.activation(out=out[:, 0:n], in_=ps[:, 0:n], func=AF.Sqrt, bias=C["eps"][:, eps_col:eps_col + 1], scale=scale), reads=[ps, C["eps"]], writes=[out])
        fw.op("dve", lambda e: e.reciprocal(out=out[:, 0:n], in_=out[:, 0:n]), reads=[out], writes=[out])

    def retention(self, xT, hT, NT, which, mixT, sched, nseq, T):
        fw, I, O, C, R = self.fw, self.I, self.O, self.C, self.R
        Wi = I["w_in_rec"]
        with ExitStack() as st:
            wsl = list(self.ws_pool) + [fw.sbuf(st, "rws3", [128, 8, 512], BF16)]
            for i in range(4):
                self.load_w(Wi, 0, 8, i * 512, 512, dst=wsl[i])
            qT = fw.sbuf(st, "r_qT", [128, GS], BF16)
            kT = fw.sbuf(st, "r_kT", [128, GS], BF16)
            sgT = fw.sbuf(st, "r_sgT", [128, GS], BF16)
            vTr = fw.sbuf(st, "r_vT", [128, GS], BF16)
            Ktok = fw.sbuf(st, "r_Ktok", [128, NTG, 128], BF16)
            Vpad = fw.sbuf(st, "r_Vpad", [128, NTG, 2, 128], BF16)
            fw.op("pool", lambda e: e.memset(Vpad[:], 0.0), writes=[Vpad])
            M, DIFF, IP, lg, nlg, LGc = R["M"], R["DIFF"], R["IP"], R["lg"], R["nlg"], R["LGc"]
            RM = R["RM"] = fw.sbuf(st, "retmask", [128, 16, 128], F32)
            for dh in range(16):
                d = dh // 8
                src = lg if d == 0 else nlg
                fw.op("act", lambda e, dh=dh, src=src: e.activation(out=RM[:, dh, :], in_=DIFF[:], func=AF.Exp, scale=src[:, dh:dh + 1]), reads=[DIFF, src], writes=[RM])
                mi = 1 if d == 0 else 3
                fw.op("dve", lambda e, dh=dh, mi=mi: e.tensor_tensor(out=RM[:, dh, :], in0=RM[:, dh, :], in1=M[:, mi, :], op=ALU.mult), reads=[RM, M], writes=[RM])
            QD = R["QD"] = fw.sbuf(st, "qdec", [128, 8, 128], F32)
            for dp in range(8):
                fw.op("act", lambda e, dp=dp: e.activation(out=QD[:, dp, :], in_=IP[:, dp // 4, :], func=AF.Exp, scale=LGc[:, dp:dp + 1]), reads=[IP, LGc], writes=[QD])
            acc = fw.sbuf(st, "r_acc", [128, NT], F32)
            SMs = [[fw.sbuf(st, f"r_SM{q}{i}", [128, 128], BF16) for i in range(2)] for q in range(2)]
            KSs = [[fw.sbuf(st, f"r_KS{q}{i}", [128, 128], BF16) for i in range(2)] for q in range(2)]
            for q_ in range(2):
                for t_ in KSs[q_]:
                    fw.op("pool", lambda e, t_=t_: e.memset(t_[:], 0.0), writes=[t_])
            qds = [fw.sbuf(st, f"r_qd{q}", [128, 128], BF16) for q in range(2)]
            RS = [fw.sbuf(st, f"r_RS{d}", [128, 128], F32) for d in range(2)]
            RSb = [fw.sbuf(st, f"r_RSb{d}", [128, 128], BF16) for d in range(2)]
            rs_t = fw.sbuf(st, "r_rst", [128, 512], F32)
            om = fw.sbuf(st, "r_om", [128, 512], F32)
            nchunk_seq = T // 128

            def project(p, grp):
                for tb in range(0, GS, 512):
                    self.proj_fm(wsl[0], 8, 128 * p, hT, self.hoff + tb, 512, lambda ps, tb=tb: fw.op("act", lambda e: e.activation(out=qT[:, tb:tb + 512], in_=ps[:], func=AF.Identity), reads=[ps], writes=[qT]))
                    self.proj_fm(wsl[1], 8, 128 * p, hT, self.hoff + tb, 512, lambda ps, tb=tb: fw.op("act", lambda e: e.activation(out=kT[:, tb:tb + 512], in_=ps[:], func=AF.Identity, scale=0.125), reads=[ps], writes=[kT]))
                    self.proj_fm(wsl[3], 8, 128 * p, hT, self.hoff + tb, 512, lambda ps, tb=tb: fw.op("act", lambda e: e.activation(out=sgT[:, tb:tb + 512], in_=ps[:], func=AF.Silu), reads=[ps], writes=[sgT]))
                for tb in range(0, GS, 512):
                    self.proj_fm(wsl[2], 8, 128 * p, hT, self.hoff + tb, 512, lambda ps, tb=tb: fw.op("act", lambda e: e.activation(out=vTr[:, tb:tb + 512], in_=ps[:], func=AF.Identity), reads=[ps], writes=[vTr]))
                for t in range(NTG):
                    psk = self.PS()
                    fw.op("pe", lambda e, psk=psk, t=t: e.matmul(psk[:, 0:128], kT[:, t * 128:(t + 1) * 128], C["identb"][:], start=True, stop=True), reads=[kT, C["identb"]], writes=[psk])
                    fw.op("act", lambda e, psk=psk, t=t: e.activation(out=Ktok[:, t, :], in_=psk[:, 0:128], func=AF.Identity), reads=[psk], writes=[Ktok])
                    psv = self.PS()
                    fw.op("pe", lambda e, psv=psv, t=t: e.matmul(psv[:, 0:128], vTr[:, t * 128:(t + 1) * 128], C["identb"][:], start=True, stop=True), reads=[vTr, C["identb"]], writes=[psv])
                    fw.op("dve", lambda e, psv=psv, t=t: e.tensor_copy(out=Vpad[:, t, 0, 0:64], in_=psv[:, 0:64]), reads=[psv], writes=[Vpad])
                    fw.op("dve", lambda e, psv=psv, t=t: e.tensor_copy(out=Vpad[:, t, 1, 64:128], in_=psv[:, 64:128]), reads=[psv], writes=[Vpad])

            def pre(p, d, t):
                c0 = t * 128
                SM, KS, qd = SMs[t % 2], KSs[t % 2], qds[t % 2]
                for hh in range(2):
                    h = 2 * p + hh
                    po = 64 * hh
                    ps = self.PS()
                    fw.op("pe", lambda e, ps=ps, po=po: e.matmul(ps[:, 0:128], kT[po:po + 64, c0:c0 + 128], qT[po:po + 64, c0:c0 + 128], start=True, stop=True), reads=[kT, qT], writes=[ps])
                    fw.op("dve", lambda e, ps=ps, hh=hh, h=h: e.tensor_tensor(out=SM[hh][:], in0=ps[:, 0:128], in1=R["RM"][:, d * 8 + h, :], op=ALU.mult), reads=[ps, R["RM"]], writes=[SM[hh]])
                    fw.op("dve", lambda e, hh=hh, h=h, po=po: e.tensor_scalar(out=KS[hh][:, po:po + 64], in0=Ktok[:, t, po:po + 64], scalar1=R["KD"][:, d * 8 + h:d * 8 + h + 1], scalar2=None, op0=ALU.mult),
                          reads=[Ktok, R["KD"]], writes=[KS[hh]])
                fw.op("dve", lambda e: e.tensor_tensor(out=qd[:], in0=qT[:, c0:c0 + 128], in1=R["QD"][:, d * 4 + p, :], op=ALU.mult), reads=[qT, R["QD"]], writes=[qd])

            def chunk(p, d, t, gtok, first):
                c0 = t * 128
                SM, KS, qd = SMs[t % 2], KSs[t % 2], qds[t % 2]
                pso = self.PS()
                fw.op("pe", lambda e: e.matmul(pso[:, 0:128], RSb[d][:, :], qd[:], start=True, stop=False), reads=[RSb[d], qd], writes=[pso])
                for hh in range(2):
                    fw.op("pe", lambda e, hh=hh: e.matmul(pso[:, 0:128], Vpad[:, t, hh, :], SM[hh][:], start=False, stop=(hh == 1)), reads=[Vpad, SM[hh]], writes=[pso])
                a0 = gtok + c0
                if first:
                    fw.op("act", lambda e: e.activation(out=acc[:, a0:a0 + 128], in_=pso[:, 0:128], func=AF.Identity), reads=[pso], writes=[acc])
                else:
                    fw.op("dve", lambda e: e.tensor_tensor(out=acc[:, a0:a0 + 128], in0=pso[:, 0:128], in1=acc[:, a0:a0 + 128], op=ALU.add), reads=[pso, acc], writes=[acc])
                psS = self.PS()
                for hh in range(2):
                    fw.op("pe", lambda e, hh=hh: e.matmul(psS[:, 0:128], KS[hh][:], Vpad[:, t, hh, :], start=(hh == 0), stop=(hh == 1)), reads=[KS[hh], Vpad], writes=[psS])
                fw.op("dve", lambda e: e.scalar_tensor_tensor(out=RS[d][:], in0=RS[d][:], scalar=R["CD"][:, d * 4 + p:d * 4 + p + 1], in1=psS[:, 0:128], op0=ALU.mult, op1=ALU.add),
                      reads=[RS[d], R["CD"], psS], writes=[RS[d]])
                fw.op("act", lambda e: e.activation(out=RSb[d][:], in_=RS[d][:], func=AF.Identity), reads=[RS[d]], writes=[RSb[d]])

            def init_state(p, d):
                fw.op("pool", lambda e: e.memset(RS[d][:], 0.0), writes=[RS[d]])
                if which == 1:
                    for hh in range(2):
                        po = 64 * hh
                        fw.dma(lambda e, hh=hh, po=po: e.dma_start(out=RS[d][po:po + 64, po:po + 64], in_=I["state_ret"][d, 2 * p + hh]), writes=[RS[d]])
                fw.op("act", lambda e: e.activation(out=RSb[d][:], in_=RS[d][:], func=AF.Identity), reads=[RS[d]], writes=[RSb[d]])

            def out_state(p, d, s):
                for hh in range(2):
                    po = 64 * hh
                    fw.dma(lambda e, hh=hh, po=po: e.dma_start(out=O["o_sret"][s, d, 2 * p + hh], in_=RS[d][po:po + 64, po:po + 64]), reads=[RS[d]], writes=[self.OUTB])

            def finalize(p, grp):
                g0 = grp * GS
                for tb in range(0, GS, 512):
                    self._srcbuf = acc
                    self.blk_stat(acc[:, g0 + tb:g0 + tb + 512], 512, 1.0 / 64, 0, rs_t)
                    fw.op("dve", lambda e, tb=tb: e.tensor_tensor(out=om[:], in0=acc[:, g0 + tb:g0 + tb + 512], in1=rs_t[:], op=ALU.mult), reads=[acc, rs_t], writes=[om])
                    fw.op("dve", lambda e, tb=tb: e.tensor_tensor(out=mixT[:, p, g0 + tb:g0 + tb + 512], in0=om[:], in1=sgT[:, tb:tb + 512], op=ALU.mult), reads=[om, sgT], writes=[mixT])

            self.run_sched(xT, hT, which, sched, nseq, nchunk_seq, project, chunk, init_state, out_state, finalize, pre=pre)
            self.flush()

    def run_sched(self, xT, hT, which, sched, nseq, nchunk_seq, project, chunk, init_state, out_state, finalize, prep_dir=None, pre=None):
        ngroups = (1024 if which == 0 else 2048) // GS
        for p in range(4):
            seen_first = set()
            for (dirs, grp, fin) in sched:
                blk = (grp * GS) // 1024
                if self.hT_group != blk:
                    self.norm_mod(xT, 1024, hT, 1, 1, which, 0, 1, t0=blk * 1024)
                    self.hT_group = blk
                    self.lora_group = None
                self.hoff = (grp * GS) % 1024
                project(p, grp)
                for d in dirs:
                    if prep_dir is not None:
                        prep_dir(p, d)
                    if which == 0:
                        spg = GS // (nchunk_seq * 128)
                        for sl in range(spg):
                            init_state(p, d)
                            tiles = list(range(sl * nchunk_seq, (sl + 1) * nchunk_seq))
                            if d == 1:
                                tiles = tiles[::-1]
                            if pre is not None:
                                pre(p, d, tiles[0])
                            for i_, t in enumerate(tiles):
                                if pre is not None and i_ + 1 < len(tiles):
                                    pre(p, d, tiles[i_ + 1])
                                chunk(p, d, t, grp * GS, (grp, t) not in seen_first)
                                seen_first.add((grp, t))
                            out_state(p, d, grp * spg + sl)
                    else:
                        start_grp = 0 if d == 0 else ngroups - 1
                        if grp == start_grp:
                            init_state(p, d)
                        tiles = list(range(NTG))
                        if d == 1:
                            tiles = tiles[::-1]
                        if pre is not None:
                            pre(p, d, tiles[0])
                        for i_, t in enumerate(tiles):
                            if pre is not None and i_ + 1 < len(tiles):
                                pre(p, d, tiles[i_ + 1])
                            chunk(p, d, t, grp * GS, (grp, t) not in seen_first)
                            seen_first.add((grp, t))
                if fin:
                    finalize(p, grp)

    def rwkv(self, xT, hT, NT, which, mixT, sched, nseq, T):
        fw, I, O, C, R = self.fw, self.I, self.O, self.C, self.R
        Wi = I["w_in_rec"]
        M = R["M"]
        NEG_E = -math.exp(-0.5)
        with ExitStack() as st:
            wsl = self.ws_pool
            for i in range(3):
                self.load_w(Wi, 0, 8, 2048 + i * 512, 512, dst=wsl[i])
            W0r = R["W0r"] = fw.sbuf(st, "w0r", [128, 2, 128], F32)
            A0r = R["A0r"] = fw.sbuf(st, "a0r", [128, 2, 128], F32)
            KKr = R["KKr"] = fw.sbuf(st, "kkr", [128, 128], F32)
            KAr = R["KAr"] = fw.sbuf(st, "kar", [128, 128], F32)
            wup = R["wup"] = fw.sbuf(st, "wupb", [128, 2, 128], BF16)
            aup = R["aup"] = fw.sbuf(st, "aupb", [128, 2, 128], BF16)
            gup = R["gup"] = fw.sbuf(st, "gupb", [128, 512], BF16)
            fw.dma(lambda e: e.dma_start(out=gup[:], in_=I["rw_g_up"]), writes=[gup], queue="pool")

            def load_pair_params(p):
                pc = 128 * p
                for d in range(2):
                    fw.dma(lambda e, d=d: e.dma_start(out=W0r[:, d, :], in_=I["rw_w0"][d, pc:pc + 128].partition_broadcast(128)), writes=[W0r])
                    fw.dma(lambda e, d=d: e.dma_start(out=A0r[:, d, :], in_=I["rw_a0"][d, pc:pc + 128].partition_broadcast(128)), writes=[A0r])
                fw.dma(lambda e: e.dma_start(out=KKr[:], in_=I["rw_k_k"][pc:pc + 128].partition_broadcast(128)), writes=[KKr])
                fw.dma(lambda e: e.dma_start(out=KAr[:], in_=I["rw_k_a"][pc:pc + 128].partition_broadcast(128)), writes=[KAr])
                fw.dma(lambda e: e.dma_start(out=wup[0:64, :, :], in_=I["rw_w_up"][:, :, pc:pc + 128].rearrange("d k n -> k d n")), writes=[wup], queue="pool")
                fw.dma(lambda e: e.dma_start(out=aup[64:128, :, :], in_=I["rw_a_up"][:, :, pc:pc + 128].rearrange("d k n -> k d n")), writes=[aup], queue="pool")
            self._pair_loaded = None
            wsx = fw.sbuf(st, "wwsx", [128, 8, 256], BF16)
            self.load_w(Wi, 0, 8, 3584, 256, dst=wsx)
            sb = lambda name, shape, dt=F32: fw.sbuf(st, name, shape, dt)
            lora = sb("w_lora", [128, GS], BF16)
            sgd = sb("w_sgd", [128, GS], BF16)
            rT = sb("w_rT", [128, GS], BF16); kT = sb("w_kT", [128, GS]); vT = sb("w_vT", [128, GS], BF16); kkT = sb("w_kkT", [128, GS], BF16)
            ktok = sb("w_ktok", [128, NTG, 128]); kktok = sb("w_kktok", [128, NTG, 128])
            Vpad = sb("w_Vpad", [128, NTG, 2, 128], BF16)
            fw.op("pool", lambda e: e.memset(Vpad[:], 0.0), writes=[Vpad])
            keffT = sb("w_keffT", [128, GS], BF16); bT = sb("w_bT", [128, GS], BF16)
            atok = sb("w_atok", [128, NTG, 128]); kefftok = sb("w_kefftok", [128, NTG, 128]); btok = sb("w_btok", [128, NTG, 128]); wlog = sb("w_wlog", [128, NTG, 128])
            acc = sb("w_acc", [128, NT])
            ssq = sb("w_ssq", [128, 2 * NTG])
            eL = sb("w_eL", [128, 128]); eLp = sb("w_eLp", [128, 128]); enL = sb("w_enL", [128, 128]); eD = sb("w_eD", [128, 128])
            AR = sb("w_AR", [128, 2, 128], BF16); BT = sb("w_BT", [128, 128], BF16); KTt = sb("w_KTt", [128, 128], BF16)
            pad = lambda name: [sb(f"{name}{i}", [128, 128], BF16) for i in range(2)]
            Bp, Kp, Up = pad("w_Bp"), pad("w_Kp"), pad("w_Up")
            for t_ in Bp + Kp + Up:
                fw.op("pool", lambda e, t_=t_: e.memset(t_[:], 0.0), writes=[t_])
            Mbr, Mak, Mkr = pad("w_Mbr"), pad("w_Mak"), pad("w_Mkr")
            XA = [[sb(f"w_X{h}{i}", [128, 2, 128], BF16) for i in range(2)] for h in range(2)]
            YA = [[sb(f"w_Y{h}{i}", [128, 128], BF16) for i in range(2)] for h in range(2)]
            XP = [[Buf("xp", XA[h][i].t) for i in range(2)] for h in range(2)]
            XT_ = [[Buf("xt", XA[h][i].t) for i in range(2)] for h in range(2)]
            XTs = [sb(f"w_XTs{i}", [128, 64], BF16) for i in range(2)]
            identb = sb("w_identb", [128, 128], BF16)
            fw.op("dve", lambda e: e.tensor_copy(out=identb[:], in_=C["ident"][:]), reads=[C["ident"]], writes=[identb])
            ST = [sb(f"w_ST{d}", [128, 128]) for d in range(2)]
            STb = [sb(f"w_STb{d}", [128, 128], BF16) for d in range(2)]
            stg = sb("w_stg", [128, 128])
            t512 = self.ntmp[0:3]
            nchunk_seq = T // 128

            def lora_inputs():
                for tb in range(0, GS, 512):
                    def ev1(ps, tb=tb):
                        fw.op("act", lambda e: e.activation(out=lora[0:64, tb:tb + 512], in_=ps[0:64, :], func=AF.Tanh), reads=[ps], writes=[lora])
                        fw.op("act", lambda e: e.activation(out=lora[64:128, tb:tb + 512], in_=ps[64:128, :], func=AF.Identity), reads=[ps], writes=[lora])
                    self.proj_fm(wsx, 8, 0, hT, self.hoff + tb, 512, ev1)
                    self.proj_fm(wsx, 8, 128, hT, self.hoff + tb, 512, lambda ps, tb=tb: fw.op("act", lambda e: e.activation(out=sgd[:, tb:tb + 512], in_=ps[:], func=AF.Sigmoid), reads=[ps], writes=[sgd]))

            def project(p, grp):
                if self._pair_loaded != p:
                    load_pair_params(p)
                    self._pair_loaded = p
                if self.lora_group != grp:
                    lora_inputs()
                    self.lora_group = grp
                pc = 128 * p
                for tb in range(0, GS, 512):
                    for w_, dst in ((wsl[0], rT), (wsl[1], kT), (wsl[2], vT)):
                        self.proj_fm(w_, 8, pc, hT, self.hoff + tb, 512, lambda ps, tb=tb, dst=dst: fw.op("act", lambda e: e.activation(out=dst[:, tb:tb + 512], in_=ps[:], func=AF.Identity), reads=[ps], writes=[dst]))
                    t1, t2 = t512[0], t512[1]
                    fw.op("dve", lambda e, tb=tb: e.tensor_scalar(out=t1[:], in0=kT[:, tb:tb + 512], scalar1=R["KKc"][:, p:p + 1], scalar2=None, op0=ALU.mult), reads=[kT, R["KKc"]], writes=[t1])
                    self._srcbuf = t1
                    self.blk_stat(t1[:], 512, 1.0, 2, t2)
                    fw.op("dve", lambda e, tb=tb: e.tensor_tensor(out=kkT[:, tb:tb + 512], in0=t1[:], in1=t2[:], op=ALU.mult), reads=[t1, t2], writes=[kkT])
                for t in range(NTG):
                    psk = self.PS()
                    fw.op("pe", lambda e, psk=psk, t=t: e.transpose(psk[:, 0:128], kT[:, t * 128:(t + 1) * 128], C["ident"][:]), reads=[kT, C["ident"]], writes=[psk])
                    fw.op("act", lambda e, psk=psk, t=t: e.activation(out=ktok[:, t, :], in_=psk[:, 0:128], func=AF.Identity), reads=[psk], writes=[ktok])
                    psv = self.PS()
                    fw.op("pe", lambda e, psv=psv, t=t: e.matmul(psv[:, 0:128], vT[:, t * 128:(t + 1) * 128], C["identb"][:], start=True, stop=True), reads=[vT, C["identb"]], writes=[psv])
                    fw.op("dve", lambda e, psv=psv, t=t: e.tensor_copy(out=Vpad[:, t, 0, 0:64], in_=psv[:, 0:64]), reads=[psv], writes=[Vpad])
                    fw.op("dve", lambda e, psv=psv, t=t: e.tensor_copy(out=Vpad[:, t, 1, 64:128], in_=psv[:, 64:128]), reads=[psv], writes=[Vpad])
                kk3 = kktok[:].rearrange("p t (h f) -> p (t h) f", f=64)
                fw.op("dve", lambda e: e.tensor_tensor(out=kktok[:], in0=ktok[:], in1=R["KKr"][:, :].unsqueeze(1).broadcast_to([128, NTG, 128]), op=ALU.mult),
                      reads=[ktok, R["KKr"]], writes=[kktok])
                sq = sb_sq
                fw.op("act", lambda e: e.activation(out=sq[:], in_=kktok[:], func=AF.Square), reads=[kktok], writes=[sq])
                fw.op("dve", lambda e: e.tensor_reduce(out=ssq[:], in_=sq[:].rearrange("p t (h f) -> p (t h) f", f=64), axis=AX.X, op=ALU.add), reads=[sq], writes=[ssq])
                fw.op("act", lambda e: e.activation(out=ssq[:], in_=ssq[:], func=AF.Sqrt, bias=C["eps"][:, 2:3], scale=1.0), reads=[ssq, C["eps"]], writes=[ssq])
                fw.op("dve", lambda e: e.reciprocal(out=ssq[:], in_=ssq[:]), reads=[ssq], writes=[ssq])
                fw.op("dve", lambda e: e.tensor_tensor(out=kk3, in0=kk3, in1=ssq[:].unsqueeze(2).broadcast_to([128, 2 * NTG, 64]), op=ALU.mult), reads=[kktok, ssq], writes=[kktok])

            def prep_dir(p, d):
                pc = 128 * p
                for tb in range(0, GS, 512):
                    ps = self.PS()
                    fw.op("pe", lambda e, ps=ps, tb=tb: e.matmul(ps[:, :], R["aup"][64:128, d, :], lora[64:128, tb:tb + 512], start=True, stop=True), reads=[R["aup"], lora], writes=[ps])
                    aTt = t512[2]
                    fw.op("act", lambda e, ps=ps, tb=tb: e.activation(out=aTt[:], in_=ps[:], func=AF.Sigmoid, bias=R["A0c"][:, d, p:p + 1], scale=1.0), reads=[ps, R["A0c"]], writes=[aTt])
                    t1 = t512[0]
                    fw.op("dve", lambda e, tb=tb: e.tensor_scalar(out=t1[:], in0=aTt[:], scalar1=-1.0, scalar2=None, op0=ALU.add), reads=[aTt], writes=[t1])
                    fw.op("dve", lambda e, tb=tb: e.tensor_scalar(out=t1[:], in0=t1[:], scalar1=R["KAc"][:, p:p + 1], scalar2=None, op0=ALU.mult), reads=[t1, R["KAc"]], writes=[t1])
                    fw.op("dve", lambda e, tb=tb: e.scalar_tensor_tensor(out=keffT[:, tb:tb + 512], in0=t1[:], scalar=1.0, in1=kT[:, tb:tb + 512], op0=ALU.add, op1=ALU.mult), reads=[t1, kT], writes=[keffT])
                    fw.op("dve", lambda e, tb=tb: e.tensor_tensor(out=bT[:, tb:tb + 512], in0=kkT[:, tb:tb + 512], in1=aTt[:], op=ALU.mult), reads=[kkT, aTt], writes=[bT])
                for t in range(NTG):
                    ps = self.PS()
                    psb = self.PS()
                    fw.op("pe", lambda e, ps=ps, t=t: e.matmul(ps[:, 0:128], lora[64:128, t * 128:(t + 1) * 128], R["aup"][64:128, d, :], start=True, stop=True), reads=[R["aup"], lora], writes=[ps])
                    fw.op("pe", lambda e, psb=psb, t=t: e.matmul(psb[:, 0:128], lora[0:64, t * 128:(t + 1) * 128], R["wup"][0:64, d, :], start=True, stop=True), reads=[R["wup"], lora], writes=[psb])
                    fw.op("dve", lambda e, ps=ps, t=t: e.tensor_tensor(out=atok[:, t, :], in0=ps[:, 0:128], in1=R["A0r"][:, d, :], op=ALU.add), reads=[ps, R["A0r"]], writes=[atok])
                    fw.op("dve", lambda e, psb=psb, t=t: e.tensor_tensor(out=wlog[:, t, :], in0=psb[:, 0:128], in1=R["W0r"][:, d, :], op=ALU.add), reads=[psb, R["W0r"]], writes=[wlog])
                fw.op("act", lambda e: e.activation(out=atok[:], in_=atok[:], func=AF.Sigmoid), reads=[atok], writes=[atok])
                fw.op("act", lambda e: e.activation(out=wlog[:], in_=wlog[:], func=AF.Sigmoid), reads=[wlog], writes=[wlog])
                fw.op("dve", lambda e: e.tensor_scalar(out=wlog[:], in0=wlog[:], scalar1=NEG_E, scalar2=None, op0=ALU.mult), reads=[wlog], writes=[wlog])
                kar = R["KAr"][:, :].unsqueeze(1).broadcast_to([128, NTG, 128])
                fw.op("dve", lambda e: e.scalar_tensor_tensor(out=kefftok[:], in0=atok[:], scalar=-1.0, in1=kar, op0=ALU.add, op1=ALU.mult), reads=[atok, R["KAr"]], writes=[kefftok])
                fw.op("dve", lambda e: e.scalar_tensor_tensor(out=kefftok[:], in0=kefftok[:], scalar=1.0, in1=ktok[:], op0=ALU.add, op1=ALU.mult), reads=[kefftok, ktok], writes=[kefftok])
                fw.op("dve", lambda e: e.tensor_tensor(out=btok[:], in0=kktok[:], in1=atok[:], op=ALU.mult), reads=[kktok, atok], writes=[btok])

            import os
            DBG = int(os.environ.get("RWDBG", "9"))

            def chunk(p, d, t, gtok, first):
                c0 = t * 128
                if DBG < 2:
                    return
                incl, excl, after = (1, 0, 2) if d == 0 else (3, 2, 0)
                m_s, m_i, m_t = (0, 1, 2) if d == 0 else (2, 3, 0)
                psL = self.PS()
                fw.op("pe", lambda e: e.matmul(psL[:, 0:128], wlog[:, t, :], M[:, incl, :], start=True, stop=True), reads=[wlog, M], writes=[psL])
                fw.op("pe", lambda e: e.matmul(psL[:, 128:256], wlog[:, t, :], M[:, excl, :], start=True, stop=True), reads=[wlog, M], writes=[psL])
                fw.op("pe", lambda e: e.matmul(psL[:, 256:384], M[:, after, :], wlog[:, t, :], start=True, stop=True), reads=[wlog, M], writes=[psL])
                fw.op("act", lambda e: e.activation(out=eL[:], in_=psL[:, 0:128], func=AF.Exp), reads=[psL], writes=[eL])
                fw.op("act", lambda e: e.activation(out=eLp[:], in_=psL[:, 128:256], func=AF.Exp), reads=[psL], writes=[eLp])
                fw.op("act", lambda e: e.activation(out=enL[:], in_=psL[:, 0:128], func=AF.Exp, scale=-1.0), reads=[psL], writes=[enL])
                fw.op("act", lambda e: e.activation(out=eD[:], in_=psL[:, 256:384], func=AF.Exp), reads=[psL], writes=[eD])
                fw.op("dve", lambda e: e.scalar_tensor_tensor(out=AR[:, 0, :], in0=kkT[:, c0:c0 + 128], scalar=-1.0, in1=eLp[:], op0=ALU.mult, op1=ALU.mult), reads=[kkT, eLp], writes=[AR])
                fw.op("dve", lambda e: e.tensor_tensor(out=AR[:, 1, :], in0=rT[:, c0:c0 + 128], in1=eL[:], op=ALU.mult), reads=[rT, eL], writes=[AR])
                fw.op("dve", lambda e: e.tensor_tensor(out=BT[:], in0=bT[:, c0:c0 + 128], in1=enL[:], op=ALU.mult), reads=[bT, enL], writes=[BT])
                fw.op("dve", lambda e: e.tensor_tensor(out=KTt[:], in0=keffT[:, c0:c0 + 128], in1=enL[:], op=ALU.mult), reads=[keffT, enL], writes=[KTt])
                Tfin = [None, None]
                if DBG < 3:
                    return
                for hh in range(2):
                    po = 64 * hh
                    fw.op("dve", lambda e, hh=hh, po=po: e.tensor_tensor(out=Bp[hh][:, po:po + 64], in0=btok[:, t, po:po + 64], in1=eD[:, po:po + 64], op=ALU.mult), reads=[btok, eD], writes=[Bp[hh]])
                    fw.op("dve", lambda e, hh=hh, po=po: e.tensor_tensor(out=Kp[hh][:, po:po + 64], in0=kefftok[:, t, po:po + 64], in1=eD[:, po:po + 64], op=ALU.mult), reads=[kefftok, eD], writes=[Kp[hh]])
                psGs, psNs = [], []
                for hh in range(2):
                    po = 64 * hh
                    psG = self.PS()
                    arv = AR[po:po + 64, :, :].rearrange("p a n -> p (a n)")
                    fw.op("pe", lambda e, psG=psG, po=po, arv=arv: e.matmul(psG[:, 0:256], BT[po:po + 64, :], arv, start=True, stop=True), reads=[BT, AR], writes=[psG])
                    fw.op("pe", lambda e, psG=psG, po=po, arv=arv: e.matmul(psG[:, 256:512], KTt[po:po + 64, :], arv, start=True, stop=True), reads=[KTt, AR], writes=[psG])
                    psN = self.PS()
                    fw.op("pe", lambda e, psN=psN, po=po: e.matmul(psN[:, 0:128], AR[po:po + 64, 0, :], BT[po:po + 64, :], start=True, stop=True), reads=[BT, AR], writes=[psN])
                    psGs.append(psG)
                    psNs.append(psN)
                for hh in range(2):
                    psG, psN = psGs[hh], psNs[hh]
                    X, Y = XA[hh][0], YA[hh][0]
                    fw.op("dve", lambda e, psG=psG, X=X: e.tensor_tensor(out=X[:, 0, :], in0=psG[:, 0:128], in1=M[:, m_s, :], op=ALU.mult), reads=[psG, M], writes=[XP[hh][0]])
                    fw.op("dve", lambda e, psN=psN, Y=Y: e.tensor_tensor(out=Y[:], in0=psN[:, 0:128], in1=M[:, m_t, :], op=ALU.mult), reads=[psN, M], writes=[Y])
                    fw.op("pool", lambda e, X=X: e.tensor_copy(out=X[:, 1, :], in_=identb[:]), reads=[identb], writes=[XT_[hh][0]])
                    fw.op("dve", lambda e, psG=psG, hh=hh: e.tensor_tensor(out=Mbr[hh][:], in0=psG[:, 128:256], in1=M[:, m_i, :], op=ALU.mult), reads=[psG, M], writes=[Mbr[hh]])
                    fw.op("dve", lambda e, psG=psG, hh=hh: e.tensor_tensor(out=Mak[hh][:], in0=psG[:, 256:384], in1=M[:, m_s, :], op=ALU.mult), reads=[psG, M], writes=[Mak[hh]])
                    fw.op("dve", lambda e, psG=psG, hh=hh: e.tensor_tensor(out=Mkr[hh][:], in0=psG[:, 384:512], in1=M[:, m_i, :], op=ALU.mult), reads=[psG, M], writes=[Mkr[hh]])
                cur = 0
                for lvl in range(7 if DBG >= 4 else 0):
                    for hh in range(2):
                        X, Y = XA[hh][cur], YA[hh][cur]
                        Xn, Yn = XA[hh][1 - cur], YA[hh][1 - cur]
                        xp, xt, xpn, xtn = XP[hh][cur], XT_[hh][cur], XP[hh][1 - cur], XT_[hh][1 - cur]
                        ps1b = self.PS()
                        fw.op("pe", lambda e, ps1b=ps1b, X=X, Y=Y: e.matmul(ps1b[:, 0:128], Y[:], X[:, 1, :], start=True, stop=True), reads=[xt, Y], writes=[ps1b])
                        if lvl < 6:
                            ps1a = self.PS()
                            fw.op("pe", lambda e, ps1a=ps1a, X=X, Y=Y: e.matmul(ps1a[:, 0:128], Y[:], X[:, 0, :], start=True, stop=True), reads=[xp, Y], writes=[ps1a])
                            ps2 = self.PS()
                            fw.op("pe", lambda e, ps2=ps2, X=X, Y=Y: e.matmul(ps2[:, 0:128], X[:, 0, :], Y[:], start=True, stop=True), reads=[xp, Y], writes=[ps2])
                        fw.op("dve", lambda e, ps1b=ps1b, X=X, Xn=Xn: e.tensor_tensor(out=Xn[:, 1, :], in0=ps1b[:, 0:128], in1=X[:, 1, :], op=ALU.add), reads=[ps1b, xt], writes=[xtn])
                        if lvl < 6:
                            fw.op("act", lambda e, ps1a=ps1a, Xn=Xn: e.activation(out=Xn[:, 0, :], in_=ps1a[:, 0:128], func=AF.Identity), reads=[ps1a], writes=[xpn])
                            fw.op("act", lambda e, ps2=ps2, Yn=Yn: e.activation(out=Yn[:], in_=ps2[:, 0:128], func=AF.Identity), reads=[ps2], writes=[Yn])
                    cur = 1 - cur
                if DBG < 5:
                    return
                psXs = []
                for hh in range(2):
                    po = 64 * hh
                    psX = self.PS()
                    fw.op("pe", lambda e, psX=psX, po=po: e.matmul(psX[:, 0:64], AR[po:po + 64, 0, :], STb[d][po:po + 64, po:po + 64], start=True, stop=False), reads=[AR, STb[d]], writes=[psX])
                    fw.op("pe", lambda e, psX=psX, po=po, hh=hh: e.matmul(psX[:, 0:64], Mak[hh][:], Vpad[:, t, hh, po:po + 64], start=False, stop=True), reads=[Mak[hh], Vpad], writes=[psX])
                    psXs.append(psX)
                for hh in range(2):
                    psX = psXs[hh]
                    fw.op("act" if hh == 0 else "dve",
                          (lambda e, psX=psX, hh=hh: e.activation(out=XTs[hh][:], in_=psX[:, 0:64], func=AF.Identity)) if hh == 0 else
                          (lambda e, psX=psX, hh=hh: e.tensor_copy(out=XTs[hh][:], in_=psX[:, 0:64])), reads=[psX], writes=[XTs[hh]])
                psUs = []
                for hh in range(2):
                    Tf = XA[hh][cur]
                    psU = self.PS()
                    fw.op("pe", lambda e, psU=psU, Tf=Tf, hh=hh: e.matmul(psU[:, 0:64], Tf[:, 1, :], XTs[hh][:], start=True, stop=True), reads=[XT_[hh][cur], XTs[hh]], writes=[psU])
                    psUs.append(psU)
                for hh in range(2):
                    po = 64 * hh
                    psU = psUs[hh]
                    fw.op("act" if hh == 0 else "dve",
                          (lambda e, psU=psU, hh=hh, po=po: e.activation(out=Up[hh][:, po:po + 64], in_=psU[:, 0:64], func=AF.Identity)) if hh == 0 else
                          (lambda e, psU=psU, hh=hh, po=po: e.tensor_copy(out=Up[hh][:, po:po + 64], in_=psU[:, 0:64])), reads=[psU], writes=[Up[hh]])
                if DBG < 6:
                    return
                psY = self.PS()
                fw.op("pe", lambda e: e.matmul(psY[:, 0:128], STb[d][:], AR[:, 1, :], start=True, stop=False), reads=[STb[d], AR], writes=[psY])
                for hh in range(2):
                    fw.op("pe", lambda e, hh=hh: e.matmul(psY[:, 0:128], Up[hh][:], Mbr[hh][:], start=False, stop=False), reads=[Up[hh], Mbr[hh]], writes=[psY])
                    fw.op("pe", lambda e, hh=hh: e.matmul(psY[:, 0:128], Vpad[:, t, hh, :], Mkr[hh][:], start=False, stop=(hh == 1)), reads=[Vpad, Mkr[hh]], writes=[psY])
                a0 = gtok + c0
                if first:
                    fw.op("act", lambda e: e.activation(out=acc[:, a0:a0 + 128], in_=psY[:, 0:128], func=AF.Identity), reads=[psY], writes=[acc])
                else:
                    fw.op("dve", lambda e: e.tensor_tensor(out=acc[:, a0:a0 + 128], in0=psY[:, 0:128], in1=acc[:, a0:a0 + 128], op=ALU.add), reads=[psY, acc], writes=[acc])
                psS = self.PS()
                for hh in range(2):
                    fw.op("pe", lambda e, hh=hh: e.matmul(psS[:, 0:128], Bp[hh][:], Up[hh][:], start=(hh == 0), stop=False), reads=[Bp[hh], Up[hh]], writes=[psS])
                    fw.op("pe", lambda e, hh=hh: e.matmul(psS[:, 0:128], Kp[hh][:], Vpad[:, t, hh, :], start=False, stop=(hh == 1)), reads=[Kp[hh], Vpad], writes=[psS])
                gcol = eL[:, 127:128] if d == 0 else eL[:, 0:1]
                fw.op("dve", lambda e: e.scalar_tensor_tensor(out=ST[d][:], in0=ST[d][:], scalar=gcol, in1=psS[:, 0:128], op0=ALU.mult, op1=ALU.add), reads=[ST[d], eL, psS], writes=[ST[d]])
                fw.op("act", lambda e: e.activation(out=STb[d][:], in_=ST[d][:], func=AF.Identity), reads=[ST[d]], writes=[STb[d]])

            def init_state(p, d):
                fw.op("pool", lambda e: e.memset(ST[d][:], 0.0), writes=[ST[d]])
                if which == 1:
                    fw.op("pool", lambda e: e.memset(stg[:], 0.0), writes=[stg])
                    for hh in range(2):
                        po = 64 * hh
                        fw.dma(lambda e, hh=hh, po=po: e.dma_start(out=stg[po:po + 64, po:po + 64], in_=I["state_rwkv"][d, 2 * p + hh]), writes=[stg])
                    ps = self.PS()
                    fw.op("pe", lambda e: e.transpose(ps[:, 0:128], stg[:], C["ident"][:]), reads=[stg, C["ident"]], writes=[ps])
                    fw.op("dve", lambda e: e.tensor_copy(out=ST[d][:], in_=ps[:, 0:128]), reads=[ps], writes=[ST[d]])
                fw.op("act", lambda e: e.activation(out=STb[d][:], in_=ST[d][:], func=AF.Identity), reads=[ST[d]], writes=[STb[d]])
                if not self._pd_done.get((p, d, self.hT_group)):
                    pass

            def out_state(p, d, s):
                ps = self.PS()
                fw.op("pe", lambda e: e.transpose(ps[:, 0:128], ST[d][:], C["ident"][:]), reads=[ST[d], C["ident"]], writes=[ps])
                fw.op("dve", lambda e: e.tensor_copy(out=stg[:], in_=ps[:, 0:128]), reads=[ps], writes=[stg])
                for hh in range(2):
                    po = 64 * hh
                    fw.dma(lambda e, hh=hh, po=po: e.dma_start(out=O["o_srw"][s, d, 2 * p + hh], in_=stg[po:po + 64, po:po + 64]), reads=[stg], writes=[self.OUTB])

            def finalize(p, grp):
                g0 = grp * GS
                pc = 128 * p
                for tb in range(0, GS, 512):
                    y = acc[:, g0 + tb:g0 + tb + 512]
                    t1, t2, t3 = t512
                    ps = self.PS()
                    fw.op("pe", lambda e, ps=ps, y=y: e.matmul(ps[:, :], C["blk_f"][:, :], y, start=True, stop=True), reads=[C["blk_f"], acc], writes=[ps])
                    fw.op("dve", lambda e, ps=ps, y=y: e.scalar_tensor_tensor(out=t1[:], in0=ps[:], scalar=-1.0 / 64, in1=y, op0=ALU.mult, op1=ALU.add), reads=[ps, acc], writes=[t1])
                    self._srcbuf = t1
                    self.blk_stat(t1[:], 512, 1.0 / 64, 1, t2)
                    fw.op("dve", lambda e: e.tensor_tensor(out=t1[:], in0=t1[:], in1=t2[:], op=ALU.mult), reads=[t1, t2], writes=[t1])
                    fw.op("dve", lambda e: e.tensor_scalar(out=t1[:], in0=t1[:], scalar1=R["LGNc"][:, p:p + 1], scalar2=R["LNBc"][:, p:p + 1], op0=ALU.mult, op1=ALU.add), reads=[t1, R["LGNc"], R["LNBc"]], writes=[t1])
                    fw.op("dve", lambda e, tb=tb: e.scalar_tensor_tensor(out=t2[:], in0=rT[:, tb:tb + 512], scalar=R["RKc"][:, p:p + 1], in1=kT[:, tb:tb + 512], op0=ALU.mult, op1=ALU.mult), reads=[rT, kT, R["RKc"]], writes=[t2])
                    ps2 = self.PS()
                    fw.op("pe", lambda e, ps2=ps2: e.matmul(ps2[:, :], C["blk_f"][:, :], t2[:], start=True, stop=True), reads=[C["blk_f"], t2], writes=[ps2])
                    fw.op("dve", lambda e, ps2=ps2, tb=tb: e.tensor_tensor(out=t3[:], in0=ps2[:], in1=vT[:, tb:tb + 512], op=ALU.mult), reads=[ps2, vT], writes=[t3])
                    fw.op("dve", lambda e: e.tensor_tensor(out=t1[:], in0=t1[:], in1=t3[:], op=ALU.add), reads=[t1, t3], writes=[t1])
                    ps3 = self.PS()
                    fw.op("pe", lambda e, ps3=ps3, tb=tb: e.matmul(ps3[:, :], R["gup"][:, pc:pc + 128], sgd[:, tb:tb + 512], start=True, stop=True), reads=[R["gup"], sgd], writes=[ps3])
                    fw.op("dve", lambda e, ps3=ps3, tb=tb: e.tensor_tensor(out=mixT[:, 4 + p, g0 + tb:g0 + tb + 512], in0=t1[:], in1=ps3[:], op=ALU.mult), reads=[t1, ps3], writes=[mixT])

            sb_sq = atok
            self.lora_group = None
            self._pd_done = {}
            if DBG < 1:
                lvl0 = int(os.environ.get("RWSUB", "0"))
                noop = lambda *a, **k: None
                self.run_sched(xT, hT, which, sched, nseq, nchunk_seq, project, chunk, noop if lvl0 < 3 else init_state, noop if lvl0 < 3 else out_state,
                               noop if lvl0 < 2 else finalize, prep_dir=None if lvl0 < 1 else prep_dir)
            else:
                self.run_sched(xT, hT, which, sched, nseq, nchunk_seq, project, chunk, init_state, out_state, finalize, prep_dir=prep_dir)
            self.flush()

    def hT_for(self, xT, hT, blk, gain_idx, layer, which, shift_i, scale_i):
        if self.hT_group != blk:
            self.norm_mod(xT, 1024, hT, gain_idx, layer, which, shift_i, scale_i, t0=blk * 1024)
            self.hT_group = blk

    def cache_T(self, dst, src_dram, c0, st_tile):
        fw, C = self.fw, self.C
        for t in range(2):
            fw.dma(lambda e, t=t: e.dma_start(out=st_tile[:, t, :], in_=src_dram[t * 128:(t + 1) * 128, c0:c0 + 128]), writes=[st_tile])
        ps = self.PS()
        for t in range(2):
            fw.op("pe", lambda e, t=t, ps=ps: e.transpose(ps[:, t * 128:(t + 1) * 128], st_tile[:, t, :], C["ident"][:]), reads=[st_tile, C["ident"]], writes=[ps])
        fw.op("dve", lambda e, ps=ps: e.tensor_copy(out=dst[:, 2048:2304], in_=ps[:, 0:256]), reads=[ps], writes=[dst])

    def lat_attention(self, xT, hT, mixT):
        fw, I, C = self.fw, self.I, self.C
        which = 1
        Wi = I["w_in_attn"]
        acc_banks = self.ps_pool[0:4]
        self.ps_rot = self.ps_pool[4:8]
        self.ps_i = 0
        st0 = ExitStack()
        hTx = fw.sbuf(st0, "hTx", [128, 8, 1024], BF16)
        hTs = [hT, hTx]
        self.norm_mod(xT, 1024, hT, 0, 0, which, 0, 1, t0=0)
        self.norm_mod(xT, 1024, hTx, 0, 0, which, 0, 1, t0=1024)
        with ExitStack() as st:
            wq, wk, wv = self.ws_pool
            for i, w_ in enumerate((wq, wk, wv)):
                self.load_w(Wi, 0, 8, i * 512, 512, dst=w_)
            KT = fw.sbuf(st, "l_KT", [128, 2304], BF16)
            QT = fw.sbuf(st, "l_QT", [128, 2048], BF16)
            V = fw.sbuf(st, "l_V", [128, 18, 128], BF16)
            T3 = fw.sbuf(st, "l_T3", [128, 22, 64], F32)
            RMK = fw.sbuf(st, "l_RMK", [128, 4, 16, 8], F32)
            fw.dma(lambda e: e.dma_start(out=RMK[:], in_=I["c_rowmask"]), writes=[RMK])
            cst = fw.sbuf(st, "l_cst", [128, 2, 128], F32)
            PT = [fw.sbuf(st, f"l_PT{i}", [128, 512], BF16) for i in range(4)]
            rdn = fw.sbuf(st, "l_rdn", [128, 512], F32)
            for hp in range(4):
                pc = 128 * hp
                for blk in range(2):
                    for tb in (0, 512):
                        g0 = blk * 1024 + tb
                        self.proj_fm(wq, 8, pc, hTs[blk], tb, 512, lambda ps, g0=g0: fw.op("act", lambda e: e.activation(out=QT[:, g0:g0 + 512], in_=ps[:], func=AF.Identity), reads=[ps], writes=[QT]))
                        self.proj_fm(wk, 8, pc, hTs[blk], tb, 512, lambda ps, g0=g0: fw.op("act", lambda e: e.activation(out=KT[:, g0:g0 + 512], in_=ps[:], func=AF.Identity), reads=[ps], writes=[KT]))
                    for t in range(8):
                        self.proj_tm(wv, 8, pc, 128, hTs[blk], t * 128, lambda ps, t=t, blk=blk: fw.op("act", lambda e: e.activation(out=V[:, blk * 8 + t, :], in_=ps[:, 0:128], func=AF.Identity), reads=[ps], writes=[V]))
                self.cache_T(KT, I["cache_na_k"], pc, cst)
                for t in range(2):
                    fw.dma(lambda e, t=t, pc=pc: e.dma_start(out=V[:, 16 + t, :], in_=I["cache_na_v"][t * 128:(t + 1) * 128, pc:pc + 128]), writes=[V], queue="pool")
                for hh in range(2):
                    h = 2 * hp + hh
                    po = 64 * hh
                    fw.dma(lambda e, h=h: e.dma_start(out=T3[:], in_=I["c_rpbT"][h]), writes=[T3])
                    for qb in range(4):
                        self.na_block(h, hp, hh, po, qb, KT, QT, V, T3, RMK, PT, rdn, mixT, acc_banks)
            self.flush()
        with ExitStack() as st:
            wq, wk, wv = self.ws_pool
            for i, w_ in enumerate((wq, wk, wv)):
                self.load_w(Wi, 0, 8, 1536 + i * 512, 512, dst=w_)
            KT = fw.sbuf(st, "d_KT", [128, 2304], BF16)
            QT = fw.sbuf(st, "d_QT", [128, 2048], BF16)
            V = fw.sbuf(st, "d_V", [128, 18, 128], BF16)
            cos = fw.sbuf(st, "d_cos", [128, 1024], F32)
            sin = fw.sbuf(st, "d_sin", [128, 1024], F32)
            perm = fw.sbuf(st, "d_perm", [128, 128], BF16)
            fw.dma(lambda e: e.dma_start(out=perm[:], in_=I["c_perm"]), writes=[perm], queue="pool")
            cst = fw.sbuf(st, "d_cst", [128, 2, 128], F32)
            xb = fw.sbuf(st, "d_xb", [128, 512], BF16)
            PT = [fw.sbuf(st, f"d_PT{i}", [128, 512], BF16) for i in range(4)]
            for j in range(4):
                pc = 128 * j
                for blk in range(2):
                    fw.dma(lambda e, blk=blk: e.dma_start(out=cos[:], in_=I["c_cos"][:, blk * 1024:(blk + 1) * 1024]), writes=[cos])
                    fw.dma(lambda e, blk=blk: e.dma_start(out=sin[:], in_=I["c_sin"][:, blk * 1024:(blk + 1) * 1024]), writes=[sin])
                    for tb in (0, 512):
                        g0 = blk * 1024 + tb
                        for w_, dst in ((wq, QT), (wk, KT)):
                            def ev(ps, g0=g0, tb=tb, dst=dst):
                                xf = self.ntmp[0]
                                t2 = self.ntmp[1]
                                fw.op("act", lambda e: e.activation(out=xf[:], in_=ps[:], func=AF.Identity), reads=[ps], writes=[xf])
                                fw.op("dve", lambda e: e.tensor_copy(out=xb[:], in_=xf[:]), reads=[xf], writes=[xb])
                                ps2 = self.PS()
                                fw.op("pe", lambda e: e.matmul(ps2[:, :], perm[:], xb[:], start=True, stop=True), reads=[perm, xb], writes=[ps2])
                                fw.op("dve", lambda e: e.tensor_tensor(out=t2[:], in0=ps2[:], in1=sin[:, tb:tb + 512], op=ALU.mult), reads=[ps2, sin], writes=[t2])
                                fw.op("dve", lambda e: e.tensor_tensor(out=xf[:], in0=xf[:], in1=cos[:, tb:tb + 512], op=ALU.mult), reads=[xf, cos], writes=[xf])
                                fw.op("dve", lambda e: e.tensor_tensor(out=dst[:, g0:g0 + 512], in0=xf[:], in1=t2[:], op=ALU.add), reads=[xf, t2], writes=[dst])
                            self.proj_fm(w_, 8, pc, hTs[blk], tb, 512, ev)
                    for t in range(8):
                        self.proj_tm(wv, 8, pc, 128, hTs[blk], t * 128, lambda ps, t=t, blk=blk: fw.op("act", lambda e: e.activation(out=V[:, blk * 8 + t, :], in_=ps[:, 0:128], func=AF.Identity), reads=[ps], writes=[V]))
                self.cache_T(KT, I["cache_diff_k"], pc, cst)
                for t in range(2):
                    fw.dma(lambda e, t=t, pc=pc: e.dma_start(out=V[:, 16 + t, :], in_=I["cache_diff_v"][t * 128:(t + 1) * 128, pc:pc + 128]), writes=[V], queue="pool")
                for qb in range(4):
                    self.df_block(j, qb, KT, QT, V, PT, mixT, acc_banks)
            self.flush()
        st0.close()
        self.ps_rot = self.ps_pool
        self.ps_i = 0

    def na_block(self, h, hp, hh, po, qb, KT, QT, V, T3, RMK, PT, rdn, mixT, acc_banks):
        fw, C = self.fw, self.C
        R0 = 8 * qb
        q0 = qb * 512
        lo = min(max(R0 - 4, 0), 24)
        hi = min(max(R0 + 3, 0), 24) + 8
        tiles = list(range(lo // 2, (hi - 1) // 2 + 1)) + [16, 17]
        num, den = acc_banks[0], acc_banks[1]
        n = len(tiles)
        def issue_s(i, kt):
            ps = self.PS()
            fw.op("pe", lambda e, ps=ps: e.matmul(ps[:, :], KT[po:po + 64, kt * 128:(kt + 1) * 128], QT[po:po + 64, q0:q0 + 512], start=True, stop=True), reads=[KT, QT], writes=[ps])
            pt = PT[i % 4]
            if kt < 16:
                sp = self.ntmp[1 + (i % 3)]
                mlo = R0 - 2 * kt + 7 + 3
                fw.op("dve", lambda e, ps=ps, sp=sp, mlo=mlo: e.scalar_tensor_tensor(out=sp[:], in0=ps[:], scalar=0.125, in1=T3[:, mlo:mlo + 8, :].rearrange("p m q -> p (m q)"),
                                                                                 op0=ALU.mult, op1=ALU.add), reads=[ps, T3], writes=[sp])
                fw.op("dve", lambda e, sp=sp: e.tensor_tensor(out=sp[:].rearrange("p (m q) -> p m q", q=64), in0=sp[:].rearrange("p (m q) -> p m q", q=64),
                                                             in1=RMK[:, qb, kt, :].unsqueeze(2).broadcast_to([128, 8, 64]), op=ALU.add), reads=[sp, RMK], writes=[sp])
                fw.op("act", lambda e, sp=sp, pt=pt: e.activation(out=pt[:], in_=sp[:], func=AF.Exp), reads=[sp], writes=[pt])
            else:
                fw.op("act", lambda e, ps=ps, pt=pt: e.activation(out=pt[:], in_=ps[:], func=AF.Exp, scale=0.125), reads=[ps], writes=[pt])
        issue_s(0, tiles[0])
        issue_s(1, tiles[1])
        for i, kt in enumerate(tiles):
            if i + 2 < n:
                issue_s(i + 2, tiles[i + 2])
            pt = PT[i % 4]
            fw.op("pe", lambda e, kt=kt, pt=pt, i=i: e.matmul(num[:, :], V[:, kt, :], pt[:], start=(i == 0), stop=(i == n - 1)), reads=[V, pt], writes=[num])
            fw.op("pe", lambda e, pt=pt, i=i: e.matmul(den[:, :], C["ones_b"][:, :], pt[:], start=(i == 0), stop=(i == n - 1)), reads=[C["ones_b"], pt], writes=[den])
        fw.op("dve", lambda e: e.reciprocal(out=rdn[po:po + 64, :], in_=den[po:po + 64, :]), reads=[den], writes=[rdn])
        fw.op("dve", lambda e: e.tensor_tensor(out=mixT[po:po + 64, hp, q0:q0 + 512], in0=num[po:po + 64, :], in1=rdn[po:po + 64, :], op=ALU.mult), reads=[num, rdn], writes=[mixT])

    def df_block(self, j, qb, KT, QT, V, PT, mixT, acc_banks):
        fw, C = self.fw, self.C
        q0 = qb * 512
        for c in range(2):
            po = 64 * c
            num, den = acc_banks[2 * c], acc_banks[2 * c + 1]
            def issue_s(kt, po=po):
                ps = self.PS()
                fw.op("pe", lambda e, ps=ps: e.matmul(ps[:, :], KT[po:po + 64, kt * 128:(kt + 1) * 128], QT[po:po + 64, q0:q0 + 512], start=True, stop=True), reads=[KT, QT], writes=[ps])
                pt = PT[kt % 4]
                fw.op("act", lambda e, ps=ps, pt=pt: e.activation(out=pt[:], in_=ps[:], func=AF.Exp, scale=0.125), reads=[ps], writes=[pt])
            issue_s(0)
            issue_s(1)
            for kt in range(18):
                if kt + 2 < 18:
                    issue_s(kt + 2)
                pt = PT[kt % 4]
                fw.op("pe", lambda e, kt=kt, pt=pt, num=num: e.matmul(num[:, :], V[:, kt, :], pt[:], start=(kt == 0), stop=(kt == 17)), reads=[V, pt], writes=[num])
                fw.op("pe", lambda e, kt=kt, pt=pt, den=den: e.matmul(den[:, :], C["ones_b"][:, :], pt[:], start=(kt == 0), stop=(kt == 17)), reads=[C["ones_b"], pt], writes=[den])
        nA, dA, nB, dB = acc_banks
        r0, r1, t0, t1 = self.ntmp
        n = 512
        fw.op("dve", lambda e: e.reciprocal(out=r0[:], in_=dA[:]), reads=[dA], writes=[r0])
        fw.op("dve", lambda e: e.reciprocal(out=r1[:], in_=dB[:]), reads=[dB], writes=[r1])
        fw.op("dve", lambda e: e.tensor_tensor(out=t0[:], in0=nA[:], in1=r0[:], op=ALU.mult), reads=[nA, r0], writes=[t0])
        fw.op("dve", lambda e: e.tensor_tensor(out=t1[:], in0=nB[:], in1=r1[:], op=ALU.mult), reads=[nB, r1], writes=[t1])
        fw.op("dve", lambda e: e.scalar_tensor_tensor(out=t0[:], in0=t1[:], scalar=self.lamc[:, 0:1], in1=t0[:], op0=ALU.mult, op1=ALU.add), reads=[t0, t1, self.lamc], writes=[t0])
        fw.op("act", lambda e: e.activation(out=t1[:], in_=t0[:], func=AF.Square), reads=[t0], writes=[t1])
        ps = self.PS()
        fw.op("pe", lambda e: e.matmul(ps[:, :], C["ones_f"][:, :], t1[:], start=True, stop=True), reads=[C["ones_f"], t1], writes=[ps])
        fw.op("act", lambda e: e.activation(out=r0[:], in_=ps[:], func=AF.Sqrt, bias=C["eps"][:, 0:1], scale=1.0 / 128), reads=[ps, C["eps"]], writes=[r0])
        fw.op("dve", lambda e: e.reciprocal(out=r0[:], in_=r0[:]), reads=[r0], writes=[r0])
        fw.op("dve", lambda e: e.scalar_tensor_tensor(out=mixT[:, 4 + j, q0:q0 + n], in0=t0[:], scalar=1.0 - self.lam_init, in1=r0[:], op0=ALU.mult, op1=ALU.mult), reads=[t0, r0], writes=[mixT])


_PROG = {}


def get_prog(stop_after=None):
    if stop_after not in _PROG:
        _PROG[stop_after] = Prog(stop_after)
    return _PROG[stop_after]


def host_consts():
    r = np.arange(128)
    row, col = r[:, None], r[None, :]
    masks = np.stack([row < col, row <= col, row > col, row >= col]).astype(np.float32)
    diff = (col - row).astype(np.float32) * np.ones((128, 128), np.float32)
    ip = np.stack([np.broadcast_to(col + 1.0, (128, 128)), np.broadcast_to(128.0 - col, (128, 128))]).astype(np.float32)
    jc = np.stack([127.0 - r, r * 1.0], 1).astype(np.float32)
    half = np.stack([r < 64, r >= 64], 1).astype(np.float32)
    NEG = -1e30
    rs = np.clip(np.arange(32) - 4, 0, 24)
    rowmask = np.zeros((4, 16, 128, 8), np.float32)
    for qb in range(4):
        for kt in range(16):
            for e in range(2):
                kr = 2 * kt + e
                for i in range(8):
                    qr = 8 * qb + i
                    if not (rs[qr] <= kr < rs[qr] + 8):
                        rowmask[qb, kt, 64 * e:64 * e + 64, i] = NEG
    n = np.arange(2048)
    d = np.arange(128) % 64
    pos = np.where((d < 32)[:, None], (n // 64)[None, :], (n % 64)[None, :]).astype(np.float32)
    inv = (10000.0 ** (-(d % 16).astype(np.float32) / 16.0)).astype(np.float32)
    ang = pos * inv[:, None]
    perm = np.zeros((128, 128), np.float32)
    for o in range(128):
        if (o % 32) < 16:
            perm[o + 16, o] = -1.0
        else:
            perm[o - 16, o] = 1.0
    return {"c_masks": masks, "c_diff": diff, "c_ip": ip, "c_jc": jc, "c_half": half, "c_rowmask": np.ascontiguousarray(rowmask.transpose(2, 0, 1, 3)),
            "c_cos": np.cos(ang).astype(np.float32), "c_sin": np.sin(ang).astype(np.float32), "c_perm": perm}


def rpb_table(rpb):
    NEG = np.float32(-1e30)
    e = (np.arange(128) // 64)[:, None, None]
    kc = (np.arange(128) % 64)[:, None, None]
    mm = np.arange(22)[None, :, None]
    qc = np.arange(64)[None, None, :]
    dr = e + 7 - (mm - 3)
    cstart = np.clip(qc - 8, 0, 48)
    ok = (dr >= -7) & (dr <= 7) & (kc >= cstart) & (kc < cstart + 16)
    dri = np.clip(dr + 7, 0, 14) + 0 * kc + 0 * qc
    dci = np.clip(kc - qc + 15, 0, 30) + 0 * mm
    out = np.empty((8, 128, 22, 64), np.float32)
    for h in range(8):
        g = rpb[h][dri, dci]
        out[h] = np.where(ok, g, NEG)
    return out


def make_in_maps(inp):
    f = lambda a: np.ascontiguousarray(np.asarray(a, dtype=np.float32))
    hc = host_consts()
    rpbT = rpb_table(np.asarray(inp["na_rpb"][0], dtype=np.float32))
    maps = []
    for i in range(8):
        b = i // 4
        m = {
            "x_prompt": f(inp["x_prompt"][4 * i:4 * i + 4].reshape(1024, D)),
            "x_sample": f(inp["x_sample"][b]),
            "cond": f(np.stack([inp["c_ctx"], inp["c"][b]])),
            "cache_na_k": f(inp["cache_na_k"][b, 0].reshape(256, 512)),
            "cache_na_v": f(inp["cache_na_v"][b, 0].reshape(256, 512)),
            "cache_diff_k": f(inp["cache_diff_k"][b, 0].reshape(256, 512)),
            "cache_diff_v": f(inp["cache_diff_v"][b, 0].reshape(256, 512)),
            "state_ret": f(inp["state_ret"][b, 0]),
            "state_rwkv": f(inp["state_rwkv"][b, 0]),
            "norm_mix_g": f(inp["norm_mix_g"]), "norm_ffn_g": f(inp["norm_ffn_g"]), "norm_final_g": f(inp["norm_final_g"]),
            "w_ada": f(inp["w_ada"]), "b_ada": f(inp["b_ada"]),
            "w_in_attn": f(inp["w_in_attn"][0]), "w_out_attn": f(inp["w_out_attn"][0]), "na_rpb": f(inp["na_rpb"][0]),
            "diff_l": f(np.stack([inp["diff_lq1"][0], inp["diff_lk1"][0], inp["diff_lq2"][0], inp["diff_lk2"][0]])),
            "w_in_rec": f(inp["w_in_rec"][0]), "w_out_rec": f(inp["w_out_rec"][0]),
            "ret_decay_logit": f(inp["ret_decay_logit"][0].reshape(16)),
            "rw_w0": f(inp["rw_w0"][0]), "rw_w_up": f(inp["rw_w_up"][0]), "rw_a0": f(inp["rw_a0"][0]), "rw_a_up": f(inp["rw_a_up"][0]),
            "rw_g_up": f(inp["rw_g_up"][0]), "rw_k_k": f(inp["rw_k_k"][0]), "rw_k_a": f(inp["rw_k_a"][0]),
            "rw_r_k": f(inp["rw_r_k"][0].reshape(512)), "rw_ln_g": f(inp["rw_ln_g"][0]), "rw_ln_b": f(inp["rw_ln_b"][0]),
            "w_ffn_in": f(inp["w_ffn_in"]), "w_ffn_out": f(inp["w_ffn_out"]),
        }
        m.update({k: f(v) for k, v in hc.items()})
        m["c_rpbT"] = rpbT
        maps.append(m)
    return maps


def kernel(**inputs):
    prog = get_prog()
    maps = make_in_maps(inputs)
    res = run_bass_kernel_spmd(prog.nc, maps, core_ids=list(range(8)))
    R = res.results
    y_prompt = np.concatenate([R[i]["y_prompt"].reshape(4, 256, D) for i in range(8)], 0)
    y_sample = np.stack([R[0]["y_sample"], R[4]["y_sample"]], 0)
    cat = lambda k, shp: np.concatenate([R[i][k].reshape((4,) + shp) for i in range(8)], 0)
    na_k = cat("o_na_k", (1, 256, 8, 64))
    na_v = cat("o_na_v", (1, 256, 8, 64))
    df_k = cat("o_df_k", (1, 256, 4, 2, 64))
    df_v = cat("o_df_v", (1, 256, 4, 128))
    sret = cat("o_sret", (1, 2, 8, 64, 64))
    srw = cat("o_srw", (1, 2, 8, 64, 64))
    return tuple(np.ascontiguousarray(a, dtype=np.float32) for a in (y_prompt, y_sample, na_k, na_v, df_k, df_v, sret, srw))
```
